# Optimizing a Trainium2 kernel written in Bass

```python
import math
import jax, jax.numpy as jnp
from jax import lax
import numpy as np

D_MODEL = 1024
BATCH = 8
SEQ = 4096
DEPTH = 4

N_ATT_HEADS = 8
HEAD_DIM = 64
ATT_W = N_ATT_HEADS * HEAD_DIM
N_IDX_HEADS = 4
IDX_DIM = HEAD_DIM
INDEX_TOPK_MAX = 256
Q_BLOCK = 128
D_SSM = D_MODEL
SSM_HEAD_DIM = 64
N_SSM_HEADS = D_SSM // SSM_HEAD_DIM
N_GROUPS = 2
D_STATE = 128
CONV_K = 4
CHUNK = 128
CONV_CH = D_SSM + 2 * N_GROUPS * D_STATE
ROPE_THETA = 10000.0
EPS = 1e-6
NEG = -1e30
IN_SIZES = (ATT_W, HEAD_DIM, HEAD_DIM, ATT_W,
            N_IDX_HEADS * IDX_DIM, IDX_DIM, N_IDX_HEADS,
            D_SSM, CONV_CH, N_SSM_HEADS,
            D_MODEL, D_MODEL)
N_IN = sum(IN_SIZES)

kernel_name = 'hybrid_dsa_ssd_gated_merge_adaln'


def _rmsnorm(u, g):
    u32 = u.astype(jnp.float32)
    y = u32 * lax.rsqrt(jnp.mean(u32 * u32, axis=-1, keepdims=True) + EPS)
    return (y * g.astype(jnp.float32)).astype(u.dtype)


def _split_cols(p, sizes):
    offs, acc = [], 0
    for s in sizes[:-1]:
        acc += s
        offs.append(acc)
    return jnp.split(p, offs, axis=-1)


def _rope_tables(L):
    inv = ROPE_THETA ** (-jnp.arange(0, HEAD_DIM, 2, dtype=jnp.float32) / HEAD_DIM)
    ang = jnp.arange(L, dtype=jnp.float32)[:, None] * inv[None, :]
    ang = jnp.concatenate([ang, ang], axis=-1)
    return jnp.cos(ang), jnp.sin(ang)


def _rope(u, cos, sin):
    u32 = u.astype(jnp.float32)
    half = u32.shape[-1] // 2
    rot = jnp.concatenate([-u32[..., half:], u32[..., :half]], axis=-1)
    return (u32 * cos + rot * sin).astype(u.dtype)


def _causal_dwconv(u, w, b):
    out = lax.conv_general_dilated(u, w[:, None, :], window_strides=(1,), padding=((CONV_K - 1, 0),),
                                   dimension_numbers=('NWC', 'WIO', 'NWC'), feature_group_count=u.shape[-1])
    return out + b


def _dsa_attention(q, k, v, qi, ki, wi, topk):
    B, L = q.shape[0], q.shape[1]
    nb = L // Q_BLOCK
    kv = jnp.concatenate([k, v], axis=-1)
    key_pos = jnp.arange(L, dtype=jnp.int32)
    ki32 = ki.astype(jnp.float32)

    def blocks(u):
        return u.reshape((B, nb, Q_BLOCK) + u.shape[2:]).swapaxes(0, 1)

    starts = jnp.arange(nb, dtype=jnp.int32) * Q_BLOCK

    def one_block(inp):
        qb, qib, wb, t0 = inp
        qpos = t0 + jnp.arange(Q_BLOCK, dtype=jnp.int32)
        dots = jnp.einsum('bqhd,bsd->bqhs', qib.astype(jnp.float32), ki32) * IDX_DIM ** -0.5
        score = jnp.einsum('bqhs,bqh->bqs', jax.nn.relu(dots), wb.astype(jnp.float32))
        admissible = key_pos[None, :] <= qpos[:, None]
        score = jnp.where(admissible[None], score, NEG)
        _, sel = lax.top_k(score, topk)
        valid = sel <= qpos[None, :, None]
        kvg = jax.vmap(lambda t, i: t[i])(kv, sel)
        kg, vg = kvg[..., :HEAD_DIM], kvg[..., HEAD_DIM:]
        logits = jnp.einsum('bqhd,bqkd->bqhk', qb.astype(jnp.float32), kg.astype(jnp.float32)) * HEAD_DIM ** -0.5
        logits = jnp.where(valid[:, :, None, :], logits, NEG)
        p = jax.nn.softmax(logits, axis=-1)
        o = jnp.einsum('bqhk,bqkd->bqhd', p, vg.astype(jnp.float32))
        return o.astype(q.dtype)

    out = lax.map(one_block, (blocks(q), blocks(qi), blocks(wi), starts))
    return out.swapaxes(0, 1).reshape(B, L, ATT_W)


def _ssd_chunked(xh, dt, a, bm, cm):
    Bsz, L, H, P = xh.shape
    G, N = bm.shape[2], bm.shape[3]
    R = H // G
    nc = L // CHUNK
    xd = (xh * dt[..., None]).reshape(Bsz, nc, CHUNK, G, R, P)
    adt = (dt * a).reshape(Bsz, nc, CHUNK, G, R).transpose(0, 3, 4, 1, 2)
    a_cs = jnp.cumsum(adt, axis=-1)
    bm = bm.reshape(Bsz, nc, CHUNK, G, N)
    cm = cm.reshape(Bsz, nc, CHUNK, G, N)
    tril = jnp.tril(jnp.ones((CHUNK, CHUNK), dtype=bool))
    seg = a_cs[..., :, None] - a_cs[..., None, :]
    decay = jnp.exp(jnp.where(tril, seg, -jnp.inf))
    cb = jnp.einsum('bclgn,bcsgn->bgcls', cm, bm)
    y_diag = jnp.einsum('bgcls,bgrcls,bcsgrp->bclgrp', cb, decay, xd)
    decay_states = jnp.exp(a_cs[..., -1:] - a_cs)
    states = jnp.einsum('bclgn,bgrcl,bclgrp->bcgrpn', bm, decay_states, xd)
    chunk_decay = jnp.exp(a_cs[..., -1])

    def step(h, inp):
        s, d = inp
        return h * d[..., None, None] + s, h

    h0 = jnp.zeros((Bsz, G, R, P, N), dtype=xd.dtype)
    _, prev = lax.scan(step, h0, (states.transpose(1, 0, 2, 3, 4, 5), chunk_decay.transpose(3, 0, 1, 2)))
    prev = prev.transpose(1, 0, 2, 3, 4, 5)
    y_off = jnp.einsum('bclgn,bcgrpn,bgrcl->bclgrp', cm, prev, jnp.exp(a_cs))
    return (y_diag + y_off).reshape(Bsz, L, H, P)


def _layer(x, c, w_ada, b_ada, norm_g, w_in, conv_w, conv_b, dt_bias, a_log, d_skip, ssm_norm_g,
           w_branch_a, w_branch_s, w_out, cos, sin, topk):
    B, L, _ = x.shape
    mod = jax.nn.silu(c) @ w_ada + b_ada
    shift, scale, gate = jnp.split(mod, 3, axis=-1)
    h = _rmsnorm(x, norm_g) * (1.0 + scale[:, None, :]) + shift[:, None, :]
    proj = h @ w_in
    (q, k, v, g_att, qi, ki, wi, z, xbc, dt_raw, gl_att, gl_ssd) = _split_cols(proj, IN_SIZES)

    q = _rope(q.reshape(B, L, N_ATT_HEADS, HEAD_DIM), cos[:, None, :], sin[:, None, :])
    k = _rope(k, cos, sin)
    qi = _rope(qi.reshape(B, L, N_IDX_HEADS, IDX_DIM), cos[:, None, :], sin[:, None, :])
    ki = _rope(ki, cos, sin)
    wi = wi * N_IDX_HEADS ** -0.5
    o_att = _dsa_attention(q, k, v, qi, ki, wi, topk)
    y_att = (o_att * jax.nn.silu(g_att)) @ w_branch_a

    xbc = jax.nn.silu(_causal_dwconv(xbc, conv_w, conv_b))
    xs, bm, cm = jnp.split(xbc, [D_SSM, D_SSM + N_GROUPS * D_STATE], axis=-1)
    xh = xs.reshape(B, L, N_SSM_HEADS, SSM_HEAD_DIM).astype(jnp.float32)
    dt = jax.nn.softplus(dt_raw.astype(jnp.float32) + dt_bias.astype(jnp.float32))
    a = -jnp.exp(a_log.astype(jnp.float32))
    y = _ssd_chunked(xh, dt, a,
                     bm.reshape(B, L, N_GROUPS, D_STATE).astype(jnp.float32),
                     cm.reshape(B, L, N_GROUPS, D_STATE).astype(jnp.float32))
    y = y + d_skip.astype(jnp.float32)[:, None] * xh
    y = y.reshape(B, L, D_SSM) * jax.nn.silu(z.astype(jnp.float32))
    y = y.reshape(B, L, N_GROUPS, D_SSM // N_GROUPS)
    y = y * lax.rsqrt(jnp.mean(y * y, axis=-1, keepdims=True) + EPS)
    y = (y.reshape(B, L, D_SSM) * ssm_norm_g.astype(jnp.float32)).astype(x.dtype)
    y_ssd = y @ w_branch_s

    merged = jax.nn.sigmoid(gl_att) * y_att + jax.nn.sigmoid(gl_ssd) * y_ssd
    return x + gate[:, None, :] * (merged @ w_out)


def setup_inputs(seed: int = 0) -> dict:
    key = jax.random.key(seed)
    ks = jax.random.split(key, 16)
    f32 = jnp.float32
    x = jax.random.normal(ks[0], (BATCH, SEQ, D_MODEL), f32)
    c = jax.random.normal(ks[1], (BATCH, D_MODEL), f32)
    w_ada = jax.random.normal(ks[2], (DEPTH, D_MODEL, 3 * D_MODEL), f32) * (0.5 * D_MODEL ** -0.5)
    b_ada = jax.random.normal(ks[3], (DEPTH, 3 * D_MODEL), f32) * 0.02
    norm_g = 1.0 + 0.05 * jax.random.normal(ks[4], (DEPTH, D_MODEL), f32)
    w_in = jax.random.normal(ks[5], (DEPTH, D_MODEL, N_IN), f32) * D_MODEL ** -0.5
    conv_w = jax.random.normal(ks[6], (DEPTH, CONV_K, CONV_CH), f32) * CONV_K ** -0.5
    conv_b = jax.random.normal(ks[7], (DEPTH, CONV_CH), f32) * 0.02
    dt0 = jnp.exp(jax.random.uniform(ks[8], (DEPTH, N_SSM_HEADS), f32, math.log(1e-3), math.log(1e-1)))
    dt_bias = dt0 + jnp.log(-jnp.expm1(-dt0))
    a_log = jnp.log(jax.random.uniform(ks[9], (DEPTH, N_SSM_HEADS), f32, 1.0, 16.0))
    d_skip = 1.0 + 0.1 * jax.random.normal(ks[10], (DEPTH, N_SSM_HEADS), f32)
    ssm_norm_g = 1.0 + 0.05 * jax.random.normal(ks[11], (DEPTH, D_SSM), f32)
    w_branch_a = jax.random.normal(ks[12], (DEPTH, ATT_W, D_MODEL), f32) * ATT_W ** -0.5
    w_branch_s = jax.random.normal(ks[13], (DEPTH, D_SSM, D_MODEL), f32) * D_SSM ** -0.5
    w_out = jax.random.normal(ks[14], (DEPTH, D_MODEL, D_MODEL), f32) * D_MODEL ** -0.5
    final_g = 1.0 + 0.05 * jax.random.normal(ks[15], (D_MODEL,), f32)
    return {'x': x, 'c': c, 'w_ada': w_ada, 'b_ada': b_ada, 'norm_g': norm_g, 'w_in': w_in,
            'conv_w': conv_w, 'conv_b': conv_b, 'dt_bias': dt_bias, 'a_log': a_log, 'd_skip': d_skip,
            'ssm_norm_g': ssm_norm_g, 'w_branch_a': w_branch_a, 'w_branch_s': w_branch_s,
            'w_out': w_out, 'final_g': final_g}


def reference(x, c, w_ada, b_ada, norm_g, w_in, conv_w, conv_b, dt_bias, a_log, d_skip,
              ssm_norm_g, w_branch_a, w_branch_s, w_out, final_g):
    L = x.shape[1]
    topk = min(INDEX_TOPK_MAX, L // 4)
    cos, sin = _rope_tables(L)
    for i in range(DEPTH):
        x = _layer(x, c, w_ada[i], b_ada[i], norm_g[i], w_in[i], conv_w[i], conv_b[i], dt_bias[i],
                   a_log[i], d_skip[i], ssm_norm_g[i], w_branch_a[i], w_branch_s[i], w_out[i],
                   cos, sin, topk)
    return _rmsnorm(x, final_g)
```

```python
import numpy as np
from contextlib import ExitStack
import concourse.bass as bass
import concourse.mybir as mybir
from concourse.bass_utils import run_bass_kernel_spmd

F32 = mybir.dt.float32
BF16 = mybir.dt.bfloat16
AF = mybir.ActivationFunctionType
ALU = mybir.AluOpType
AX = mybir.AxisListType

L = 4096
D = 1024
NT = L // 128
DEPTH = 4
NIN = 6100
Q0, K0, V0, GA0, QI0, KI0, WI0, Z0, XBC0, DT0, GLA0, GLS0 = 0, 512, 576, 640, 1152, 1408, 1472, 1476, 2500, 4036, 4052, 5076
EPS = 1e-6
NBIS = 20
NDMASEM = 12


class Buf:
    __slots__ = ("name", "lw", "rd", "excl")

    def __init__(self, name="", excl=False):
        self.name = name
        self.lw = None
        self.rd = []
        self.excl = excl


class Sched:
    ENGS = ("sp", "act", "pe", "dve", "pool")
    DQ = ("sp", "act", "pool")

    def __init__(self, nc, stack):
        self.nc = nc
        self.batch = 0
        self.ops = {e: [] for e in self.ENGS}
        self.waited = {e: {} for e in self.ENGS}
        self.base = {e: 0 for e in self.ENGS}
        self.esem = {e: stack.enter_context(nc.semaphore("es_" + e)) for e in self.ENGS}
        self.dsem = {}
        self.dcnt = {}
        self.dval = {}
        for q in self.DQ:
            self.dsem[q] = [stack.enter_context(nc.semaphore(f"ds_{q}{i}")) for i in range(NDMASEM)]
            self.dcnt[q] = 0
            self.dval[q] = [0] * NDMASEM
        self.nops = 0

    def _need(self, eng, ev, waits, same_ok=False):
        if ev is None or ev[1] != self.batch:
            return
        if ev[0] == "e":
            _, _, e2, i2 = ev
            if e2 == eng and same_ok:
                return
            self.ops[e2][i2]["inc"] = True
            if self.waited[eng].get(e2, -1) >= i2:
                return
            self.waited[eng][e2] = i2
            waits.append(ev)
        else:
            _, _, q, si, val = ev
            key = (q, si)
            if self.waited[eng].get(key, -1) >= val:
                return
            self.waited[eng][key] = val
            waits.append(ev)

    def _deps(self, eng, reads, writes, waits):
        for r in reads:
            self._need(eng, r.lw, waits, same_ok=(eng == "pe"))
        for w in writes:
            self._need(eng, w.lw, waits, same_ok=(eng == "pe"))
            for ev in w.rd:
                self._need(eng, ev, waits, same_ok=(eng == "pe"))

    def op(self, eng, fn, reads=(), writes=()):
        ex = [r for r in reads if r.excl and r not in writes]
        if ex:
            writes = list(writes) + ex
        waits = []
        self._deps(eng, reads, writes, waits)
        idx = len(self.ops[eng])
        self.ops[eng].append({"fn": fn, "waits": waits, "inc": False, "dma": None})
        ev = ("e", self.batch, eng, idx)
        for r in reads:
            r.rd.append(ev)
        for w in writes:
            w.lw = ev
            w.rd = []
        self.nops += 1
        return ev

    def dma(self, q, out, in_, reads=(), writes=(), **kw):
        waits = []
        self._deps(q, reads, writes, waits)
        n = self.dcnt[q]
        si = n % NDMASEM
        self.dcnt[q] += 1
        prev = self.dval[q][si]
        if prev > 0:
            self._need(q, ("d", self.batch, q, si, prev), waits)
        val = prev + 16
        self.dval[q][si] = val
        ev = ("d", self.batch, q, si, val)
        self.ops[q].append({"fn": (lambda e: e.dma_start(out=out, in_=in_, **kw)), "waits": waits,
                            "inc": False, "dma": (self.dsem[q][si], 16)})
        for r in reads:
            r.rd.append(ev)
        for w in writes:
            w.lw = ev
            w.rd = []
        self.nops += 1
        return ev

    def emit(self, final=False):
        for e in self.ENGS:
            for o in reversed(self.ops[e]):
                if o["fn"] is not None and o["dma"] is None:
                    o["inc"] = True
                    break
            c = self.base[e]
            for o in self.ops[e]:
                if o["inc"]:
                    c += 1
                o["cnt"] = c
        newbase = {e: (self.ops[e][-1]["cnt"] if self.ops[e] else self.base[e]) for e in self.ENGS}
        dvals = {q: list(self.dval[q]) for q in self.DQ}

        def body(engine, ename):
            for o in self.ops[ename]:
                for ev in o["waits"]:
                    if ev[0] == "e":
                        engine.wait_ge(self.esem[ev[2]], self.ops[ev[2]][ev[3]]["cnt"])
                    else:
                        engine.wait_ge(self.dsem[ev[2]][ev[3]], ev[4])
                if o["fn"] is None:
                    continue
                ins = o["fn"](engine)
                if o["dma"] is not None:
                    ins.then_inc(o["dma"][0], o["dma"][1])
                elif o["inc"]:
                    ins.then_inc(self.esem[ename], 1)
            for e2 in self.ENGS:
                if e2 != ename and newbase[e2] > 0:
                    engine.wait_ge(self.esem[e2], newbase[e2])
            for q in self.DQ:
                for si in range(NDMASEM):
                    if dvals[q][si] > 0:
                        engine.wait_ge(self.dsem[q][si], dvals[q][si])

        with self.nc.Block() as block:
            @block.sync
            def _(eng):
                body(eng, "sp")

            @block.scalar
            def _(eng):
                body(eng, "act")

            @block.tensor
            def _(eng):
                body(eng, "pe")

            @block.vector
            def _(eng):
                body(eng, "dve")

            @block.gpsimd
            def _(eng):
                body(eng, "pool")

        self.base = newbase
        self.batch += 1
        self.ops = {e: [] for e in self.ENGS}
        self.waited = {e: {} for e in self.ENGS}


_UID = [0]


def uname(n):
    _UID[0] += 1
    return f"{n}_u{_UID[0]}"


class PsumPool:
    def __init__(self, nc, stack, names):
        self.banks = [stack.enter_context(nc.psum_tensor(uname(n), [128, 512], F32)) for n in names]
        self.bufs = [Buf(n, excl=True) for n in names]
        self.i = 0

    def get(self):
        k = self.i % len(self.banks)
        self.i += 1
        return self.banks[k], self.bufs[k]


def build_program(n_layers=DEPTH, dbg=None):
    nc = bass.Bass("TRN2", target_bir_lowering=False)

    def din(name, shape, dt=F32):
        return nc.dram_tensor(name, list(shape), dt, kind="ExternalInput").ap()

    def dscr(name, shape, dt):
        kind = "ExternalOutput" if (dbg and name in dbg) else "Internal"
        return nc.dram_tensor(name, list(shape), dt, kind=kind).ap()

    x_in = din("x", [L, D])
    cT_d = din("cT", [128, 8])
    w_ada_d = din("w_ada", [DEPTH, D, 3 * D])
    b_ada_d = din("b_ada", [DEPTH, 1, 3 * D])
    norm_g_d = din("norm_g", [DEPTH, 1, D])
    w_in_d = din("w_in", [DEPTH, D, NIN])
    convw_d = din("conv_wT", [DEPTH, 128, 12, 4])
    convb_d = din("conv_bT", [DEPTH, 128, 12])
    dtb_d = din("dt_bias", [DEPTH, 1, 16])
    alog_d = din("a_log", [DEPTH, 1, 16])
    dskip_d = din("d_skip", [DEPTH, 1, 16])
    sng_d = din("ssm_norm_g", [DEPTH, 1, D])
    wa_d = din("w_branch_a", [DEPTH, 512, D])
    ws_d = din("w_branch_s", [DEPTH, D, D])
    wo_d = din("w_out", [DEPTH, D, D])
    fg_d = din("final_g", [1, D])
    cosT_d = din("cosT", [128, L])
    sinT_d = din("sinT", [128, L])
    ident_d = din("ident", [128, 128])
    cbias_d = din("cbias", [128, 128])
    triu_d = din("triu", [128, 128])
    sut_d = din("sut", [128, 128])
    pow2_d = din("pow2", [128, NBIS + 2])
    out_d = nc.dram_tensor("out", [L, D], F32, kind="ExternalOutput").ap()

    X1 = dscr("X1", [L, D], F32)
    QT = dscr("QT", [8, 64, L], BF16)
    KT = dscr("KT", [64, L], BF16)
    QiT = dscr("QiT", [4, 64, L], BF16)
    KiT = dscr("KiT", [64, L], BF16)
    GA = dscr("GA", [8, 64, L], BF16)
    XBC = dscr("XBC", [1536, L], F32)
    GLA = dscr("GLA", [D, L], BF16)
    GLS = dscr("GLS", [D, L], BF16)
    VA = dscr("VA", [L, 65], BF16)
    WI = dscr("WI", [L, 4], F32)
    ZS = dscr("ZS", [L, D], BF16)
    DTs = dscr("DT", [L, 16], F32)
    OG = dscr("OG", [8, 64, L], BF16)
    YN = dscr("YN", [D, L], BF16)
    DBG = dscr("DBG", [128, 4096], F32)

    with ExitStack() as top:
        S = Sched(nc, top)

        def sbt(stack, name, shape, dt):
            return stack.enter_context(nc.sbuf_tensor(uname(name), list(shape), dt))

        ident = sbt(top, "ident", [128, 128], BF16); b_ident = Buf()
        identf = sbt(top, "identf", [128, 128], F32); b_identf = Buf()
        ones_f = sbt(top, "ones_f", [128, 128], F32); b_ones = Buf()
        ones_b = sbt(top, "ones_b", [128, 128], BF16); b_onesb = Buf()
        S.dma("pool", ident[:], ident_d, writes=[b_ident])
        S.dma("sp", identf[:], ident_d, writes=[b_identf])
        S.op("dve", lambda e: e.memset(ones_f[:], 1.0), writes=[b_ones])
        S.op("dve", lambda e: e.memset(ones_b[:], 1.0), writes=[b_onesb])
        S.emit()

        mods = [(sbt(top, "G_b", [128, D], F32), sbt(top, "SH_b", [128, D], F32), sbt(top, "GATE_b", [128, D], F32)) for _ in range(2)]
        for l in range(n_layers):
            x_src = x_in if l == 0 else X1
            with ExitStack() as lay:
                G_b, SH_b, GATE_b = mods[l % 2]
                if l == 0 or (dbg and "no_hoist" in dbg):
                    phase_adaln(nc, S, lay, l, cT_d, w_ada_d, b_ada_d, norm_g_d, G_b, SH_b, GATE_b, ones_f)
                if dbg and "stop_adaln" in dbg:
                    dump(nc, S, DBG, [G_b, SH_b, GATE_b])
                    break
                phase_proj(nc, S, l, x_src, w_in_d, dtb_d, cosT_d, sinT_d, G_b, SH_b, ident,
                           dict(QT=QT, KT=KT, QiT=QiT, KiT=KiT, GA=GA, XBC=XBC, GLA=GLA, GLS=GLS, VA=VA, WI=WI,
                                ZS=ZS, DT=DTs))
                if dbg and "stop_proj" in dbg:
                    break
                if not (dbg and "skip_attn" in dbg):
                    phase_attn(nc, S, l, QT, KT, QiT, KiT, GA, VA, WI, OG, cbias_d, pow2_d, ident_d, ones_f, ident, DBG, dbg)
                if dbg and "stop_attn" in dbg:
                    break
                phase_ssd(nc, S, l, XBC, ZS, DTs, YN, convw_d, convb_d, alog_d, dskip_d, sng_d, triu_d, sut_d,
                          ident, identf, ones_f, ones_b, DBG, dbg)
                if dbg and "stop_ssd" in dbg:
                    break
                nxt = None
                if l + 1 < n_layers and not (dbg and ("no_hoist" in dbg or "stop_layer" in dbg)):
                    Gn, SHn, GATEn = mods[(l + 1) % 2]
                    nxt = lambda stk, l=l, Gn=Gn, SHn=SHn, GATEn=GATEn: phase_adaln(nc, S, None, l + 1, cT_d, w_ada_d, b_ada_d, norm_g_d,
                                                                                   Gn, SHn, GATEn, ones_f, ext_stack=stk)
                phase_out(nc, S, l, x_src, X1, OG, YN, GLA, GLS, wa_d, ws_d, wo_d, GATE_b, nxt)
                if dbg and "stop_layer" in dbg:
                    break
        else:
            phase_final(nc, S, X1 if n_layers > 0 else x_in, fg_d, out_d)
    return nc


def dump(nc, S, DBG, tiles):
    off = 0
    for t in tiles:
        w = t.shape[1]
        S.dma("sp", DBG[0:t.shape[0], off:off + w], t[:])
        off += w
    S.emit()


def phase_adaln(nc, S, lay, l, cT_d, w_ada_d, b_ada_d, norm_g_d, G_b, SH_b, GATE_b, ones_f, ext_stack=None):
    with ExitStack() as st_own:
        st = ext_stack if ext_stack is not None else st_own
        def sb(name, shape, dt=F32):
            return st.enter_context(nc.sbuf_tensor(uname(name), list(shape), dt))
        cT = sb("cT", [128, 8]); b_cT = Buf()
        sc = sb("sc", [128, 8]); b_sc = Buf()
        sg = sb("sg", [128, 8]); b_sg = Buf()
        modrow = sb("modrow", [1, 3 * D]); b_mod = Buf()
        bada = sb("bada", [1, 3 * D]); b_bada = Buf()
        ng = sb("ng", [1, D]); b_ng = Buf()
        grow = sb("grow", [1, D]); b_grow = Buf()
        nwb = 2 if ext_stack is None else 1
        wts = [sb(f"wada{i}", [128, 8, 512]) for i in range(nwb)]
        b_wts = [Buf() for _ in range(nwb)]
        pp = PsumPool(nc, st, ["pa0", "pa1", "pa2", "pa3"] if ext_stack is None else ["pa0", "pa1"])

        S.dma("sp", cT[:], cT_d, writes=[b_cT])
        S.dma("sp", bada[:], b_ada_d[l], writes=[b_bada])
        S.dma("sp", ng[:], norm_g_d[l], writes=[b_ng])
        S.op("act", lambda e: e.activation(out=sg[:], in_=cT[:], func=AF.Sigmoid), reads=[b_cT], writes=[b_sg])
        S.op("dve", lambda e: e.tensor_tensor(out=sc[:], in0=cT[:], in1=sg[:], op=ALU.mult), reads=[b_cT, b_sg], writes=[b_sc])
        wv = w_ada_d[l].rearrange("(kc p) n -> p kc n", p=128)
        for nb in range(6):
            wt, bw = wts[nb % nwb], b_wts[nb % nwb]
            S.dma("sp", wt[:], wv[:, :, nb * 512:(nb + 1) * 512], writes=[bw])
            ps, bp = pp.get()
            for kc in range(8):
                S.op("pe", lambda e, kc=kc, wt=wt, ps=ps: e.matmul(ps[0:1, :], lhsT=sc[:, kc:kc + 1], rhs=wt[:, kc, :],
                                                                  start=(kc == 0), stop=(kc == 7)),
                     reads=[b_sc, bw], writes=[bp])
            S.op("dve", lambda e, nb=nb, ps=ps: e.tensor_tensor(out=modrow[:, nb * 512:(nb + 1) * 512], in0=ps[0:1, :],
                                                                in1=bada[:, nb * 512:(nb + 1) * 512], op=ALU.add),
                 reads=[bp, b_bada], writes=[b_mod])
        S.op("dve", lambda e: e.scalar_tensor_tensor(out=grow[:], in0=modrow[:, D:2 * D], scalar=1.0, in1=ng[:],
                                                     op0=ALU.add, op1=ALU.mult),
             reads=[b_mod, b_ng], writes=[b_grow])
        b_dst = Buf()
        for (row, bsrc, dst) in ((grow[:, :], b_grow, G_b), (modrow[:, 0:D], b_mod, SH_b), (modrow[:, 2 * D:3 * D], b_mod, GATE_b)):
            for hb in range(2):
                ps, bp = pp.get()
                S.op("pe", lambda e, ps=ps, row=row, hb=hb: e.matmul(ps[:], lhsT=ones_f[0:1, :], rhs=row[:, hb * 512:(hb + 1) * 512],
                                                                     start=True, stop=True),
                     reads=[bsrc], writes=[bp])
                S.op("act", lambda e, ps=ps, dst=dst, hb=hb: e.activation(out=dst[:, hb * 512:(hb + 1) * 512], in_=ps[:], func=AF.Copy),
                     reads=[bp], writes=[b_dst])
        if ext_stack is None:
            S.emit()


def phase_proj(nc, S, l, x_src, w_in_d, dtb_d, cosT_d, sinT_d, G_b, SH_b, ident, dst):
    with ExitStack() as st:
        def sb(name, shape, dt=F32):
            return st.enter_context(nc.sbuf_tensor(uname(name), list(shape), dt))
        hT = sb("hT", [128, 8, L], BF16)
        b_hT = [Buf() for _ in range(NT)]
        cosT = sb("cosT", [128, L], BF16); b_cos = Buf()
        sinT = sb("sinT", [128, L], BF16); b_sin = Buf()
        S.dma("pool", cosT[:], cosT_d, writes=[b_cos])
        S.dma("pool", sinT[:], sinT_d, writes=[b_sin])
        tps = [st.enter_context(nc.psum_tensor(uname("tp"), [128, 1024], BF16)) for _ in range(2)]
        b_tps = [Buf(excl=True), Buf(excl=True)]
        pp = PsumPool(nc, st, [f"pj{i}" for i in range(6)])
        NB = 4
        xt = [sb("xt", [128, D]) for _ in range(NB)]; b_xt = [Buf() for _ in range(NB)]
        junk = sb("junk", [128, D], BF16); b_junk = Buf()
        ss = [sb("ss", [128, 4]) for _ in range(NB)]; b_ss = [Buf() for _ in range(NB)]
        h1 = [sb("h1", [128, D]) for _ in range(NB)]; b_h1 = [Buf() for _ in range(NB)]
        hb = [sb("hb", [128, D], BF16) for _ in range(NB)]; b_hb = [Buf() for _ in range(NB)]
        def stage_x(tt):
            k = tt % NB
            tok = slice(tt * 128, (tt + 1) * 128)
            S.dma("sp", xt[k][:], x_src[tok, :], writes=[b_xt[k]])
            S.op("act", lambda e: e.activation(out=junk[:], in_=xt[k][:], func=AF.Square, accum_out=ss[k][:, 0:1]),
                 reads=[b_xt[k]], writes=[b_junk, b_ss[k]])
            S.op("dve", lambda e: e.tensor_scalar(out=ss[k][:, 1:2], in0=ss[k][:, 0:1], scalar1=1.0 / D, scalar2=EPS,
                                                  op0=ALU.mult, op1=ALU.add), reads=[b_ss[k]], writes=[b_ss[k]])
            S.op("act", lambda e: e.activation(out=ss[k][:, 2:3], in_=ss[k][:, 1:2], func=AF.Ln), reads=[b_ss[k]], writes=[b_ss[k]])
            S.op("act", lambda e: e.activation(out=ss[k][:, 3:4], in_=ss[k][:, 2:3], func=AF.Exp, scale=-0.5), reads=[b_ss[k]], writes=[b_ss[k]])
            S.op("dve", lambda e: e.scalar_tensor_tensor(out=h1[k][:], in0=xt[k][:], scalar=ss[k][:, 3:4], in1=G_b[:],
                                                         op0=ALU.mult, op1=ALU.mult),
                 reads=[b_xt[k], b_ss[k]], writes=[b_h1[k]])
            S.op("pool", lambda e: e.tensor_tensor(out=hb[k][:], in0=h1[k][:], in1=SH_b[:], op=ALU.add),
                 reads=[b_h1[k]], writes=[b_hb[k]])

        def stage_y(tt):
            k = tt % NB
            tok = slice(tt * 128, (tt + 1) * 128)
            tp, btp = tps[tt % 2], b_tps[tt % 2]
            for kc in range(8):
                S.op("pe", lambda e, kc=kc: e.transpose(out=tp[:, kc * 128:(kc + 1) * 128],
                                                        in_=hb[k][:, kc * 128:(kc + 1) * 128], identity=ident[:]),
                     reads=[b_hb[k]], writes=[btp])
            S.op("act", lambda e: e.activation(out=hT[:, :, tok], in_=tp[:].rearrange("p (k t) -> p k t", k=8), func=AF.Copy),
                 reads=[btp], writes=[b_hT[tt]])

        stage_x(0)
        stage_x(1)
        for tt in range(NT):
            if tt + 2 < NT:
                stage_x(tt + 2)
            stage_y(tt)

        wv = w_in_d[l].rearrange("(kc p) n -> p kc n", p=128)
        tiles = []
        for t in range(4):
            tiles.append(dict(cols=[(Q0 + 128 * t, 64), (Q0 + 128 * t + 64, 64)], rope=True, kind="rope",
                              dst=[dst["QT"][2 * t], dst["QT"][2 * t + 1]]))
        for t in range(2):
            tiles.append(dict(cols=[(QI0 + 128 * t, 64), (QI0 + 128 * t + 64, 64)], rope=True, kind="rope",
                              dst=[dst["QiT"][2 * t], dst["QiT"][2 * t + 1]]))
        tiles.append(dict(cols=[(K0, 64), (KI0, 64)], rope=True, kind="rope", dst=[dst["KT"], dst["KiT"]]))
        for t in range(4):
            tiles.append(dict(cols=[(GA0 + 128 * t, 128)], rope=False, kind="silu",
                              dst=[dst["GA"][2 * t], dst["GA"][2 * t + 1]]))
        for t in range(12):
            tiles.append(dict(cols=[(XBC0 + 128 * t, 128)], rope=False, kind="copy", dst=[dst["XBC"][t * 128:(t + 1) * 128]]))
        for t in range(8):
            tiles.append(dict(cols=[(GLA0 + 128 * t, 128)], rope=False, kind="sig", dst=[dst["GLA"][t * 128:(t + 1) * 128]]))
        for t in range(8):
            tiles.append(dict(cols=[(GLS0 + 128 * t, 128)], rope=False, kind="sig", dst=[dst["GLS"][t * 128:(t + 1) * 128]]))
        wts = [sb("wt", [128, 8, 128], BF16) for _ in range(2)]
        wtps = [sb("wtp", [128, 8, 128], BF16) for _ in range(2)]
        b_w = [[Buf() for _ in range(2)] for _ in range(2)]
        b_wp = [[Buf() for _ in range(4)] for _ in range(2)]
        NO = 3
        t1 = [sb("t1", [128, 512]) for _ in range(2)]; b_t1 = [Buf(), Buf()]
        t2 = [sb("t2", [128, 512]) for _ in range(2)]; b_t2 = [Buf(), Buf()]
        ob = [sb("ob", [128, 512], BF16) for _ in range(NO)]; b_ob = [Buf() for _ in range(NO)]
        of = [sb("of", [128, 512]) for _ in range(NO)]; b_of = [Buf() for _ in range(NO)]
        cnt = 0
        def load_w(ti):
            T = tiles[ti]
            w = ti % 2
            wt, wtp = wts[w], wtps[w]
            o = 0
            rb = []
            for ci, (c0, n) in enumerate(T["cols"]):
                S.dma("pool", wt[:, :, o:o + n], wv[:, :, c0:c0 + n], writes=[b_w[w][ci]])
                rb.append(b_w[w][ci])
                o += n
            rbp = []
            if T["rope"]:
                o = 0
                for ci, (c0, n) in enumerate(T["cols"]):
                    S.dma("pool", wtp[:, :, o:o + 32], wv[:, :, c0 + 32:c0 + 64], writes=[b_wp[w][2 * ci]])
                    S.dma("pool", wtp[:, :, o + 32:o + 64], wv[:, :, c0:c0 + 32], writes=[b_wp[w][2 * ci + 1]])
                    rbp += [b_wp[w][2 * ci], b_wp[w][2 * ci + 1]]
                    o += 64
            return rb, rbp

        nxt_w = load_w(0)
        for ti, T in enumerate(tiles):
            w = ti % 2
            wt, wtp = wts[w], wtps[w]
            rb, rbp = nxt_w
            if ti + 1 < len(tiles):
                nxt_w = load_w(ti + 1)
            for c in range(8):
                tok = slice(c * 512, (c + 1) * 512)
                hbufs = b_hT[4 * c:4 * c + 4]
                ps, bp = pp.get()
                for kc in range(8):
                    S.op("pe", lambda e, ps=ps, wt=wt, kc=kc, tok=tok: e.matmul(ps[:], lhsT=wt[:, kc, :], rhs=hT[:, kc, tok],
                                                                               start=(kc == 0), stop=(kc == 7)),
                         reads=rb + hbufs, writes=[bp])
                kind = T["kind"]
                if kind == "rope":
                    psp, bpp = pp.get()
                    for kc in range(8):
                        S.op("pe", lambda e, psp=psp, wtp=wtp, kc=kc, tok=tok: e.matmul(psp[:], lhsT=wtp[:, kc, :], rhs=hT[:, kc, tok],
                                                                                     start=(kc == 0), stop=(kc == 7)),
                             reads=rbp + hbufs, writes=[bpp])
                    a = cnt % 2
                    k = cnt % NO
                    S.op("dve", lambda e, a=a, ps=ps, tok=tok: e.tensor_tensor(out=t1[a][:], in0=ps[:], in1=cosT[:, tok], op=ALU.mult),
                         reads=[bp, b_cos], writes=[b_t1[a]])
                    S.op("dve", lambda e, a=a, psp=psp, tok=tok: e.tensor_tensor(out=t2[a][:], in0=psp[:], in1=sinT[:, tok], op=ALU.mult),
                         reads=[bpp, b_sin], writes=[b_t2[a]])
                    S.op("pool", lambda e, a=a, k=k: e.tensor_tensor(out=ob[k][:], in0=t1[a][:], in1=t2[a][:], op=ALU.add),
                         reads=[b_t1[a], b_t2[a]], writes=[b_ob[k]])
                    S.dma("sp", T["dst"][0][:, tok], ob[k][0:64, :], reads=[b_ob[k]])
                    S.dma("sp", T["dst"][1][:, tok], ob[k][64:128, :], reads=[b_ob[k]])
                elif kind == "copy":
                    k = cnt % NO
                    S.op("act", lambda e, k=k, ps=ps: e.activation(out=of[k][:], in_=ps[:], func=AF.Copy), reads=[bp], writes=[b_of[k]])
                    S.dma("sp", T["dst"][0][:, tok], of[k][:], reads=[b_of[k]])
                else:
                    k = cnt % NO
                    fn = AF.Silu if kind == "silu" else AF.Sigmoid
                    S.op("act", lambda e, k=k, ps=ps, fn=fn: e.activation(out=ob[k][:], in_=ps[:], func=fn), reads=[bp], writes=[b_ob[k]])
                    if len(T["dst"]) == 2:
                        S.dma("sp", T["dst"][0][:, tok], ob[k][0:64, :], reads=[b_ob[k]])
                        S.dma("sp", T["dst"][1][:, tok], ob[k][64:128, :], reads=[b_ob[k]])
                    else:
                        S.dma("sp", T["dst"][0][:, tok], ob[k][:], reads=[b_ob[k]])
                cnt += 1
        wz = sb("wz", [128, 8, 1024], BF16); b_wz = Buf()
        S.dma("pool", wz[:], wv[:, :, Z0:Z0 + 1024], writes=[b_wz])
        wsm = sb("wsm", [128, 8, 84], BF16); b_wsm = [Buf() for _ in range(3)]
        S.dma("pool", wsm[:, :, 0:64], wv[:, :, V0:V0 + 64], writes=[b_wsm[0]])
        S.dma("pool", wsm[:, :, 64:68], wv[:, :, WI0:WI0 + 4], writes=[b_wsm[1]])
        S.dma("pool", wsm[:, :, 68:84], wv[:, :, DT0:DT0 + 16], writes=[b_wsm[2]])
        dtb = sb("dtb", [128, 16]); b_dtb = Buf()
        S.dma("sp", dtb[:], dtb_d[l].partition_broadcast(128), writes=[b_dtb])
        zb = [sb("zb", [128, D], BF16) for _ in range(2)]; b_zb = [Buf(), Buf()]
        va = [sb("va", [128, 65], BF16) for _ in range(2)]; b_va = [Buf(), Buf()]
        wib = [sb("wib", [128, 4]) for _ in range(2)]; b_wib = [Buf(), Buf()]
        wiall = sb("wiall", [128, NT, 4]); b_wiall = Buf()
        dtall = sb("dtall", [128, 2, NT, 16]); b_dtall = Buf()
        for tt in range(NT):
            k = tt % 2
            tok = slice(tt * 128, (tt + 1) * 128)
            for nb in range(2):
                ps, bp = pp.get()
                for kc in range(8):
                    S.op("pe", lambda e, ps=ps, kc=kc, tok=tok, nb=nb: e.matmul(ps[:], lhsT=hT[:, kc, tok], rhs=wz[:, kc, nb * 512:(nb + 1) * 512],
                                                                               start=(kc == 0), stop=(kc == 7)),
                         reads=[b_wz, b_hT[tt]], writes=[bp])
                S.op("act", lambda e, k=k, ps=ps, nb=nb: e.activation(out=zb[k][:, nb * 512:(nb + 1) * 512], in_=ps[:], func=AF.Silu),
                     reads=[bp], writes=[b_zb[k]])
            S.dma("sp", dst["ZS"][tok, :], zb[k][:], reads=[b_zb[k]])
            ps, bp = pp.get()
            for kc in range(8):
                S.op("pe", lambda e, ps=ps, kc=kc, tok=tok: e.matmul(ps[:, 0:84], lhsT=hT[:, kc, tok], rhs=wsm[:, kc, :],
                                                                    start=(kc == 0), stop=(kc == 7)),
                     reads=b_wsm + [b_hT[tt]], writes=[bp])
            S.op("pool", lambda e, k=k: e.memset(va[k][:, 64:65], 1.0), writes=[b_va[k]])
            S.op("act", lambda e, k=k, ps=ps: e.activation(out=va[k][:, 0:64], in_=ps[:, 0:64], func=AF.Copy), reads=[bp], writes=[b_va[k]])
            S.dma("sp", dst["VA"][tok, :], va[k][:], reads=[b_va[k]])
            S.op("dve", lambda e, ps=ps, tt=tt: e.tensor_scalar(out=wiall[:, tt, :], in0=ps[:, 64:68], scalar1=0.5, scalar2=None, op0=ALU.mult),
                 reads=[bp], writes=[b_wiall])
            S.op("dve", lambda e, ps=ps, tt=tt: e.tensor_tensor(out=dtall[:, 0, tt, :], in0=ps[:, 68:84], in1=dtb[:], op=ALU.add),
                 reads=[bp, b_dtb], writes=[b_dtall])
        S.dma("sp", dst["WI"].rearrange("(j p) d -> p j d", p=128), wiall[:], reads=[b_wiall])
        S.op("act", lambda e: e.activation(out=dtall[:, 1, :, :], in_=dtall[:, 0, :, :], func=AF.Exp), reads=[b_dtall], writes=[b_dtall])
        S.op("dve", lambda e: e.tensor_scalar(out=dtall[:, 0, :, :], in0=dtall[:, 1, :, :], scalar1=1.0, scalar2=None, op0=ALU.add),
             reads=[b_dtall], writes=[b_dtall])
        S.op("act", lambda e: e.activation(out=dtall[:, 1, :, :], in_=dtall[:, 0, :, :], func=AF.Ln), reads=[b_dtall], writes=[b_dtall])
        S.dma("sp", dst["DT"].rearrange("(j p) h -> p j h", p=128), dtall[:, 1, :, :], reads=[b_dtall])
        S.emit()


def phase_attn(nc, S, l, QT, KT, QiT, KiT, GA, VA, WI, OG, cbias_d, pow2_d, ident_d, ones_f, ident_bf, DBG, dbg):
    import os
    with ExitStack() as st:
        def sb(name, shape, dt=F32):
            return st.enter_context(nc.sbuf_tensor(uname(name), list(shape), dt))
        KTa = sb("KTa", [65, L], BF16); b_KTa = Buf()
        KiTs = sb("KiTs", [64, L], BF16); b_KiT = Buf()
        VAs = sb("VAs", [128, NT, 65], BF16); b_VA = Buf()
        WIs = sb("WIs", [128, NT, 4]); b_WI = Buf()
        cb = sb("cb", [128, 128]); b_cb = Buf()
        pw = sb("pw", [128, NBIS + 2]); b_pw = Buf()
        Irep = sb("Irep", [128, 8, 128], BF16); b_Irep = [Buf() for _ in range(8)]
        sel = sb("sel", [64, 65], BF16); b_sel = Buf()
        ksq = sb("ksq", [64, L], BF16); b_ksq = Buf()
        km = sb("km", [65, 16]); b_km = Buf()
        S.dma("sp", KTa[0:64, :], KT, writes=[b_KTa])
        b_KTa1 = Buf()
        S.op("dve", lambda e: e.memset(KTa[64:65, :], 1.0), reads=[], writes=[b_KTa1])
        S.dma("sp", KiTs[:], KiT, writes=[b_KiT])
        S.dma("sp", VAs[:], VA.rearrange("(j p) d -> p j d", p=128), writes=[b_VA])
        S.dma("sp", WIs[:], WI.rearrange("(j p) d -> p j d", p=128), writes=[b_WI])
        S.dma("sp", cb[:], cbias_d, writes=[b_cb])
        S.dma("sp", pw[:], pow2_d, writes=[b_pw])
        for h in range(8):
            S.dma("pool", Irep[:, h, :], ident_d, writes=[b_Irep[h]])
        S.op("dve", lambda e: e.memset(sel[:], 0.0), writes=[b_sel])
        S.op("dve", lambda e: e.memset(sel[:, 64:65], 1.0), writes=[b_sel])
        pO = [st.enter_context(nc.psum_tensor(uname("pO"), [128, 512], F32)) for _ in range(2)]
        b_pO = [Buf(excl=True), Buf(excl=True)]
        pST = PsumPool(nc, st, [f"pst{i}" for i in range(2)])
        pX = PsumPool(nc, st, ["px0", "px1", "px2"])
        pS = PsumPool(nc, st, ["psc0"])
        S.op("act", lambda e: e.activation(out=ksq[:], in_=KTa[0:64, :], func=AF.Square), reads=[b_KTa], writes=[b_ksq])
        for c in range(8):
            ps, bp = pX.get()
            S.op("pe", lambda e, ps=ps, c=c: e.matmul(ps[0:65, :], lhsT=sel[:], rhs=ksq[:, c * 512:(c + 1) * 512], start=True, stop=True),
                 reads=[b_sel, b_ksq], writes=[bp])
            S.op("dve", lambda e, ps=ps, c=c: e.tensor_reduce(out=km[64:65, c:c + 1], in_=ps[64:65, :], axis=AX.X, op=ALU.max),
                 reads=[bp], writes=[b_km])
        S.op("dve", lambda e: e.tensor_reduce(out=km[64:65, 8:9], in_=km[64:65, 0:8], axis=AX.X, op=ALU.max), reads=[b_km], writes=[b_km])

        NB = 2
        NR = 4
        qa = [sb("qa", [65, 8, 128], BF16) for _ in range(NR)]; b_qa = [Buf() for _ in range(NR)]; b_qar = [Buf() for _ in range(NR)]
        qi = [sb("qi", [64, 4, 128], BF16) for _ in range(NB)]; b_qi = [Buf() for _ in range(NB)]
        ga = [sb("ga", [64, 8, 128], BF16) for _ in range(2)]; b_ga = [Buf() for _ in range(2)]
        qsq = sb("qsq", [64, 1024], BF16); b_qsq = Buf()
        qtmp = sb("qtmp", [65, 1024]); b_qtmp = Buf()
        wab = [sb("wab", [128, 8]) for _ in range(NB)]; b_wab = [Buf() for _ in range(NB)]
        score = [sb("score", [128, L]) for _ in range(NB)]; b_score = [Buf() for _ in range(NB)]
        mb = [sb("mb", [128, L], BF16) for _ in range(NR)]; b_mb = [Buf() for _ in range(NR)]
        junk = [sb("junkb", [128, L], BF16) for _ in range(NB)]; b_junk = [Buf() for _ in range(NB)]
        NRL = 8
        rl = [sb("rl", [128, 512], BF16) for _ in range(NRL)]; b_rl = [Buf() for _ in range(NRL)]
        Dh = [sb("Dh", [128, 4, 128], BF16) for _ in range(NB)]; b_Dh = [Buf() for _ in range(NB)]
        bs = [sb("bs", [128, 8]) for _ in range(NB)]; b_bs = [Buf() for _ in range(NB)]
        ab = [sb("ab", [128, 8]) for _ in range(NB)]; b_ab = [Buf() for _ in range(NB)]
        p2 = sb("p2", [128, 2]); b_p2 = Buf()
        q2 = sb("q2", [128, 2]); b_q2 = Buf()
        C2 = sb("C2", [128, 2, 4]); b_cnt = [Buf(), Buf()]; b_t2 = Buf()
        ab2 = sb("ab2", [128, 2]); b_ab2 = [Buf(), Buf()]
        Sp2 = sb("Sp2", [128, 2, NBIS + 2]); b_S2 = Buf()
        b_junkA = [Buf() for _ in range(NB)]
        ALPHA = float(os.environ.get("ATT_ALPHA", "0.7"))
        Sp = [sb("Sp", [128, NBIS + 2]) for _ in range(NB)]
        Sn = [sb("Sn", [128, NBIS + 2]) for _ in range(NB)]
        b_S = [Buf() for _ in range(NB)]
        PT = [sb("PT", [128, 1024], BF16) for _ in range(3)]; b_PT = [Buf() for _ in range(3)]
        rr = sb("rr", [65, 1024]); b_rr = Buf()
        rbc = sb("rbc", [64, 1024]); b_rbc = Buf()
        o1 = sb("o1", [64, 1024]); b_o1 = Buf()
        ogb = [sb("ogb", [64, 8, 128], BF16) for _ in range(2)]; b_ogb = [Buf(), Buf()]
        rlc = [0]
        ptc = [0]

        AMX_ = os.environ.get("ATT_AMX", "dve")
        ACT_EVERY = int(os.environ.get("ATT_ACT_EVERY", "4"))

        def on_act(i):
            return (i % ACT_EVERY == ACT_EVERY - 1) and not (dbg and "no_act_bis" in dbg)

        def pre(i):
            k = i % NB
            k4 = i % NR
            n = 128 * (i + 1)
            tok = slice(i * 128, (i + 1) * 128)
            S.dma("sp", qa[k4][0:64, :, :], QT[:, :, tok].rearrange("h d t -> d h t"), writes=[b_qa[k4]])
            S.dma("sp", qi[k][:], QiT[:, :, tok].rearrange("h d t -> d h t"), writes=[b_qi[k]])
            S.op("act", lambda e: e.activation(out=wab[k][:, 0:4], in_=WIs[:, i, :], func=AF.Abs, scale=0.125),
                 reads=[b_WI], writes=[b_wab[k]])
            S.op("dve", lambda e: e.tensor_scalar(out=wab[k][:, 4:8], in0=WIs[:, i, :], scalar1=0.0, scalar2=2.0,
                                                  op0=ALU.is_ge, op1=ALU.mult), reads=[b_WI], writes=[b_wab[k]])
            S.op("dve", lambda e: e.tensor_scalar(out=wab[k][:, 4:8], in0=wab[k][:, 4:8], scalar1=-1.0, scalar2=None,
                                                  op0=ALU.add), reads=[b_wab[k]], writes=[b_wab[k]])
            S.op("act", lambda e: e.activation(out=qsq[:], in_=qa[k4][0:64, :, :].rearrange("p h t -> p (h t)"), func=AF.Square),
                 reads=[b_qa[k4]], writes=[b_qsq])
            for hb in range(2):
                ps, bp = pX.get()
                S.op("pe", lambda e, ps=ps, hb=hb: e.matmul(ps[0:65, :], lhsT=sel[:], rhs=qsq[:, hb * 512:(hb + 1) * 512], start=True, stop=True),
                     reads=[b_sel, b_qsq], writes=[bp])
                S.op("act", lambda e, ps=ps, hb=hb: e.activation(out=qtmp[64:65, hb * 512:(hb + 1) * 512], in_=ps[64:65, :], func=AF.Sqrt,
                                                                 scale=km[64:65, 8:9]), reads=[bp, b_km], writes=[b_qtmp])
                S.op("dve", lambda e, hb=hb: e.tensor_scalar(out=qa[k4][64:65, hb * 4:(hb + 1) * 4, :].rearrange("p h t -> p (h t)"),
                                                             in0=qtmp[64:65, hb * 512:(hb + 1) * 512], scalar1=-1.0, scalar2=None, op0=ALU.mult),
                     reads=[b_qtmp], writes=[b_qar[k4]])
            S.op("dve", lambda e: e.tensor_tensor(out=Dh[k][:], in0=ident_bf[:].unsqueeze(1).to_broadcast([128, 4, 128]),
                                                  in1=wab[k][:, 4:8].unsqueeze(2).to_broadcast([128, 4, 128]), op=ALU.mult),
                 reads=[b_wab[k]], writes=[b_Dh[k]])
            def dots(c0):
                w = min(512, n - c0)
                rs = []
                for h in range(4):
                    ps, bp = pX.get()
                    S.op("pe", lambda e, ps=ps, h=h, c0=c0, w=w: e.matmul(ps[:, 0:w], lhsT=qi[k][:, h, :], rhs=KiTs[:, c0:c0 + w], start=True, stop=True),
                         reads=[b_qi[k], b_KiT], writes=[bp])
                    r = rlc[0] % NRL
                    rlc[0] += 1
                    rs.append(r)
                    if h < 2:
                        S.op("act", lambda e, ps=ps, h=h, w=w, r=r: e.activation(out=rl[r][:, 0:w], in_=ps[:, 0:w], func=AF.Relu, scale=wab[k][:, h:h + 1]),
                             reads=[bp, b_wab[k]], writes=[b_rl[r]])
                    else:
                        S.op("dve", lambda e, ps=ps, h=h, w=w, r=r: e.tensor_scalar(out=rl[r][:, 0:w], in0=ps[:, 0:w], scalar1=wab[k][:, h:h + 1], scalar2=0.0,
                                                                                   op0=ALU.mult, op1=ALU.max),
                             reads=[bp, b_wab[k]], writes=[b_rl[r]])
                return (c0, w, rs)

            def signs(c0, w, rs):
                pss, bps = pS.get()
                for h in range(4):
                    r = rs[h]
                    S.op("pe", lambda e, pss=pss, h=h, w=w, r=r: e.matmul(pss[:, 0:w], lhsT=Dh[k][:, h, :], rhs=rl[r][:, 0:w], start=(h == 0), stop=(h == 3)),
                         reads=[b_Dh[k], b_rl[r]], writes=[bps])
                if (c0 // 512) % 2 == 0:
                    S.op("act", lambda e, pss=pss, w=w, c0=c0: e.activation(out=score[k][:, c0:c0 + w], in_=pss[:, 0:w], func=AF.Copy),
                         reads=[bps], writes=[b_score[k]])
                else:
                    S.op("dve", lambda e, pss=pss, w=w, c0=c0: e.tensor_copy(out=score[k][:, c0:c0 + w], in_=pss[:, 0:w]),
                         reads=[bps], writes=[b_score[k]])

            pend = None
            for c0 in range(0, n, 512):
                cur = dots(c0)
                if pend is not None:
                    signs(*pend)
                pend = cur
            signs(*pend)
            B = bs[k]
            S.op(AMX_, lambda e: e.tensor_reduce(out=B[:, 0:1], in_=score[k][:, 0:n], axis=AX.X, op=ALU.max, apply_absolute_value=True),
                 reads=[b_score[k]], writes=[b_bs[k]])
            S.op("dve", lambda e: e.tensor_scalar(out=B[:, 1:2], in0=B[:, 0:1], scalar1=1.001, scalar2=1e-6, op0=ALU.mult, op1=ALU.add),
                 reads=[b_bs[k]], writes=[b_bs[k]])
            S.op("dve", lambda e: e.tensor_tensor(out=score[k][:, i * 128:(i + 1) * 128], in0=score[k][:, i * 128:(i + 1) * 128], in1=cb[:], op=ALU.add),
                 reads=[b_score[k], b_cb], writes=[b_score[k]])
            S.op("dve", lambda e: e.tensor_scalar(out=Sp2[:, k, :], in0=pw[:], scalar1=B[:, 1:2], scalar2=None, op0=ALU.mult),
                 reads=[b_bs[k], b_pw], writes=[b_S2])
            S.op("dve", lambda e: e.tensor_scalar(out=q2[:, k:k + 1], in0=B[:, 1:2], scalar1=-1.0, scalar2=None, op0=ALU.mult),
                 reads=[b_bs[k]], writes=[b_q2])
            S.op("dve", lambda e: e.tensor_tensor(out=p2[:, k:k + 1], in0=Sp2[:, k, 0:1], in1=q2[:, k:k + 1], op=ALU.add),
                 reads=[b_S2, b_q2], writes=[b_p2])
            if split(n) == n:
                S.op("dve", lambda e: e.memset(ab2[:, k:k + 1], 0.0), writes=[b_ab2[k]])

        def split(n):
            nA = int(round(ALPHA * n / 128.0)) * 128
            nA = max(128, min(n, nA))
            if n - nA < 256:
                nA = n
            return nA

        def bis_iter_group(tiles, j):
            for i in tiles:
                k = i % NB
                n = 128 * (i + 1)
                nA = split(n)
                nB = n - nA
                S.op("dve", lambda e, k=k, nA=nA, nB=nB: e.tensor_scalar(out=junk[k][:, 0:nA], in0=score[k][:, 0:nA], scalar1=p2[:, k:k + 1],
                                                                        scalar2=float(-(256.0 - nB / 2.0)), op0=ALU.is_ge, op1=ALU.add,
                                                                        accum_out=C2[:, k, 0:1]),
                     reads=[b_score[k], b_p2], writes=[b_junk[k], b_cnt[k]])
                if nB > 0:
                    S.op("act", lambda e, k=k, nA=nA, n=n: e.activation(out=junk[k][:, nA:n], in_=score[k][:, nA:n], func=AF.Sign, bias=p2[:, k:k + 1], scale=-1.0,
                                                                         accum_out=ab2[:, k:k + 1]), reads=[b_score[k], b_p2], writes=[b_junkA[k], b_ab2[k]])
            S.op("dve", lambda e: e.scalar_tensor_tensor(out=C2[:, :, 1], in0=ab2[:], scalar=-0.5, in1=C2[:, :, 0], op0=ALU.mult, op1=ALU.add),
                 reads=b_ab2 + b_cnt, writes=[b_t2])
            S.op("dve", lambda e: e.scalar_tensor_tensor(out=C2[:, :, 2], in0=C2[:, :, 1], scalar=0.0, in1=Sp2[:, :, j], op0=ALU.is_ge, op1=ALU.mult),
                 reads=[b_t2, b_S2], writes=[b_t2])
            S.op("dve", lambda e: e.tensor_tensor(out=q2[:], in0=q2[:], in1=C2[:, :, 2], op=ALU.add), reads=[b_q2, b_t2], writes=[b_q2])
            S.op("dve", lambda e: e.tensor_tensor(out=p2[:], in0=q2[:], in1=Sp2[:, :, j + 1], op=ALU.add), reads=[b_q2, b_S2], writes=[b_p2])

        def post(i):
            k = i % NB
            k4 = i % NR
            n = 128 * (i + 1)
            B = bs[k]
            S.op("dve", lambda e: e.tensor_scalar(out=mb[k4][:, 0:n], in0=score[k][:, 0:n], scalar1=q2[:, k:k + 1], scalar2=-30000.0,
                                                  op0=ALU.is_lt, op1=ALU.mult), reads=[b_score[k], b_q2], writes=[b_mb[k4]])
            if dbg and "dump_attn" in dbg and i == dbg["dump_attn"]:
                S.dma("sp", DBG[:, 0:n], score[k][:, 0:n], reads=[b_score[k]])
                S.dma("sp", DBG[:, 4080:4082], bs[k][:, 0:2], reads=[b_bs[k]])
                S.dma("sp", DBG[:, 4085:4086], q2[:, k:k + 1], reads=[b_q2], allow_slow_non_contiguous=True)

        def stage1_group(tiles, units):
            nslots = len(tiles) + NBIS
            per = -(-len(units) // nslots) if units else 0
            pos = [0]

            def drain(final=False):
                m = len(units) if final else min(len(units), pos[0] + per)
                while pos[0] < m:
                    units[pos[0]]()
                    pos[0] += 1

            for t in tiles:
                pre(t)
                drain()
            for j in range(NBIS):
                bis_iter_group(tiles, j)
                drain()
            for t in tiles:
                post(t)
            drain(final=True)

        def stage2_units(i):
            k = i % NR
            tok = slice(i * 128, (i + 1) * 128)
            g = i % 2
            units = []

            pidx = {}

            def block(j):
                if j == 0:
                    S.dma("sp", ga[g][:], GA[:, :, tok].rearrange("h d t -> d h t"), writes=[b_ga[g]])
                ks = slice(j * 128, (j + 1) * 128)
                p = ptc[0] % 3
                ptc[0] += 1
                pidx[j] = p
                for hb in range(2):
                    ps, bp = pST.get()
                    S.op("pe", lambda e, ps=ps, hb=hb, ks=ks: e.matmul(ps[:], lhsT=KTa[:, ks], rhs=qa[k][:, hb * 4:(hb + 1) * 4, :], start=True, stop=False),
                         reads=[b_KTa, b_KTa1, b_qa[k], b_qar[k]], writes=[bp])
                    S.op("pe", lambda e, ps=ps, hb=hb, ks=ks: e.matmul(ps[:], lhsT=mb[k][:, ks], rhs=Irep[:, hb * 4:(hb + 1) * 4, :], start=False, stop=True),
                         reads=[b_mb[k]] + b_Irep, writes=[bp])
                    S.op("act", lambda e, ps=ps, hb=hb, p=p: e.activation(out=PT[p][:, hb * 512:(hb + 1) * 512], in_=ps[:], func=AF.Exp, scale=0.125),
                         reads=[bp], writes=[b_PT[p]])

            def pv(j):
                p = pidx[j]
                for hb in range(2):
                    S.op("pe", lambda e, hb=hb, p=p, j=j: e.matmul(pO[hb][0:65, :], lhsT=VAs[:, j, :], rhs=PT[p][:, hb * 512:(hb + 1) * 512],
                                                                  start=(j == 0), stop=(j == i)), reads=[b_VA, b_PT[p]], writes=[b_pO[hb]])

            def fin():
              for hb in range(2):
                hs = slice(hb * 512, (hb + 1) * 512)
                S.op("act", lambda e, hb=hb, hs=hs: e.activation(out=rr[64:65, hs], in_=pO[hb][64:65, :], func=AF.Ln), reads=[b_pO[hb]], writes=[b_rr])
                S.op("act", lambda e, hs=hs: e.activation(out=rr[64:65, hs], in_=rr[64:65, hs], func=AF.Exp, scale=-1.0), reads=[b_rr], writes=[b_rr])
                ps, bp = pST.get()
                S.op("pe", lambda e, ps=ps, hs=hs: e.matmul(ps[0:64, :], lhsT=ones_f[64:65, 0:64], rhs=rr[64:65, hs], start=True, stop=True),
                     reads=[b_rr], writes=[bp])
                S.op("act", lambda e, ps=ps, hs=hs: e.activation(out=rbc[:, hs], in_=ps[0:64, :], func=AF.Copy), reads=[bp], writes=[b_rbc])
                S.op("dve", lambda e, hb=hb, hs=hs: e.tensor_tensor(out=o1[:, hs], in0=pO[hb][0:64, :], in1=rbc[:, hs], op=ALU.mult),
                     reads=[b_pO[hb], b_rbc], writes=[b_o1])
              S.op("pool", lambda e: e.tensor_tensor(out=ogb[g][:].rearrange("p h t -> p (h t)"), in0=o1[:],
                                                     in1=ga[g][:].rearrange("p h t -> p (h t)"), op=ALU.mult),
                   reads=[b_o1, b_ga[g]], writes=[b_ogb[g]])
              S.dma("sp", OG[:, :, tok].rearrange("h d t -> d h t"), ogb[g][:], reads=[b_ogb[g]])

            def mk(j):
                def u():
                    block(j)
                    if j > 0:
                        pv(j - 1)
                return u

            for j in range(i + 1):
                units.append(mk(j))

            def last():
                pv(i)
                fin()
            units.append(last)
            return units

        nq = NT if not (dbg and "nq" in dbg) else dbg["nq"]
        groups = [list(range(a, min(a + NB, nq))) for a in range(0, nq, NB)]
        prev = []
        for grp in groups:
            stage1_group(grp, prev)
            prev = []
            for t in grp:
                prev += stage2_units(t)
        for u in prev:
            u()
        S.emit()


def phase_ssd(nc, S, l, XBC, ZS, DTs, YN, convw_d, convb_d, alog_d, dskip_d, sng_d, triu_d, sut_d, ident, identf, ones_f, ones_b, DBG, dbg):
    with ExitStack() as st:
        def sb(name, shape, dt=F32):
            return st.enter_context(nc.sbuf_tensor(uname(name), list(shape), dt))

        import os
        PE_ = os.environ.get('SSD_POOL', 'pool')

        def bc3(ap, shape):
            return ap.unsqueeze(2).to_broadcast(shape)

        cw = sb("cw", [128, 12, 4]); b_cw = Buf()
        cbv = sb("cbv", [128, 12]); b_cbv = Buf()
        a_b = sb("a_b", [128, 16]); b_ab = Buf()
        dsk = sb("dsk", [128, 16]); b_dsk = Buf()
        sng = sb("sng", [128, D]); b_sng = Buf()
        triu = sb("triu", [128, 128]); b_triu = Buf()
        sut = sb("sut", [128, 128]); b_sut = Buf()
        b_idf = Buf()
        DTall = sb("DTall", [128, NT, 16]); b_DT = Buf()
        H = sb("H", [128, 16, 64]); b_H = Buf()
        Hb = sb("Hb", [128, 16, 64], BF16); b_Hb = Buf()
        Ht = sb("Ht", [128, 16, 64]); b_Ht = Buf()
        S.dma("sp", cw[:], convw_d[l], writes=[b_cw])
        S.dma("sp", cbv[:], convb_d[l], writes=[b_cbv])
        S.dma("sp", a_b[:], alog_d[l].partition_broadcast(128), writes=[b_ab])
        S.dma("sp", dsk[:], dskip_d[l].partition_broadcast(128), writes=[b_dsk])
        S.dma("sp", sng[:], sng_d[l].partition_broadcast(128), writes=[b_sng])
        S.dma("sp", triu[:], triu_d, writes=[b_triu])
        S.dma("sp", sut[:], sut_d, writes=[b_sut])
        S.dma("sp", DTall[:], DTs.rearrange("(c p) h -> p c h", p=128), writes=[b_DT])
        S.op("act", lambda e: e.activation(out=a_b[:], in_=a_b[:], func=AF.Exp), reads=[b_ab], writes=[b_ab])
        S.op("dve", lambda e: e.tensor_scalar(out=a_b[:], in0=a_b[:], scalar1=-1.0, scalar2=None, op0=ALU.mult), reads=[b_ab], writes=[b_ab])
        S.op("dve", lambda e: e.memset(H[:], 0.0), writes=[b_H])
        S.op("dve", lambda e: e.memset(Hb[:], 0.0), writes=[b_Hb])

        pp = PsumPool(nc, st, [f"ps{i}" for i in range(6)])
        tpB = st.enter_context(nc.psum_tensor(uname("tpB"), [128, 1024], BF16)); b_tpB = Buf(excl=True)
        tpY = st.enter_context(nc.psum_tensor(uname("tpY"), [128, 1024], BF16)); b_tpY = Buf(excl=True)

        xin = sb("xin", [128, 12, 515]); b_xin = Buf()
        cv = sb("cv", [128, 12, 512]); b_cv = Buf()
        cvx = sb("cvx", [128, 8, 512]); b_cvx = Buf()
        cvbc2 = [sb("cvbc", [128, 4, 512], BF16) for _ in range(2)]; b_cvbc2 = [Buf(), Buf()]
        xs_tok2 = [sb("xs_tok", [128, 16, 64]) for _ in range(2)]; b_xst2 = [Buf(), Buf()]
        xd2 = [sb("xd", [128, 16, 64], BF16) for _ in range(2)]; b_xd2 = [Buf(), Buf()]
        xds2 = [sb("xds", [128, 16, 64], BF16) for _ in range(2)]; b_xds2 = [Buf(), Buf()]
        Btok2 = [sb("Btok", [128, 256], BF16) for _ in range(2)]; b_Btok2 = [Buf(), Buf()]
        dd2 = [sb("dd", [128, 8, 16]) for _ in range(2)]; b_dd2 = [Buf(), Buf()]
        Mm2 = [sb("Mm", [128, 16, 128], BF16) for _ in range(2)]; b_Mm2 = [Buf(), Buf()]
        zs2 = [sb("zs", [128, D], BF16) for _ in range(2)]; b_zs2 = [Buf(), Buf()]
        csb = sb("csb", [128, 32]); b_csb = Buf()
        LH = sb("LH", [128, 2, 16, 128], BF16); b_LH = Buf()
        hl = sb("hl", [128, 2, 16], BF16); b_hl = Buf()
        hlf = sb("hlf", [128, 2, 16]); b_hlf = Buf()
        triub = sb("triub", [128, 128], BF16); b_triub = Buf()
        sutb = sb("sutb", [128, 128], BF16); b_sutb = Buf()
        S.op("dve", lambda e: e.tensor_copy(out=triub[:], in_=triu[:]), reads=[b_triu], writes=[b_triub])
        S.op("dve", lambda e: e.tensor_copy(out=sutb[:], in_=sut[:]), reads=[b_sut], writes=[b_sutb])
        dec = sb("dec", [128, 16, 128], BF16); b_dec = Buf()
        mcb = sb("mcb", [128, 2, 128], BF16); b_mcb = Buf()
        yo = sb("yo", [128, 16, 64]); b_yo = Buf()
        y1 = sb("y1", [128, 16, 64]); b_y1 = Buf()
        y2 = sb("y2", [128, 16, 64]); b_y2 = Buf()
        nst = sb("nst", [128, 8]); b_nst = Buf()
        junk = sb("junk3", [128, 512], BF16); b_junk = Buf()
        yn = sb("yn", [128, D], BF16); b_yn = Buf()
        ynT = sb("ynT", [128, 8, 128], BF16); b_ynT = Buf()

        XBCv = XBC.rearrange("(ct p) t -> p ct t", p=128)
        YNv = YN.rearrange("(ct p) t -> p ct t", p=128)
        nsc = 8 if not (dbg and "nsc" in dbg) else dbg["nsc"]

        xin2 = [xin, sb("xin_b", [128, 12, 515])]; b_xin2 = [b_xin, Buf()]
        b_cvc = [Buf() for _ in range(12)]

        def conv_load(sc):
            t0 = sc * 512
            xi, bx = xin2[sc % 2], b_xin2[sc % 2]
            if sc == 0:
                S.op("pool", lambda e: e.memset(xi[:, :, 0:3], 0.0), writes=[bx])
                S.dma("sp", xi[:, :, 3:515], XBCv[:, :, 0:512], writes=[bx])
            else:
                S.dma("sp", xi[:], XBCv[:, :, t0 - 3:t0 + 512], writes=[bx])

        def conv_units(sc):
            xi, bx = xin2[sc % 2], b_xin2[sc % 2]
            cvbc, b_cvbc = cvbc2[sc % 2], b_cvbc2[sc % 2]

            def mk(cts, last):
                def u():
                    for ct in cts:
                        S.op(PE_, lambda e, ct=ct: e.tensor_scalar(out=cv[:, ct, :], in0=xi[:, ct, 3:515], scalar1=cw[:, ct, 3:4], scalar2=cbv[:, ct:ct + 1],
                                                                   op0=ALU.mult, op1=ALU.add), reads=[bx, b_cw, b_cbv], writes=[b_cvc[ct]])
                        for kk in range(3):
                            S.op("dve", lambda e, ct=ct, kk=kk: e.scalar_tensor_tensor(out=cv[:, ct, :], in0=xi[:, ct, kk:kk + 512], scalar=cw[:, ct, kk:kk + 1],
                                                                                        in1=cv[:, ct, :], op0=ALU.mult, op1=ALU.add),
                                 reads=[bx, b_cw, b_cvc[ct]], writes=[b_cvc[ct]])
                    if last:
                        S.op("act", lambda e: e.activation(out=cvx[:], in_=cv[:, 0:8, :], func=AF.Silu), reads=b_cvc[0:8], writes=[b_cvx])
                        S.op("act", lambda e: e.activation(out=cvbc[:], in_=cv[:, 8:12, :], func=AF.Silu), reads=b_cvc[8:12], writes=[b_cvbc])
                return u
            return [mk([0, 1, 2], False), mk([3, 4, 5], False), mk([6, 7, 8], False), mk([9, 10, 11], True)]

        def units_A(c):
            sc, cc = c // 4, c % 4
            k = c % 2
            cvbc, b_cvbc = cvbc2[sc % 2], b_cvbc2[sc % 2]
            xs_tok, b_xst, xd, b_xd, xds, b_xds = xs_tok2[k], b_xst2[k], xd2[k], b_xd2[k], xds2[k], b_xds2[k]
            Btok, b_Btok, dd, b_dd, Mm, b_Mm, zs, b_zs = Btok2[k], b_Btok2[k], dd2[k], b_dd2[k], Mm2[k], b_Mm2[k], zs2[k], b_zs2[k]
            cols = slice(cc * 128, (cc + 1) * 128)
            tok = slice(c * 128, (c + 1) * 128)
            dt = DTall[:, c, :]

            def a0():
                S.dma("sp", zs[:], ZS[tok, :], writes=[b_zs])
                tA = [pp.get(), pp.get()]
                for ct in range(8):
                    ps, bp = tA[ct // 4]
                    S.op("pe", lambda e, ps=ps, ct=ct: e.transpose(out=ps[:, (ct % 4) * 128:(ct % 4 + 1) * 128], in_=cvx[:, ct, cols], identity=identf[:]),
                         reads=[b_cvx, b_idf], writes=[bp])
                for g in range(2):
                    S.op("pe", lambda e, g=g: e.transpose(out=tpB[:, g * 128:(g + 1) * 128], in_=cvbc[:, g, cols], identity=ident[:]),
                         reads=[b_cvbc], writes=[b_tpB])
                S.op("act", lambda e: e.activation(out=Btok[:], in_=tpB[:, 0:256], func=AF.Copy), reads=[b_tpB], writes=[b_Btok])
                for hb in range(2):
                    ps, bp = tA[hb]
                    hs = slice(hb * 8, (hb + 1) * 8)
                    S.op("dve", lambda e, ps=ps, hs=hs: e.tensor_tensor(out=xd[:, hs, :], in0=ps[:].rearrange("p (h d) -> p h d", h=8),
                                                                        in1=bc3(dt[:, hs], [128, 8, 64]), op=ALU.mult),
                         reads=[bp, b_DT], writes=[b_xd])
                    S.op("act", lambda e, ps=ps, hs=hs: e.activation(out=xs_tok[:, hs, :], in_=ps[:].rearrange("p (h d) -> p h d", h=8), func=AF.Copy),
                         reads=[bp], writes=[b_xst])

            def a1():
                S.op("dve", lambda e: e.tensor_tensor(out=dd[:, 0, :], in0=dt, in1=a_b[:], op=ALU.mult), reads=[b_DT, b_ab], writes=[b_dd])
                psc, bpc = pp.get()
                S.op("dve", lambda e: e.tensor_copy(out=hl[:, 0, :], in_=dd[:, 0, :]), reads=[b_dd], writes=[b_hl])
                S.op("dve", lambda e: e.tensor_tensor(out=hl[:, 1, :], in0=dd[:, 0, :], in1=hl[:, 0, :], op=ALU.subtract), reads=[b_dd, b_hl], writes=[b_hl])
                S.op("dve", lambda e: e.tensor_copy(out=hlf[:], in_=hl[:]), reads=[b_hl], writes=[b_hlf])
                for pi in range(2):
                    S.op("pe", lambda e, pi=pi: e.matmul(psc[:, 0:16], lhsT=triub[:], rhs=hl[:, pi, :], start=(pi == 0), stop=(pi == 1)),
                         reads=[b_triub, b_hl], writes=[bpc])
                for pi in range(2):
                    S.op("pe", lambda e, pi=pi: e.matmul(psc[:, 16:32], lhsT=ones_b[:], rhs=hl[:, pi, :], start=(pi == 0), stop=(pi == 1)),
                         reads=[b_hl], writes=[bpc])
                S.op("act", lambda e: e.activation(out=csb[:], in_=psc[:, 0:32], func=AF.Copy), reads=[bpc], writes=[b_csb])
                S.op("act", lambda e: e.activation(out=dd[:, 1:3, :].rearrange("p a h -> p (a h)"), in_=csb[:], func=AF.Exp), reads=[b_csb], writes=[b_dd])
                S.op("dve", lambda e: e.tensor_tensor(out=dd[:, 3, :], in0=csb[:, 16:32], in1=csb[:, 0:16], op=ALU.subtract), reads=[b_csb], writes=[b_dd])
                S.op("act", lambda e: e.activation(out=dd[:, 4, :], in_=dd[:, 3, :], func=AF.Exp), reads=[b_dd], writes=[b_dd])

            def a2():
                S.op("dve", lambda e: e.tensor_tensor(out=xds[:], in0=xd[:], in1=bc3(dd[:, 4, :], [128, 16, 64]), op=ALU.mult),
                     reads=[b_xd, b_dd], writes=[b_xds])
                pcb, bpcb = pp.get()
                for g in range(2):
                    S.op("pe", lambda e, g=g: e.matmul(pcb[:, g * 128:(g + 1) * 128], lhsT=cvbc[:, g, cols], rhs=cvbc[:, 2 + g, cols], start=True, stop=True),
                         reads=[b_cvbc], writes=[bpcb])
                S.op("dve", lambda e: e.tensor_tensor(out=mcb[:], in0=pcb[:, 0:256].rearrange("p (g l) -> p g l", g=2),
                                                      in1=triu[:].unsqueeze(1).to_broadcast([128, 2, 128]), op=ALU.mult),
                     reads=[bpcb, b_triu], writes=[b_mcb])

            def a3():
                for pi in range(2):
                    S.op("dve", lambda e, pi=pi: e.tensor_tensor(out=LH[:, pi, :, :], in0=sutb[:].unsqueeze(1).to_broadcast([128, 16, 128]),
                                                                 in1=hlf[:, pi, :].unsqueeze(2).to_broadcast([128, 16, 128]), op=ALU.mult),
                         reads=[b_sutb, b_hlf], writes=[b_LH])

            def mk_seg(q4s):
                def u():
                    for q4 in q4s:
                        ps, bp = pp.get()
                        for hh in range(4):
                            h = q4 * 4 + hh
                            for pi in range(2):
                                S.op("pe", lambda e, ps=ps, h=h, hh=hh, pi=pi: e.matmul(ps[:, hh * 128:(hh + 1) * 128], lhsT=LH[:, pi, h, :], rhs=triub[:],
                                                                                       start=(pi == 0), stop=(pi == 1)),
                                     reads=[b_LH, b_triub], writes=[bp])
                        S.op("act", lambda e, ps=ps, q4=q4: e.activation(out=dec[:, q4 * 4:(q4 + 1) * 4, :].rearrange("p h l -> p (h l)"), in_=ps[:], func=AF.Exp),
                             reads=[bp], writes=[b_dec])
                return u

            def a6():
                for g in range(2):
                    S.op("dve", lambda e, g=g: e.tensor_tensor(out=Mm[:, g * 8:(g + 1) * 8, :], in0=dec[:, g * 8:(g + 1) * 8, :],
                                                               in1=mcb[:, g, :].unsqueeze(1).to_broadcast([128, 8, 128]), op=ALU.mult),
                         reads=[b_dec, b_mcb], writes=[b_Mm])

            return [a0, a1, a2, a3, mk_seg([0, 1]), mk_seg([2, 3]), a6]

        def units_B(c):
            sc, cc = c // 4, c % 4
            k = c % 2
            cvbc, b_cvbc = cvbc2[sc % 2], b_cvbc2[sc % 2]
            xs_tok, b_xst, xd, b_xd, xds, b_xds = xs_tok2[k], b_xst2[k], xd2[k], b_xd2[k], xds2[k], b_xds2[k]
            Btok, b_Btok, dd, b_dd, Mm, b_Mm, zs, b_zs = Btok2[k], b_Btok2[k], dd2[k], b_dd2[k], Mm2[k], b_Mm2[k], zs2[k], b_zs2[k]
            cols = slice(cc * 128, (cc + 1) * 128)
            tok = slice(c * 128, (c + 1) * 128)

            def b0():
                for g in range(2):
                    ps, bp = pp.get()
                    S.op("pe", lambda e, ps=ps, g=g: e.matmul(ps[:], lhsT=cvbc[:, 2 + g, cols], rhs=Hb[:, g * 8:(g + 1) * 8, :], start=True, stop=True),
                         reads=[b_cvbc, b_Hb], writes=[bp])
                    S.op("dve", lambda e, ps=ps, g=g: e.tensor_tensor(out=yo[:, g * 8:(g + 1) * 8, :], in0=ps[:].rearrange("p (h d) -> p h d", h=8),
                                                                      in1=bc3(dd[:, 1, g * 8:(g + 1) * 8], [128, 8, 64]), op=ALU.mult),
                         reads=[bp, b_dd], writes=[b_yo])

            def b1():
                sts = [pp.get(), pp.get()]
                for g in range(2):
                    ps, bp = sts[g]
                    S.op("pe", lambda e, ps=ps, g=g: e.matmul(ps[:], lhsT=Btok[:, g * 128:(g + 1) * 128], rhs=xds[:, g * 8:(g + 1) * 8, :], start=True, stop=True),
                         reads=[b_Btok, b_xds], writes=[bp])
                S.op("dve", lambda e: e.tensor_tensor(out=Ht[:], in0=H[:], in1=bc3(dd[:, 2, :], [128, 16, 64]), op=ALU.mult),
                     reads=[b_H, b_dd], writes=[b_Ht])
                for g in range(2):
                    ps, bp = sts[g]
                    S.op("dve", lambda e, ps=ps, g=g: e.tensor_tensor(out=H[:, g * 8:(g + 1) * 8, :], in0=ps[:].rearrange("p (h d) -> p h d", h=8),
                                                                      in1=Ht[:, g * 8:(g + 1) * 8, :], op=ALU.add),
                         reads=[bp, b_Ht], writes=[b_H])
                S.op("act", lambda e: e.activation(out=Hb[:], in_=H[:], func=AF.Copy), reads=[b_H], writes=[b_Hb])

            def b2():
                for hb in range(2):
                    ps, bp = pp.get()
                    for hh in range(8):
                        h = hb * 8 + hh
                        S.op("pe", lambda e, ps=ps, h=h, hh=hh: e.matmul(ps[:, hh * 64:(hh + 1) * 64], lhsT=Mm[:, h, :], rhs=xd[:, h, :], start=True, stop=True),
                             reads=[b_Mm, b_xd], writes=[bp])
                    hs = slice(hb * 8, (hb + 1) * 8)
                    S.op("dve", lambda e, ps=ps, hs=hs: e.tensor_tensor(out=y1[:, hs, :], in0=ps[:].rearrange("p (h d) -> p h d", h=8), in1=yo[:, hs, :], op=ALU.add),
                         reads=[bp, b_yo], writes=[b_y1])

            def b3():
                S.op(PE_, lambda e: e.tensor_tensor(out=y2[:], in0=xs_tok[:], in1=bc3(dsk[:], [128, 16, 64]), op=ALU.mult),
                     reads=[b_xst, b_dsk], writes=[b_y2])
                S.op(PE_, lambda e: e.tensor_tensor(out=y2[:], in0=y2[:], in1=y1[:], op=ALU.add), reads=[b_y2, b_y1], writes=[b_y2])
                S.op(PE_, lambda e: e.tensor_tensor(out=y2[:].rearrange("p h d -> p (h d)"), in0=y2[:].rearrange("p h d -> p (h d)"), in1=zs[:], op=ALU.mult),
                     reads=[b_y2, b_zs], writes=[b_y2])

            def b4():
                for g in range(2):
                    S.op("act", lambda e, g=g: e.activation(out=junk[:], in_=y2[:, g * 8:(g + 1) * 8, :].rearrange("p h d -> p (h d)"), func=AF.Square,
                                                            accum_out=nst[:, g:g + 1]), reads=[b_y2], writes=[b_junk, b_nst])
                S.op("dve", lambda e: e.tensor_scalar(out=nst[:, 2:4], in0=nst[:, 0:2], scalar1=1.0 / 512, scalar2=EPS, op0=ALU.mult, op1=ALU.add),
                     reads=[b_nst], writes=[b_nst])
                S.op("act", lambda e: e.activation(out=nst[:, 4:6], in_=nst[:, 2:4], func=AF.Ln), reads=[b_nst], writes=[b_nst])
                S.op("act", lambda e: e.activation(out=nst[:, 6:8], in_=nst[:, 4:6], func=AF.Exp, scale=-0.5), reads=[b_nst], writes=[b_nst])

            def b5():
                for g in range(2):
                    S.op("dve", lambda e, g=g: e.scalar_tensor_tensor(out=yn[:, g * 512:(g + 1) * 512], in0=y2[:, g * 8:(g + 1) * 8, :].rearrange("p h d -> p (h d)"),
                                                                      scalar=nst[:, 6 + g:7 + g], in1=sng[:, g * 512:(g + 1) * 512], op0=ALU.mult, op1=ALU.mult),
                         reads=[b_y2, b_nst, b_sng], writes=[b_yn])

            def b6():
                for ct in range(8):
                    S.op("pe", lambda e, ct=ct: e.transpose(out=tpY[:, ct * 128:(ct + 1) * 128], in_=yn[:, ct * 128:(ct + 1) * 128], identity=ident[:]),
                         reads=[b_yn], writes=[b_tpY])
                S.op("act", lambda e: e.activation(out=ynT[:].rearrange("p c t -> p (c t)"), in_=tpY[:], func=AF.Copy), reads=[b_tpY], writes=[b_ynT])
                S.dma("sp", YNv[:, :, tok], ynT[:], reads=[b_ynT])

            return [b0, b1, b2, b3, b4, b5, b6]

        def zip_emit(*streams):
            for i in range(max(len(u) for u in streams)):
                for u in streams:
                    if i < len(u):
                        u[i]()

        nch = nsc * 4
        conv_load(0)
        for u in conv_units(0):
            u()
        if nsc > 1:
            conv_load(1)
        zip_emit(units_A(0))
        for c in range(nch):
            sc, cc = c // 4, c % 4
            ua = units_A(c + 1) if c + 1 < nch else []
            uc = []
            if sc + 1 < nsc:
                cu = conv_units(sc + 1)
                uc = [lambda: None] * 2 + [cu[cc]]
                if cc == 3 and sc + 2 < nsc:
                    uc.append(lambda sc=sc: conv_load(sc + 2))
            if cc == 3 and sc + 1 < nsc:
                zip_emit(uc, units_B(c))
                zip_emit(ua)
            else:
                zip_emit(ua, units_B(c), uc)
        S.emit()


def phase_out(nc, S, l, x_src, X1, OG, YN, GLA, GLS, wa_d, ws_d, wo_d, GATE_b, nxt=None):
    with ExitStack() as st:
        if nxt is not None:
            nxt(st)
        def sb(name, shape, dt=F32):
            return st.enter_context(nc.sbuf_tensor(uname(name), list(shape), dt))
        wa = sb("wa", [64, 8, D], BF16); b_wa = [Buf() for _ in range(8)]
        ws = sb("ws", [128, 8, D], BF16); b_ws = [Buf() for _ in range(8)]
        wo = sb("wo", [128, 8, D], BF16); b_wo = [Buf() for _ in range(2)]
        wav = wa_d[l].rearrange("(h d) n -> d h n", d=64)
        wsv = ws_d[l].rearrange("(kc p) n -> p kc n", p=128)
        wov = wo_d[l].rearrange("(kc p) n -> p kc n", p=128)
        for nchunk in range(8):
            ns = slice(nchunk * 128, (nchunk + 1) * 128)
            S.dma("pool", wa[:, :, ns], wav[:, :, ns], writes=[b_wa[nchunk]])
            S.dma("pool", ws[:, :, ns], wsv[:, :, ns], writes=[b_ws[nchunk]])
        for nb in range(2):
            S.dma("pool", wo[:, :, nb * 512:(nb + 1) * 512], wov[:, :, nb * 512:(nb + 1) * 512], writes=[b_wo[nb]])
        pp = PsumPool(nc, st, [f"po{i}" for i in range(8 if nxt is None else 6)])
        ogt2 = [sb("ogt", [64, 8, 512], BF16) for _ in range(2)]; b_ogt2 = [Buf(), Buf()]
        ynt2 = [sb("ynt", [128, 8, 512], BF16) for _ in range(2)]; b_ynt2 = [Buf(), Buf()]
        gla = sb("gla", [128, 8, 512], BF16); b_gla = Buf()
        gls = sb("gls", [128, 8, 512], BF16); b_gls = Buf()
        ta = [sb("ta", [128, 512]) for _ in range(2)]; b_ta = [Buf(), Buf()]
        tb = [sb("tb", [128, 512]) for _ in range(2)]; b_tb = [Buf(), Buf()]
        mg = sb("mg", [128, 8, 512], BF16); b_mg = [Buf() for _ in range(8)]
        xt = [sb("xo", [128, D]) for _ in range(2)]; b_xt = [Buf(), Buf()]
        t3 = [sb("t3", [128, D]) for _ in range(2)]; b_t3 = [Buf(), Buf()]
        OGv = OG.rearrange("h d t -> d h t")
        YNv = YN.rearrange("(ct p) t -> p ct t", p=128)
        GLAv = GLA.rearrange("(ct p) t -> p ct t", p=128)
        GLSv = GLS.rearrange("(ct p) t -> p ct t", p=128)
        def load_blk(c):
            tok = slice(c * 512, (c + 1) * 512)
            j = c % 2
            S.dma("sp", ogt2[j][:], OGv[:, :, tok], writes=[b_ogt2[j]])
            S.dma("sp", ynt2[j][:], YNv[:, :, tok], writes=[b_ynt2[j]])

        def load_gates(c):
            tok = slice(c * 512, (c + 1) * 512)
            S.dma("sp", gla[:], GLAv[:, :, tok], writes=[b_gla])
            S.dma("sp", gls[:], GLSv[:, :, tok], writes=[b_gls])

        load_blk(0)
        load_gates(0)
        for c in range(8):
            tok = slice(c * 512, (c + 1) * 512)
            if c + 1 < 8:
                load_blk(c + 1)
            ogt, b_ogt, ynt, b_ynt = ogt2[c % 2], b_ogt2[c % 2], ynt2[c % 2], b_ynt2[c % 2]
            for nchunk in range(8):
                ns = slice(nchunk * 128, (nchunk + 1) * 128)
                a = nchunk % 2
                psa, bpa = pp.get()
                for h in range(8):
                    S.op("pe", lambda e, psa=psa, h=h, ns=ns, ogt=ogt: e.matmul(psa[:], lhsT=wa[:, h, ns], rhs=ogt[:, h, :], start=(h == 0), stop=(h == 7)),
                         reads=[b_wa[nchunk], b_ogt], writes=[bpa])
                pss, bps = pp.get()
                for kc in range(8):
                    S.op("pe", lambda e, pss=pss, kc=kc, ns=ns, ynt=ynt: e.matmul(pss[:], lhsT=ws[:, kc, ns], rhs=ynt[:, kc, :], start=(kc == 0), stop=(kc == 7)),
                         reads=[b_ws[nchunk], b_ynt], writes=[bps])
                S.op("dve", lambda e, psa=psa, a=a, nchunk=nchunk, gla=gla: e.tensor_tensor(out=ta[a][:], in0=psa[:], in1=gla[:, nchunk, :], op=ALU.mult),
                     reads=[bpa, b_gla], writes=[b_ta[a]])
                S.op("dve", lambda e, pss=pss, a=a, nchunk=nchunk, gls=gls: e.tensor_tensor(out=tb[a][:], in0=pss[:], in1=gls[:, nchunk, :], op=ALU.mult),
                     reads=[bps, b_gls], writes=[b_tb[a]])
                S.op("pool", lambda e, a=a, nchunk=nchunk: e.tensor_tensor(out=mg[:, nchunk, :], in0=ta[a][:], in1=tb[a][:], op=ALU.add),
                     reads=[b_ta[a], b_tb[a]], writes=[b_mg[nchunk]])
            if c + 1 < 8:
                load_gates(c + 1)
            for tq in range(4):
                k = tq % 2
                t128 = slice(c * 512 + tq * 128, c * 512 + (tq + 1) * 128)
                S.dma("sp", xt[k][:], x_src[t128, :], writes=[b_xt[k]])
                for nb in range(2):
                    ps, bp = pp.get()
                    for kc in range(8):
                        S.op("pe", lambda e, ps=ps, kc=kc, tq=tq, nb=nb: e.matmul(ps[:], lhsT=mg[:, kc, tq * 128:(tq + 1) * 128], rhs=wo[:, kc, nb * 512:(nb + 1) * 512],
                                                                                 start=(kc == 0), stop=(kc == 7)),
                             reads=b_mg + [b_wo[nb]], writes=[bp])
                    S.op("dve", lambda e, ps=ps, k=k, nb=nb: e.tensor_tensor(out=t3[k][:, nb * 512:(nb + 1) * 512], in0=ps[:], in1=GATE_b[:, nb * 512:(nb + 1) * 512], op=ALU.mult),
                         reads=[bp], writes=[b_t3[k]])
                S.op("pool", lambda e, k=k: e.tensor_tensor(out=t3[k][:], in0=t3[k][:], in1=xt[k][:], op=ALU.add), reads=[b_t3[k], b_xt[k]], writes=[b_t3[k]])
                S.dma("sp", X1[t128, :], t3[k][:], reads=[b_t3[k]])
        S.emit()


def phase_final(nc, S, x_src, fg_d, out_d):
    with ExitStack() as st:
        def sb(name, shape, dt=F32):
            return st.enter_context(nc.sbuf_tensor(uname(name), list(shape), dt))
        fg = sb("fg", [128, D]); b_fg = Buf()
        S.dma("sp", fg[:], fg_d.partition_broadcast(128), writes=[b_fg])
        NF = 4
        xt = [sb("xf", [128, D]) for _ in range(NF)]; b_xt = [Buf() for _ in range(NF)]
        junk = sb("junkf", [128, D], BF16); b_junk = Buf()
        ss = [sb("ssf", [128, 4]) for _ in range(NF)]; b_ss = [Buf() for _ in range(NF)]
        yo = [sb("yof", [128, D]) for _ in range(NF)]; b_yo = [Buf() for _ in range(NF)]
        for tt in range(NT):
            k = tt % NF
            tok = slice(tt * 128, (tt + 1) * 128)
            S.dma("sp", xt[k][:], x_src[tok, :], writes=[b_xt[k]])
            S.op("act", lambda e, k=k: e.activation(out=junk[:], in_=xt[k][:], func=AF.Square, accum_out=ss[k][:, 0:1]),
                 reads=[b_xt[k]], writes=[b_junk, b_ss[k]])
            S.op("dve", lambda e, k=k: e.tensor_scalar(out=ss[k][:, 1:2], in0=ss[k][:, 0:1], scalar1=1.0 / D, scalar2=EPS, op0=ALU.mult, op1=ALU.add),
                 reads=[b_ss[k]], writes=[b_ss[k]])
            S.op("act", lambda e, k=k: e.activation(out=ss[k][:, 2:3], in_=ss[k][:, 1:2], func=AF.Ln), reads=[b_ss[k]], writes=[b_ss[k]])
            S.op("act", lambda e, k=k: e.activation(out=ss[k][:, 3:4], in_=ss[k][:, 2:3], func=AF.Exp, scale=-0.5), reads=[b_ss[k]], writes=[b_ss[k]])
            S.op("dve", lambda e, k=k: e.scalar_tensor_tensor(out=yo[k][:], in0=xt[k][:], scalar=ss[k][:, 3:4], in1=fg[:], op0=ALU.mult, op1=ALU.mult),
                 reads=[b_xt[k], b_ss[k], b_fg], writes=[b_yo[k]])
            S.dma("sp", out_d[tok, :], yo[k][:], reads=[b_yo[k]])
        S.emit()


def host_consts():
    inv = (10000.0 ** (-np.arange(0, 64, 2, dtype=np.float32) / 64.0)).astype(np.float32)
    ang = np.arange(L, dtype=np.float32)[:, None] * inv[None, :]
    ang = np.concatenate([ang, ang], axis=-1)
    cos = np.cos(ang).astype(np.float32).T
    sin = np.sin(ang).astype(np.float32).T
    sgn = np.where(np.arange(64) < 32, -1.0, 1.0).astype(np.float32)[:, None]
    cosT = np.concatenate([cos, cos], axis=0)
    sinT = np.concatenate([sin * sgn, sin * sgn], axis=0)
    r = np.arange(128)
    ident = np.eye(128, dtype=np.float32)
    cbias = np.where(r[None, :] <= r[:, None], 0.0, -1e30).astype(np.float32)
    triu = (r[:, None] <= r[None, :]).astype(np.float32)
    sut = (r[:, None] > r[None, :]).astype(np.float32)
    pow2 = np.tile(((65.0 / 64.0) * 2.0 ** (-np.arange(NBIS + 2, dtype=np.float64))).astype(np.float32)[None, :], (128, 1))
    return dict(cosT=np.ascontiguousarray(cosT), sinT=np.ascontiguousarray(sinT), ident=ident, cbias=cbias,
                triu=triu, sut=sut, pow2=pow2)


def make_in_maps(inputs, n_cores=8):
    f = lambda a: np.ascontiguousarray(np.asarray(a, dtype=np.float32))
    shared = dict(
        w_ada=f(inputs["w_ada"]), b_ada=f(inputs["b_ada"]).reshape(DEPTH, 1, 3 * D),
        norm_g=f(inputs["norm_g"]).reshape(DEPTH, 1, D), w_in=f(inputs["w_in"]),
        conv_wT=f(np.asarray(inputs["conv_w"]).reshape(DEPTH, 4, 12, 128).transpose(0, 3, 2, 1)),
        conv_bT=f(np.asarray(inputs["conv_b"]).reshape(DEPTH, 12, 128).transpose(0, 2, 1)),
        dt_bias=f(inputs["dt_bias"]).reshape(DEPTH, 1, 16), a_log=f(inputs["a_log"]).reshape(DEPTH, 1, 16),
        d_skip=f(inputs["d_skip"]).reshape(DEPTH, 1, 16), ssm_norm_g=f(inputs["ssm_norm_g"]).reshape(DEPTH, 1, D),
        w_branch_a=f(inputs["w_branch_a"]), w_branch_s=f(inputs["w_branch_s"]), w_out=f(inputs["w_out"]),
        final_g=f(inputs["final_g"]).reshape(1, D),
    )
    shared.update(host_consts())
    maps = []
    for b in range(n_cores):
        m = dict(shared)
        m["x"] = f(inputs["x"][b])
        m["cT"] = f(np.asarray(inputs["c"][b]).reshape(8, 128).T)
        maps.append(m)
    return maps


def kernel(**inputs):
    nc = build_program()
    maps = make_in_maps(inputs)
    res = run_bass_kernel_spmd(nc, maps, core_ids=list(range(8)))
    return np.stack([np.asarray(r["out"], dtype=np.float32) for r in res.results], axis=0)
```

```python
import numpy as np
from contextlib import ExitStack
import concourse.bass as bass
import concourse.mybir as mybir
from concourse.bass_utils import run_bass_kernel_spmd

F32 = mybir.dt.float32
BF16 = mybir.dt.bfloat16
AF = mybir.ActivationFunctionType
ALU = mybir.AluOpType
AX = mybir.AxisListType

L = 4096
D = 1024
NT = L // 128
DEPTH = 4
NIN = 6100
Q0, K0, V0, GA0, QI0, KI0, WI0, Z0, XBC0, DT0, GLA0, GLS0 = 0, 512, 576, 640, 1152, 1408, 1472, 1476, 2500, 4036, 4052, 5076
EPS = 1e-6
NBIS = 20
NDMASEM = 12


class Buf:
    __slots__ = ("name", "lw", "rd", "excl")

    def __init__(self, name="", excl=False):
        self.name = name
        self.lw = None
        self.rd = []
        self.excl = excl


class Sched:
    ENGS = ("sp", "act", "pe", "dve", "pool")
    DQ = ("sp", "act", "pool")

    def __init__(self, nc, stack):
        self.nc = nc
        self.batch = 0
        self.ops = {e: [] for e in self.ENGS}
        self.waited = {e: {} for e in self.ENGS}
        self.base = {e: 0 for e in self.ENGS}
        self.esem = {e: stack.enter_context(nc.semaphore("es_" + e)) for e in self.ENGS}
        self.dsem = {}
        self.dcnt = {}
        self.dval = {}
        for q in self.DQ:
            self.dsem[q] = [stack.enter_context(nc.semaphore(f"ds_{q}{i}")) for i in range(NDMASEM)]
            self.dcnt[q] = 0
            self.dval[q] = [0] * NDMASEM
        self.nops = 0

    def _need(self, eng, ev, waits, same_ok=False):
        if ev is None or ev[1] != self.batch:
            return
        if ev[0] == "e":
            _, _, e2, i2 = ev
            if e2 == eng and same_ok:
                return
            self.ops[e2][i2]["inc"] = True
            if self.waited[eng].get(e2, -1) >= i2:
                return
            self.waited[eng][e2] = i2
            waits.append(ev)
        else:
            _, _, q, si, val = ev
            key = (q, si)
            if self.waited[eng].get(key, -1) >= val:
                return
            self.waited[eng][key] = val
            waits.append(ev)

    def _deps(self, eng, reads, writes, waits):
        for r in reads:
            self._need(eng, r.lw, waits, same_ok=(eng == "pe"))
        for w in writes:
            self._need(eng, w.lw, waits, same_ok=(eng == "pe"))
            for ev in w.rd:
                self._need(eng, ev, waits, same_ok=(eng == "pe"))

    def op(self, eng, fn, reads=(), writes=()):
        ex = [r for r in reads if r.excl and r not in writes]
        if ex:
            writes = list(writes) + ex
        waits = []
        self._deps(eng, reads, writes, waits)
        idx = len(self.ops[eng])
        self.ops[eng].append({"fn": fn, "waits": waits, "inc": False, "dma": None})
        ev = ("e", self.batch, eng, idx)
        for r in reads:
            r.rd.append(ev)
        for w in writes:
            w.lw = ev
            w.rd = []
        self.nops += 1
        return ev

    def dma(self, q, out, in_, reads=(), writes=(), **kw):
        waits = []
        self._deps(q, reads, writes, waits)
        n = self.dcnt[q]
        si = n % NDMASEM
        self.dcnt[q] += 1
        prev = self.dval[q][si]
        if prev > 0:
            self._need(q, ("d", self.batch, q, si, prev), waits)
        val = prev + 16
        self.dval[q][si] = val
        ev = ("d", self.batch, q, si, val)
        self.ops[q].append({"fn": (lambda e: e.dma_start(out=out, in_=in_, **kw)), "waits": waits,
                            "inc": False, "dma": (self.dsem[q][si], 16)})
        for r in reads:
            r.rd.append(ev)
        for w in writes:
            w.lw = ev
            w.rd = []
        self.nops += 1
        return ev

    def emit(self, final=False):
        for e in self.ENGS:
            for o in reversed(self.ops[e]):
                if o["fn"] is not None and o["dma"] is None:
                    o["inc"] = True
                    break
            c = self.base[e]
            for o in self.ops[e]:
                if o["inc"]:
                    c += 1
                o["cnt"] = c
        newbase = {e: (self.ops[e][-1]["cnt"] if self.ops[e] else self.base[e]) for e in self.ENGS}
        dvals = {q: list(self.dval[q]) for q in self.DQ}

        def body(engine, ename):
            for o in self.ops[ename]:
                for ev in o["waits"]:
                    if ev[0] == "e":
                        engine.wait_ge(self.esem[ev[2]], self.ops[ev[2]][ev[3]]["cnt"])
                    else:
                        engine.wait_ge(self.dsem[ev[2]][ev[3]], ev[4])
                if o["fn"] is None:
                    continue
                ins = o["fn"](engine)
                if o["dma"] is not None:
                    ins.then_inc(o["dma"][0], o["dma"][1])
                elif o["inc"]:
                    ins.then_inc(self.esem[ename], 1)
            for e2 in self.ENGS:
                if e2 != ename and newbase[e2] > 0:
                    engine.wait_ge(self.esem[e2], newbase[e2])
            for q in self.DQ:
                for si in range(NDMASEM):
                    if dvals[q][si] > 0:
                        engine.wait_ge(self.dsem[q][si], dvals[q][si])

        with self.nc.Block() as block:
            @block.sync
            def _(eng):
                body(eng, "sp")

            @block.scalar
            def _(eng):
                body(eng, "act")

            @block.tensor
            def _(eng):
                body(eng, "pe")

            @block.vector
            def _(eng):
                body(eng, "dve")

            @block.gpsimd
            def _(eng):
                body(eng, "pool")

        self.base = newbase
        self.batch += 1
        self.ops = {e: [] for e in self.ENGS}
        self.waited = {e: {} for e in self.ENGS}


_UID = [0]


def uname(n):
    _UID[0] += 1
    return f"{n}_u{_UID[0]}"


class PsumPool:
    def __init__(self, nc, stack, names):
        self.banks = [stack.enter_context(nc.psum_tensor(uname(n), [128, 512], F32)) for n in names]
        self.bufs = [Buf(n, excl=True) for n in names]
        self.i = 0

    def get(self):
        k = self.i % len(self.banks)
        self.i += 1
        return self.banks[k], self.bufs[k]


def build_program(n_layers=DEPTH, dbg=None):
    nc = bass.Bass("TRN2", target_bir_lowering=False)

    def din(name, shape, dt=F32):
        return nc.dram_tensor(name, list(shape), dt, kind="ExternalInput").ap()

    def dscr(name, shape, dt):
        kind = "ExternalOutput" if (dbg and name in dbg) else "Internal"
        return nc.dram_tensor(name, list(shape), dt, kind=kind).ap()

    x_in = din("x", [L, D])
    cT_d = din("cT", [128, 8])
    w_ada_d = din("w_ada", [DEPTH, D, 3 * D])
    b_ada_d = din("b_ada", [DEPTH, 1, 3 * D])
    norm_g_d = din("norm_g", [DEPTH, 1, D])
    w_in_d = din("w_in", [DEPTH, D, NIN])
    convw_d = din("conv_wT", [DEPTH, 128, 12, 4])
    convb_d = din("conv_bT", [DEPTH, 128, 12])
    dtb_d = din("dt_bias", [DEPTH, 1, 16])
    alog_d = din("a_log", [DEPTH, 1, 16])
    dskip_d = din("d_skip", [DEPTH, 1, 16])
    sng_d = din("ssm_norm_g", [DEPTH, 1, D])
    wa_d = din("w_branch_a", [DEPTH, 512, D])
    ws_d = din("w_branch_s", [DEPTH, D, D])
    wo_d = din("w_out", [DEPTH, D, D])
    fg_d = din("final_g", [1, D])
    cosT_d = din("cosT", [128, L])
    sinT_d = din("sinT", [128, L])
    ident_d = din("ident", [128, 128])
    cbias_d = din("cbias", [128, 128])
    triu_d = din("triu", [128, 128])
    sut_d = din("sut", [128, 128])
    pow2_d = din("pow2", [128, NBIS + 2])
    out_d = nc.dram_tensor("out", [L, D], F32, kind="ExternalOutput").ap()

    X1 = dscr("X1", [L, D], F32)
    QT = dscr("QT", [8, 64, L], BF16)
    KT = dscr("KT", [64, L], BF16)
    QiT = dscr("QiT", [4, 64, L], BF16)
    KiT = dscr("KiT", [64, L], BF16)
    GA = dscr("GA", [8, 64, L], BF16)
    XBC = dscr("XBC", [1536, L], F32)
    GLA = dscr("GLA", [D, L], BF16)
    GLS = dscr("GLS", [D, L], BF16)
    VA = dscr("VA", [L, 65], BF16)
    WI = dscr("WI", [L, 4], F32)
    ZS = dscr("ZS", [L, D], BF16)
    DTs = dscr("DT", [L, 16], F32)
    OG = dscr("OG", [8, 64, L], BF16)
    YN = dscr("YN", [D, L], BF16)
    DBG = dscr("DBG", [128, 4096], F32)

    with ExitStack() as top:
        S = Sched(nc, top)

        def sbt(stack, name, shape, dt):
            return stack.enter_context(nc.sbuf_tensor(uname(name), list(shape), dt))

        ident = sbt(top, "ident", [128, 128], BF16); b_ident = Buf()
        identf = sbt(top, "identf", [128, 128], F32); b_identf = Buf()
        ones_f = sbt(top, "ones_f", [128, 128], F32); b_ones = Buf()
        ones_b = sbt(top, "ones_b", [128, 128], BF16); b_onesb = Buf()
        S.dma("pool", ident[:], ident_d, writes=[b_ident])
        S.dma("sp", identf[:], ident_d, writes=[b_identf])
        S.op("dve", lambda e: e.memset(ones_f[:], 1.0), writes=[b_ones])
        S.op("dve", lambda e: e.memset(ones_b[:], 1.0), writes=[b_onesb])
        S.emit()

        mods = [(sbt(top, "G_b", [128, D], F32), sbt(top, "SH_b", [128, D], F32), sbt(top, "GATE_b", [128, D], F32)) for _ in range(2)]
        for l in range(n_layers):
            x_src = x_in if l == 0 else X1
            with ExitStack() as lay:
                G_b, SH_b, GATE_b = mods[l % 2]
                if l == 0 or (dbg and "no_hoist" in dbg):
                    phase_adaln(nc, S, lay, l, cT_d, w_ada_d, b_ada_d, norm_g_d, G_b, SH_b, GATE_b, ones_f)
                if dbg and "stop_adaln" in dbg:
                    dump(nc, S, DBG, [G_b, SH_b, GATE_b])
                    break
                phase_proj(nc, S, l, x_src, w_in_d, dtb_d, cosT_d, sinT_d, G_b, SH_b, ident,
                           dict(QT=QT, KT=KT, QiT=QiT, KiT=KiT, GA=GA, XBC=XBC, GLA=GLA, GLS=GLS, VA=VA, WI=WI,
                                ZS=ZS, DT=DTs))
                if dbg and "stop_proj" in dbg:
                    break
                if not (dbg and "skip_attn" in dbg):
                    phase_attn(nc, S, l, QT, KT, QiT, KiT, GA, VA, WI, OG, cbias_d, pow2_d, ident_d, ones_f, ident, DBG, dbg)
                if dbg and "stop_attn" in dbg:
                    break
                phase_ssd(nc, S, l, XBC, ZS, DTs, YN, convw_d, convb_d, alog_d, dskip_d, sng_d, triu_d, sut_d,
                          ident, identf, ones_f, ones_b, DBG, dbg)
                if dbg and "stop_ssd" in dbg:
                    break
                nxt = None
                if l + 1 < n_layers and not (dbg and ("no_hoist" in dbg or "stop_layer" in dbg)):
                    Gn, SHn, GATEn = mods[(l + 1) % 2]
                    nxt = lambda stk, l=l, Gn=Gn, SHn=SHn, GATEn=GATEn: phase_adaln(nc, S, None, l + 1, cT_d, w_ada_d, b_ada_d, norm_g_d,
                                                                                   Gn, SHn, GATEn, ones_f, ext_stack=stk)
                phase_out(nc, S, l, x_src, X1, OG, YN, GLA, GLS, wa_d, ws_d, wo_d, GATE_b, nxt)
                if dbg and "stop_layer" in dbg:
                    break
        else:
            phase_final(nc, S, X1 if n_layers > 0 else x_in, fg_d, out_d)
    return nc


def dump(nc, S, DBG, tiles):
    off = 0
    for t in tiles:
        w = t.shape[1]
        S.dma("sp", DBG[0:t.shape[0], off:off + w], t[:])
        off += w
    S.emit()


def phase_adaln(nc, S, lay, l, cT_d, w_ada_d, b_ada_d, norm_g_d, G_b, SH_b, GATE_b, ones_f, ext_stack=None):
    with ExitStack() as st_own:
        st = ext_stack if ext_stack is not None else st_own
        def sb(name, shape, dt=F32):
            return st.enter_context(nc.sbuf_tensor(uname(name), list(shape), dt))
        cT = sb("cT", [128, 8]); b_cT = Buf()
        sc = sb("sc", [128, 8]); b_sc = Buf()
        sg = sb("sg", [128, 8]); b_sg = Buf()
        modrow = sb("modrow", [1, 3 * D]); b_mod = Buf()
        bada = sb("bada", [1, 3 * D]); b_bada = Buf()
        ng = sb("ng", [1, D]); b_ng = Buf()
        grow = sb("grow", [1, D]); b_grow = Buf()
        nwb = 2 if ext_stack is None else 1
        wts = [sb(f"wada{i}", [128, 8, 512]) for i in range(nwb)]
        b_wts = [Buf() for _ in range(nwb)]
        pp = PsumPool(nc, st, ["pa0", "pa1", "pa2", "pa3"] if ext_stack is None else ["pa0", "pa1"])

        S.dma("sp", cT[:], cT_d, writes=[b_cT])
        S.dma("sp", bada[:], b_ada_d[l], writes=[b_bada])
        S.dma("sp", ng[:], norm_g_d[l], writes=[b_ng])
        S.op("act", lambda e: e.activation(out=sg[:], in_=cT[:], func=AF.Sigmoid), reads=[b_cT], writes=[b_sg])
        S.op("dve", lambda e: e.tensor_tensor(out=sc[:], in0=cT[:], in1=sg[:], op=ALU.mult), reads=[b_cT, b_sg], writes=[b_sc])
        wv = w_ada_d[l].rearrange("(kc p) n -> p kc n", p=128)
        for nb in range(6):
            wt, bw = wts[nb % nwb], b_wts[nb % nwb]
            S.dma("sp", wt[:], wv[:, :, nb * 512:(nb + 1) * 512], writes=[bw])
            ps, bp = pp.get()
            for kc in range(8):
                S.op("pe", lambda e, kc=kc, wt=wt, ps=ps: e.matmul(ps[0:1, :], lhsT=sc[:, kc:kc + 1], rhs=wt[:, kc, :],
                                                                  start=(kc == 0), stop=(kc == 7)),
                     reads=[b_sc, bw], writes=[bp])
            S.op("dve", lambda e, nb=nb, ps=ps: e.tensor_tensor(out=modrow[:, nb * 512:(nb + 1) * 512], in0=ps[0:1, :],
                                                                in1=bada[:, nb * 512:(nb + 1) * 512], op=ALU.add),
                 reads=[bp, b_bada], writes=[b_mod])
        S.op("dve", lambda e: e.scalar_tensor_tensor(out=grow[:], in0=modrow[:, D:2 * D], scalar=1.0, in1=ng[:],
                                                     op0=ALU.add, op1=ALU.mult),
             reads=[b_mod, b_ng], writes=[b_grow])
        b_dst = Buf()
        for (row, bsrc, dst) in ((grow[:, :], b_grow, G_b), (modrow[:, 0:D], b_mod, SH_b), (modrow[:, 2 * D:3 * D], b_mod, GATE_b)):
            for hb in range(2):
                ps, bp = pp.get()
                S.op("pe", lambda e, ps=ps, row=row, hb=hb: e.matmul(ps[:], lhsT=ones_f[0:1, :], rhs=row[:, hb * 512:(hb + 1) * 512],
                                                                     start=True, stop=True),
                     reads=[bsrc], writes=[bp])
                S.op("act", lambda e, ps=ps, dst=dst, hb=hb: e.activation(out=dst[:, hb * 512:(hb + 1) * 512], in_=ps[:], func=AF.Copy),
                     reads=[bp], writes=[b_dst])
        if ext_stack is None:
            S.emit()


def phase_proj(nc, S, l, x_src, w_in_d, dtb_d, cosT_d, sinT_d, G_b, SH_b, ident, dst):
    with ExitStack() as st:
        def sb(name, shape, dt=F32):
            return st.enter_context(nc.sbuf_tensor(uname(name), list(shape), dt))
        hT = sb("hT", [128, 8, L], BF16)
        b_hT = [Buf() for _ in range(NT)]
        cosT = sb("cosT", [128, L], BF16); b_cos = Buf()
        sinT = sb("sinT", [128, L], BF16); b_sin = Buf()
        S.dma("pool", cosT[:], cosT_d, writes=[b_cos])
        S.dma("pool", sinT[:], sinT_d, writes=[b_sin])
        tps = [st.enter_context(nc.psum_tensor(uname("tp"), [128, 1024], BF16)) for _ in range(2)]
        b_tps = [Buf(excl=True), Buf(excl=True)]
        pp = PsumPool(nc, st, [f"pj{i}" for i in range(6)])
        NB = 4
        xt = [sb("xt", [128, D]) for _ in range(NB)]; b_xt = [Buf() for _ in range(NB)]
        junk = sb("junk", [128, D], BF16); b_junk = Buf()
        ss = [sb("ss", [128, 4]) for _ in range(NB)]; b_ss = [Buf() for _ in range(NB)]
        h1 = [sb("h1", [128, D]) for _ in range(NB)]; b_h1 = [Buf() for _ in range(NB)]
        hb = [sb("hb", [128, D], BF16) for _ in range(NB)]; b_hb = [Buf() for _ in range(NB)]
        def stage_x(tt):
            k = tt % NB
            tok = slice(tt * 128, (tt + 1) * 128)
            S.dma("sp", xt[k][:], x_src[tok, :], writes=[b_xt[k]])
            S.op("act", lambda e: e.activation(out=junk[:], in_=xt[k][:], func=AF.Square, accum_out=ss[k][:, 0:1]),
                 reads=[b_xt[k]], writes=[b_junk, b_ss[k]])
            S.op("dve", lambda e: e.tensor_scalar(out=ss[k][:, 1:2], in0=ss[k][:, 0:1], scalar1=1.0 / D, scalar2=EPS,
                                                  op0=ALU.mult, op1=ALU.add), reads=[b_ss[k]], writes=[b_ss[k]])
            S.op("act", lambda e: e.activation(out=ss[k][:, 2:3], in_=ss[k][:, 1:2], func=AF.Ln), reads=[b_ss[k]], writes=[b_ss[k]])
            S.op("act", lambda e: e.activation(out=ss[k][:, 3:4], in_=ss[k][:, 2:3], func=AF.Exp, scale=-0.5), reads=[b_ss[k]], writes=[b_ss[k]])
            S.op("dve", lambda e: e.scalar_tensor_tensor(out=h1[k][:], in0=xt[k][:], scalar=ss[k][:, 3:4], in1=G_b[:],
                                                         op0=ALU.mult, op1=ALU.mult),
                 reads=[b_xt[k], b_ss[k]], writes=[b_h1[k]])
            S.op("pool", lambda e: e.tensor_tensor(out=hb[k][:], in0=h1[k][:], in1=SH_b[:], op=ALU.add),
                 reads=[b_h1[k]], writes=[b_hb[k]])

        def stage_y(tt):
            k = tt % NB
            tok = slice(tt * 128, (tt + 1) * 128)
            tp, btp = tps[tt % 2], b_tps[tt % 2]
            for kc in range(8):
                S.op("pe", lambda e, kc=kc: e.transpose(out=tp[:, kc * 128:(kc + 1) * 128],
                                                        in_=hb[k][:, kc * 128:(kc + 1) * 128], identity=ident[:]),
                     reads=[b_hb[k]], writes=[btp])
            S.op("act", lambda e: e.activation(out=hT[:, :, tok], in_=tp[:].rearrange("p (k t) -> p k t", k=8), func=AF.Copy),
                 reads=[btp], writes=[b_hT[tt]])

        stage_x(0)
        stage_x(1)
        for tt in range(NT):
            if tt + 2 < NT:
                stage_x(tt + 2)
            stage_y(tt)

        wv = w_in_d[l].rearrange("(kc p) n -> p kc n", p=128)
        tiles = []
        for t in range(4):
            tiles.append(dict(cols=[(Q0 + 128 * t, 64), (Q0 + 128 * t + 64, 64)], rope=True, kind="rope",
                              dst=[dst["QT"][2 * t], dst["QT"][2 * t + 1]]))
        for t in range(2):
            tiles.append(dict(cols=[(QI0 + 128 * t, 64), (QI0 + 128 * t + 64, 64)], rope=True, kind="rope",
                              dst=[dst["QiT"][2 * t], dst["QiT"][2 * t + 1]]))
        tiles.append(dict(cols=[(K0, 64), (KI0, 64)], rope=True, kind="rope", dst=[dst["KT"], dst["KiT"]]))
        for t in range(4):
            tiles.append(dict(cols=[(GA0 + 128 * t, 128)], rope=False, kind="silu",
                              dst=[dst["GA"][2 * t], dst["GA"][2 * t + 1]]))
        for t in range(12):
            tiles.append(dict(cols=[(XBC0 + 128 * t, 128)], rope=False, kind="copy", dst=[dst["XBC"][t * 128:(t + 1) * 128]]))
        for t in range(8):
            tiles.append(dict(cols=[(GLA0 + 128 * t, 128)], rope=False, kind="sig", dst=[dst["GLA"][t * 128:(t + 1) * 128]]))
        for t in range(8):
            tiles.append(dict(cols=[(GLS0 + 128 * t, 128)], rope=False, kind="sig", dst=[dst["GLS"][t * 128:(t + 1) * 128]]))
        wts = [sb("wt", [128, 8, 128], BF16) for _ in range(2)]
        wtps = [sb("wtp", [128, 8, 128], BF16) for _ in range(2)]
        b_w = [[Buf() for _ in range(2)] for _ in range(2)]
        b_wp = [[Buf() for _ in range(4)] for _ in range(2)]
        NO = 3
        t1 = [sb("t1", [128, 512]) for _ in range(2)]; b_t1 = [Buf(), Buf()]
        t2 = [sb("t2", [128, 512]) for _ in range(2)]; b_t2 = [Buf(), Buf()]
        ob = [sb("ob", [128, 512], BF16) for _ in range(NO)]; b_ob = [Buf() for _ in range(NO)]
        of = [sb("of", [128, 512]) for _ in range(NO)]; b_of = [Buf() for _ in range(NO)]
        cnt = 0
        def load_w(ti):
            T = tiles[ti]
            w = ti % 2
            wt, wtp = wts[w], wtps[w]
            o = 0
            rb = []
            for ci, (c0, n) in enumerate(T["cols"]):
                S.dma("pool", wt[:, :, o:o + n], wv[:, :, c0:c0 + n], writes=[b_w[w][ci]])
                rb.append(b_w[w][ci])
                o += n
            rbp = []
            if T["rope"]:
                o = 0
                for ci, (c0, n) in enumerate(T["cols"]):
                    S.dma("pool", wtp[:, :, o:o + 32], wv[:, :, c0 + 32:c0 + 64], writes=[b_wp[w][2 * ci]])
                    S.dma("pool", wtp[:, :, o + 32:o + 64], wv[:, :, c0:c0 + 32], writes=[b_wp[w][2 * ci + 1]])
                    rbp += [b_wp[w][2 * ci], b_wp[w][2 * ci + 1]]
                    o += 64
            return rb, rbp

        nxt_w = load_w(0)
        for ti, T in enumerate(tiles):
            w = ti % 2
            wt, wtp = wts[w], wtps[w]
            rb, rbp = nxt_w
            if ti + 1 < len(tiles):
                nxt_w = load_w(ti + 1)
            for c in range(8):
                tok = slice(c * 512, (c + 1) * 512)
                hbufs = b_hT[4 * c:4 * c + 4]
                ps, bp = pp.get()
                for kc in range(8):
                    S.op("pe", lambda e, ps=ps, wt=wt, kc=kc, tok=tok: e.matmul(ps[:], lhsT=wt[:, kc, :], rhs=hT[:, kc, tok],
                                                                               start=(kc == 0), stop=(kc == 7)),
                         reads=rb + hbufs, writes=[bp])
                kind = T["kind"]
                if kind == "rope":
                    psp, bpp = pp.get()
                    for kc in range(8):
                        S.op("pe", lambda e, psp=psp, wtp=wtp, kc=kc, tok=tok: e.matmul(psp[:], lhsT=wtp[:, kc, :], rhs=hT[:, kc, tok],
                                                                                     start=(kc == 0), stop=(kc == 7)),
                             reads=rbp + hbufs, writes=[bpp])
                    a = cnt % 2
                    k = cnt % NO
                    S.op("dve", lambda e, a=a, ps=ps, tok=tok: e.tensor_tensor(out=t1[a][:], in0=ps[:], in1=cosT[:, tok], op=ALU.mult),
                         reads=[bp, b_cos], writes=[b_t1[a]])
                    S.op("dve", lambda e, a=a, psp=psp, tok=tok: e.tensor_tensor(out=t2[a][:], in0=psp[:], in1=sinT[:, tok], op=ALU.mult),
                         reads=[bpp, b_sin], writes=[b_t2[a]])
                    S.op("pool", lambda e, a=a, k=k: e.tensor_tensor(out=ob[k][:], in0=t1[a][:], in1=t2[a][:], op=ALU.add),
                         reads=[b_t1[a], b_t2[a]], writes=[b_ob[k]])
                    S.dma("sp", T["dst"][0][:, tok], ob[k][0:64, :], reads=[b_ob[k]])
                    S.dma("sp", T["dst"][1][:, tok], ob[k][64:128, :], reads=[b_ob[k]])
                elif kind == "copy":
                    k = cnt % NO
                    S.op("act", lambda e, k=k, ps=ps: e.activation(out=of[k][:], in_=ps[:], func=AF.Copy), reads=[bp], writes=[b_of[k]])
                    S.dma("sp", T["dst"][0][:, tok], of[k][:], reads=[b_of[k]])
                else:
                    k = cnt % NO
                    fn = AF.Silu if kind == "silu" else AF.Sigmoid
                    S.op("act", lambda e, k=k, ps=ps, fn=fn: e.activation(out=ob[k][:], in_=ps[:], func=fn), reads=[bp], writes=[b_ob[k]])
                    if len(T["dst"]) == 2:
                        S.dma("sp", T["dst"][0][:, tok], ob[k][0:64, :], reads=[b_ob[k]])
                        S.dma("sp", T["dst"][1][:, tok], ob[k][64:128, :], reads=[b_ob[k]])
                    else:
                        S.dma("sp", T["dst"][0][:, tok], ob[k][:], reads=[b_ob[k]])
                cnt += 1
        wz = sb("wz", [128, 8, 1024], BF16); b_wz = Buf()
        S.dma("pool", wz[:], wv[:, :, Z0:Z0 + 1024], writes=[b_wz])
        wsm = sb("wsm", [128, 8, 84], BF16); b_wsm = [Buf() for _ in range(3)]
        S.dma("pool", wsm[:, :, 0:64], wv[:, :, V0:V0 + 64], writes=[b_wsm[0]])
        S.dma("pool", wsm[:, :, 64:68], wv[:, :, WI0:WI0 + 4], writes=[b_wsm[1]])
        S.dma("pool", wsm[:, :, 68:84], wv[:, :, DT0:DT0 + 16], writes=[b_wsm[2]])
        dtb = sb("dtb", [128, 16]); b_dtb = Buf()
        S.dma("sp", dtb[:], dtb_d[l].partition_broadcast(128), writes=[b_dtb])
        zb = [sb("zb", [128, D], BF16) for _ in range(2)]; b_zb = [Buf(), Buf()]
        va = [sb("va", [128, 65], BF16) for _ in range(2)]; b_va = [Buf(), Buf()]
        wib = [sb("wib", [128, 4]) for _ in range(2)]; b_wib = [Buf(), Buf()]
        wiall = sb("wiall", [128, NT, 4]); b_wiall = Buf()
        dtall = sb("dtall", [128, 2, NT, 16]); b_dtall = Buf()
        for tt in range(NT):
            k = tt % 2
            tok = slice(tt * 128, (tt + 1) * 128)
            for nb in range(2):
                ps, bp = pp.get()
                for kc in range(8):
                    S.op("pe", lambda e, ps=ps, kc=kc, tok=tok, nb=nb: e.matmul(ps[:], lhsT=hT[:, kc, tok], rhs=wz[:, kc, nb * 512:(nb + 1) * 512],
                                                                               start=(kc == 0), stop=(kc == 7)),
                         reads=[b_wz, b_hT[tt]], writes=[bp])
                S.op("act", lambda e, k=k, ps=ps, nb=nb: e.activation(out=zb[k][:, nb * 512:(nb + 1) * 512], in_=ps[:], func=AF.Silu),
                     reads=[bp], writes=[b_zb[k]])
            S.dma("sp", dst["ZS"][tok, :], zb[k][:], reads=[b_zb[k]])
            ps, bp = pp.get()
            for kc in range(8):
                S.op("pe", lambda e, ps=ps, kc=kc, tok=tok: e.matmul(ps[:, 0:84], lhsT=hT[:, kc, tok], rhs=wsm[:, kc, :],
                                                                    start=(kc == 0), stop=(kc == 7)),
                     reads=b_wsm + [b_hT[tt]], writes=[bp])
            S.op("pool", lambda e, k=k: e.memset(va[k][:, 64:65], 1.0), writes=[b_va[k]])
            S.op("act", lambda e, k=k, ps=ps: e.activation(out=va[k][:, 0:64], in_=ps[:, 0:64], func=AF.Copy), reads=[bp], writes=[b_va[k]])
            S.dma("sp", dst["VA"][tok, :], va[k][:], reads=[b_va[k]])
            S.op("dve", lambda e, ps=ps, tt=tt: e.tensor_scalar(out=wiall[:, tt, :], in0=ps[:, 64:68], scalar1=0.5, scalar2=None, op0=ALU.mult),
                 reads=[bp], writes=[b_wiall])
            S.op("dve", lambda e, ps=ps, tt=tt: e.tensor_tensor(out=dtall[:, 0, tt, :], in0=ps[:, 68:84], in1=dtb[:], op=ALU.add),
                 reads=[bp, b_dtb], writes=[b_dtall])
        S.dma("sp", dst["WI"].rearrange("(j p) d -> p j d", p=128), wiall[:], reads=[b_wiall])
        S.op("act", lambda e: e.activation(out=dtall[:, 1, :, :], in_=dtall[:, 0, :, :], func=AF.Exp), reads=[b_dtall], writes=[b_dtall])
        S.op("dve", lambda e: e.tensor_scalar(out=dtall[:, 0, :, :], in0=dtall[:, 1, :, :], scalar1=1.0, scalar2=None, op0=ALU.add),
             reads=[b_dtall], writes=[b_dtall])
        S.op("act", lambda e: e.activation(out=dtall[:, 1, :, :], in_=dtall[:, 0, :, :], func=AF.Ln), reads=[b_dtall], writes=[b_dtall])
        S.dma("sp", dst["DT"].rearrange("(j p) h -> p j h", p=128), dtall[:, 1, :, :], reads=[b_dtall])
        S.emit()


def phase_attn(nc, S, l, QT, KT, QiT, KiT, GA, VA, WI, OG, cbias_d, pow2_d, ident_d, ones_f, ident_bf, DBG, dbg):
    import os
    with ExitStack() as st:
        def sb(name, shape, dt=F32):
            return st.enter_context(nc.sbuf_tensor(uname(name), list(shape), dt))
        KTa = sb("KTa", [65, L], BF16); b_KTa = Buf()
        KiTs = sb("KiTs", [64, L], BF16); b_KiT = Buf()
        VAs = sb("VAs", [128, NT, 65], BF16); b_VA = Buf()
        WIs = sb("WIs", [128, NT, 4]); b_WI = Buf()
        cb = sb("cb", [128, 128]); b_cb = Buf()
        pw = sb("pw", [128, NBIS + 2]); b_pw = Buf()
        Irep = sb("Irep", [128, 8, 128], BF16); b_Irep = [Buf() for _ in range(8)]
        sel = sb("sel", [64, 65], BF16); b_sel = Buf()
        ksq = sb("ksq", [64, L], BF16); b_ksq = Buf()
        km = sb("km", [65, 16]); b_km = Buf()
        S.dma("sp", KTa[0:64, :], KT, writes=[b_KTa])
        b_KTa1 = Buf()
        S.op("dve", lambda e: e.memset(KTa[64:65, :], 1.0), reads=[], writes=[b_KTa1])
        S.dma("sp", KiTs[:], KiT, writes=[b_KiT])
        S.dma("sp", VAs[:], VA.rearrange("(j p) d -> p j d", p=128), writes=[b_VA])
        S.dma("sp", WIs[:], WI.rearrange("(j p) d -> p j d", p=128), writes=[b_WI])
        S.dma("sp", cb[:], cbias_d, writes=[b_cb])
        S.dma("sp", pw[:], pow2_d, writes=[b_pw])
        for h in range(8):
            S.dma("pool", Irep[:, h, :], ident_d, writes=[b_Irep[h]])
        S.op("dve", lambda e: e.memset(sel[:], 0.0), writes=[b_sel])
        S.op("dve", lambda e: e.memset(sel[:, 64:65], 1.0), writes=[b_sel])
        pO = [st.enter_context(nc.psum_tensor(uname("pO"), [128, 512], F32)) for _ in range(2)]
        b_pO = [Buf(excl=True), Buf(excl=True)]
        pST = PsumPool(nc, st, [f"pst{i}" for i in range(2)])
        pX = PsumPool(nc, st, ["px0", "px1", "px2"])
        pS = PsumPool(nc, st, ["psc0"])
        S.op("act", lambda e: e.activation(out=ksq[:], in_=KTa[0:64, :], func=AF.Square), reads=[b_KTa], writes=[b_ksq])
        for c in range(8):
            ps, bp = pX.get()
            S.op("pe", lambda e, ps=ps, c=c: e.matmul(ps[0:65, :], lhsT=sel[:], rhs=ksq[:, c * 512:(c + 1) * 512], start=True, stop=True),
                 reads=[b_sel, b_ksq], writes=[bp])
            S.op("dve", lambda e, ps=ps, c=c: e.tensor_reduce(out=km[64:65, c:c + 1], in_=ps[64:65, :], axis=AX.X, op=ALU.max),
                 reads=[bp], writes=[b_km])
        S.op("dve", lambda e: e.tensor_reduce(out=km[64:65, 8:9], in_=km[64:65, 0:8], axis=AX.X, op=ALU.max), reads=[b_km], writes=[b_km])

        NB = 2
        NR = 4
        qa = [sb("qa", [65, 8, 128], BF16) for _ in range(NR)]; b_qa = [Buf() for _ in range(NR)]; b_qar = [Buf() for _ in range(NR)]
        qi = [sb("qi", [64, 4, 128], BF16) for _ in range(NB)]; b_qi = [Buf() for _ in range(NB)]
        ga = [sb("ga", [64, 8, 128], BF16) for _ in range(2)]; b_ga = [Buf() for _ in range(2)]
        qsq = sb("qsq", [64, 1024], BF16); b_qsq = Buf()
        qtmp = sb("qtmp", [65, 1024]); b_qtmp = Buf()
        wab = [sb("wab", [128, 8]) for _ in range(NB)]; b_wab = [Buf() for _ in range(NB)]
        score = [sb("score", [128, L]) for _ in range(NB)]; b_score = [Buf() for _ in range(NB)]
        mb = [sb("mb", [128, L], BF16) for _ in range(NR)]; b_mb = [Buf() for _ in range(NR)]
        junk = [sb("junkb", [128, L], BF16) for _ in range(NB)]; b_junk = [Buf() for _ in range(NB)]
        NRL = 8
        rl = [sb("rl", [128, 512], BF16) for _ in range(NRL)]; b_rl = [Buf() for _ in range(NRL)]
        Dh = [sb("Dh", [128, 4, 128], BF16) for _ in range(NB)]; b_Dh = [Buf() for _ in range(NB)]
        bs = [sb("bs", [128, 8]) for _ in range(NB)]; b_bs = [Buf() for _ in range(NB)]
        ab = [sb("ab", [128, 8]) for _ in range(NB)]; b_ab = [Buf() for _ in range(NB)]
        p2 = sb("p2", [128, 2]); b_p2 = Buf()
        q2 = sb("q2", [128, 2]); b_q2 = Buf()
        C2 = sb("C2", [128, 2, 4]); b_cnt = [Buf(), Buf()]; b_t2 = Buf()
        ab2 = sb("ab2", [128, 2]); b_ab2 = [Buf(), Buf()]
        Sp2 = sb("Sp2", [128, 2, NBIS + 2]); b_S2 = Buf()
        b_junkA = [Buf() for _ in range(NB)]
        ALPHA = float(os.environ.get("ATT_ALPHA", "0.7"))
        Sp = [sb("Sp", [128, NBIS + 2]) for _ in range(NB)]
        Sn = [sb("Sn", [128, NBIS + 2]) for _ in range(NB)]
        b_S = [Buf() for _ in range(NB)]
        PT = [sb("PT", [128, 1024], BF16) for _ in range(3)]; b_PT = [Buf() for _ in range(3)]
        rr = sb("rr", [65, 1024]); b_rr = Buf()
        rbc = sb("rbc", [64, 1024]); b_rbc = Buf()
        o1 = sb("o1", [64, 1024]); b_o1 = Buf()
        ogb = [sb("ogb", [64, 8, 128], BF16) for _ in range(2)]; b_ogb = [Buf(), Buf()]
        rlc = [0]
        ptc = [0]

        AMX_ = os.environ.get("ATT_AMX", "dve")
        ACT_EVERY = int(os.environ.get("ATT_ACT_EVERY", "4"))

        def on_act(i):
            return (i % ACT_EVERY == ACT_EVERY - 1) and not (dbg and "no_act_bis" in dbg)

        def pre(i):
            k = i % NB
            k4 = i % NR
            n = 128 * (i + 1)
            tok = slice(i * 128, (i + 1) * 128)
            S.dma("sp", qa[k4][0:64, :, :], QT[:, :, tok].rearrange("h d t -> d h t"), writes=[b_qa[k4]])
            S.dma("sp", qi[k][:], QiT[:, :, tok].rearrange("h d t -> d h t"), writes=[b_qi[k]])
            S.op("act", lambda e: e.activation(out=wab[k][:, 0:4], in_=WIs[:, i, :], func=AF.Abs, scale=0.125),
                 reads=[b_WI], writes=[b_wab[k]])
            S.op("dve", lambda e: e.tensor_scalar(out=wab[k][:, 4:8], in0=WIs[:, i, :], scalar1=0.0, scalar2=2.0,
                                                  op0=ALU.is_ge, op1=ALU.mult), reads=[b_WI], writes=[b_wab[k]])
            S.op("dve", lambda e: e.tensor_scalar(out=wab[k][:, 4:8], in0=wab[k][:, 4:8], scalar1=-1.0, scalar2=None,
                                                  op0=ALU.add), reads=[b_wab[k]], writes=[b_wab[k]])
            S.op("act", lambda e: e.activation(out=qsq[:], in_=qa[k4][0:64, :, :].rearrange("p h t -> p (h t)"), func=AF.Square),
                 reads=[b_qa[k4]], writes=[b_qsq])
            for hb in range(2):
                ps, bp = pX.get()
                S.op("pe", lambda e, ps=ps, hb=hb: e.matmul(ps[0:65, :], lhsT=sel[:], rhs=qsq[:, hb * 512:(hb + 1) * 512], start=True, stop=True),
                     reads=[b_sel, b_qsq], writes=[bp])
                S.op("act", lambda e, ps=ps, hb=hb: e.activation(out=qtmp[64:65, hb * 512:(hb + 1) * 512], in_=ps[64:65, :], func=AF.Sqrt,
                                                                 scale=km[64:65, 8:9]), reads=[bp, b_km], writes=[b_qtmp])
                S.op("dve", lambda e, hb=hb: e.tensor_scalar(out=qa[k4][64:65, hb * 4:(hb + 1) * 4, :].rearrange("p h t -> p (h t)"),
                                                             in0=qtmp[64:65, hb * 512:(hb + 1) * 512], scalar1=-1.0, scalar2=None, op0=ALU.mult),
                     reads=[b_qtmp], writes=[b_qar[k4]])
            S.op("dve", lambda e: e.tensor_tensor(out=Dh[k][:], in0=ident_bf[:].unsqueeze(1).to_broadcast([128, 4, 128]),
                                                  in1=wab[k][:, 4:8].unsqueeze(2).to_broadcast([128, 4, 128]), op=ALU.mult),
                 reads=[b_wab[k]], writes=[b_Dh[k]])
            def dots(c0):
                w = min(512, n - c0)
                rs = []
                for h in range(4):
                    ps, bp = pX.get()
                    S.op("pe", lambda e, ps=ps, h=h, c0=c0, w=w: e.matmul(ps[:, 0:w], lhsT=qi[k][:, h, :], rhs=KiTs[:, c0:c0 + w], start=True, stop=True),
                         reads=[b_qi[k], b_KiT], writes=[bp])
                    r = rlc[0] % NRL
                    rlc[0] += 1
                    rs.append(r)
                    if h < 2:
                        S.op("act", lambda e, ps=ps, h=h, w=w, r=r: e.activation(out=rl[r][:, 0:w], in_=ps[:, 0:w], func=AF.Relu, scale=wab[k][:, h:h + 1]),
                             reads=[bp, b_wab[k]], writes=[b_rl[r]])
                    else:
                        S.op("dve", lambda e, ps=ps, h=h, w=w, r=r: e.tensor_scalar(out=rl[r][:, 0:w], in0=ps[:, 0:w], scalar1=wab[k][:, h:h + 1], scalar2=0.0,
                                                                                   op0=ALU.mult, op1=ALU.max),
                             reads=[bp, b_wab[k]], writes=[b_rl[r]])
                return (c0, w, rs)

            def signs(c0, w, rs):
                pss, bps = pS.get()
                for h in range(4):
                    r = rs[h]
                    S.op("pe", lambda e, pss=pss, h=h, w=w, r=r: e.matmul(pss[:, 0:w], lhsT=Dh[k][:, h, :], rhs=rl[r][:, 0:w], start=(h == 0), stop=(h == 3)),
                         reads=[b_Dh[k], b_rl[r]], writes=[bps])
                if (c0 // 512) % 2 == 0:
                    S.op("act", lambda e, pss=pss, w=w, c0=c0: e.activation(out=score[k][:, c0:c0 + w], in_=pss[:, 0:w], func=AF.Copy),
                         reads=[bps], writes=[b_score[k]])
                else:
                    S.op("dve", lambda e, pss=pss, w=w, c0=c0: e.tensor_copy(out=score[k][:, c0:c0 + w], in_=pss[:, 0:w]),
                         reads=[bps], writes=[b_score[k]])

            pend = None
            for c0 in range(0, n, 512):
                cur = dots(c0)
                if pend is not None:
                    signs(*pend)
                pend = cur
            signs(*pend)
            B = bs[k]
            S.op(AMX_, lambda e: e.tensor_reduce(out=B[:, 0:1], in_=score[k][:, 0:n], axis=AX.X, op=ALU.max, apply_absolute_value=True),
                 reads=[b_score[k]], writes=[b_bs[k]])
            S.op("dve", lambda e: e.tensor_scalar(out=B[:, 1:2], in0=B[:, 0:1], scalar1=1.001, scalar2=1e-6, op0=ALU.mult, op1=ALU.add),
                 reads=[b_bs[k]], writes=[b_bs[k]])
            S.op("dve", lambda e: e.tensor_tensor(out=score[k][:, i * 128:(i + 1) * 128], in0=score[k][:, i * 128:(i + 1) * 128], in1=cb[:], op=ALU.add),
                 reads=[b_score[k], b_cb], writes=[b_score[k]])
            S.op("dve", lambda e: e.tensor_scalar(out=Sp2[:, k, :], in0=pw[:], scalar1=B[:, 1:2], scalar2=None, op0=ALU.mult),
                 reads=[b_bs[k], b_pw], writes=[b_S2])
            S.op("dve", lambda e: e.tensor_scalar(out=q2[:, k:k + 1], in0=B[:, 1:2], scalar1=-1.0, scalar2=None, op0=ALU.mult),
                 reads=[b_bs[k]], writes=[b_q2])
            S.op("dve", lambda e: e.tensor_tensor(out=p2[:, k:k + 1], in0=Sp2[:, k, 0:1], in1=q2[:, k:k + 1], op=ALU.add),
                 reads=[b_S2, b_q2], writes=[b_p2])
            if split(n) == n:
                S.op("dve", lambda e: e.memset(ab2[:, k:k + 1], 0.0), writes=[b_ab2[k]])

        def split(n):
            nA = int(round(ALPHA * n / 128.0)) * 128
            nA = max(128, min(n, nA))
            if n - nA < 256:
                nA = n
            return nA

        def bis_iter_group(tiles, j):
            for i in tiles:
                k = i % NB
                n = 128 * (i + 1)
                nA = split(n)
                nB = n - nA
                S.op("dve", lambda e, k=k, nA=nA, nB=nB: e.tensor_scalar(out=junk[k][:, 0:nA], in0=score[k][:, 0:nA], scalar1=p2[:, k:k + 1],
                                                                        scalar2=float(-(256.0 - nB / 2.0)), op0=ALU.is_ge, op1=ALU.add,
                                                                        accum_out=C2[:, k, 0:1]),
                     reads=[b_score[k], b_p2], writes=[b_junk[k], b_cnt[k]])
                if nB > 0:
                    S.op("act", lambda e, k=k, nA=nA, n=n: e.activation(out=junk[k][:, nA:n], in_=score[k][:, nA:n], func=AF.Sign, bias=p2[:, k:k + 1], scale=-1.0,
                                                                         accum_out=ab2[:, k:k + 1]), reads=[b_score[k], b_p2], writes=[b_junkA[k], b_ab2[k]])
            S.op("dve", lambda e: e.scalar_tensor_tensor(out=C2[:, :, 1], in0=ab2[:], scalar=-0.5, in1=C2[:, :, 0], op0=ALU.mult, op1=ALU.add),
                 reads=b_ab2 + b_cnt, writes=[b_t2])
            S.op("dve", lambda e: e.scalar_tensor_tensor(out=C2[:, :, 2], in0=C2[:, :, 1], scalar=0.0, in1=Sp2[:, :, j], op0=ALU.is_ge, op1=ALU.mult),
                 reads=[b_t2, b_S2], writes=[b_t2])
            S.op("dve", lambda e: e.tensor_tensor(out=q2[:], in0=q2[:], in1=C2[:, :, 2], op=ALU.add), reads=[b_q2, b_t2], writes=[b_q2])
            S.op("dve", lambda e: e.tensor_tensor(out=p2[:], in0=q2[:], in1=Sp2[:, :, j + 1], op=ALU.add), reads=[b_q2, b_S2], writes=[b_p2])

        def post(i):
            k = i % NB
            k4 = i % NR
            n = 128 * (i + 1)
            B = bs[k]
            S.op("dve", lambda e: e.tensor_scalar(out=mb[k4][:, 0:n], in0=score[k][:, 0:n], scalar1=q2[:, k:k + 1], scalar2=-30000.0,
                                                  op0=ALU.is_lt, op1=ALU.mult), reads=[b_score[k], b_q2], writes=[b_mb[k4]])
            if dbg and "dump_attn" in dbg and i == dbg["dump_attn"]:
                S.dma("sp", DBG[:, 0:n], score[k][:, 0:n], reads=[b_score[k]])
                S.dma("sp", DBG[:, 4080:4082], bs[k][:, 0:2], reads=[b_bs[k]])
                S.dma("sp", DBG[:, 4085:4086], q2[:, k:k + 1], reads=[b_q2], allow_slow_non_contiguous=True)

        def stage1_group(tiles, units):
            nslots = len(tiles) + NBIS
            per = -(-len(units) // nslots) if units else 0
            pos = [0]

            def drain(final=False):
                m = len(units) if final else min(len(units), pos[0] + per)
                while pos[0] < m:
                    units[pos[0]]()
                    pos[0] += 1

            for t in tiles:
                pre(t)
                drain()
            for j in range(NBIS):
                bis_iter_group(tiles, j)
                drain()
            for t in tiles:
                post(t)
            drain(final=True)

        def stage2_units(i):
            k = i % NR
            tok = slice(i * 128, (i + 1) * 128)
            g = i % 2
            units = []

            pidx = {}

            def block(j):
                if j == 0:
                    S.dma("sp", ga[g][:], GA[:, :, tok].rearrange("h d t -> d h t"), writes=[b_ga[g]])
                ks = slice(j * 128, (j + 1) * 128)
                p = ptc[0] % 3
                ptc[0] += 1
                pidx[j] = p
                for hb in range(2):
                    ps, bp = pST.get()
                    S.op("pe", lambda e, ps=ps, hb=hb, ks=ks: e.matmul(ps[:], lhsT=KTa[:, ks], rhs=qa[k][:, hb * 4:(hb + 1) * 4, :], start=True, stop=False),
                         reads=[b_KTa, b_KTa1, b_qa[k], b_qar[k]], writes=[bp])
                    S.op("pe", lambda e, ps=ps, hb=hb, ks=ks: e.matmul(ps[:], lhsT=mb[k][:, ks], rhs=Irep[:, hb * 4:(hb + 1) * 4, :], start=False, stop=True),
                         reads=[b_mb[k]] + b_Irep, writes=[bp])
                    S.op("act", lambda e, ps=ps, hb=hb, p=p: e.activation(out=PT[p][:, hb * 512:(hb + 1) * 512], in_=ps[:], func=AF.Exp, scale=0.125),
                         reads=[bp], writes=[b_PT[p]])

            def pv(j):
                p = pidx[j]
                for hb in range(2):
                    S.op("pe", lambda e, hb=hb, p=p, j=j: e.matmul(pO[hb][0:65, :], lhsT=VAs[:, j, :], rhs=PT[p][:, hb * 512:(hb + 1) * 512],
                                                                  start=(j == 0), stop=(j == i)), reads=[b_VA, b_PT[p]], writes=[b_pO[hb]])

            def fin():
              for hb in range(2):
                hs = slice(hb * 512, (hb + 1) * 512)
                S.op("act", lambda e, hb=hb, hs=hs: e.activation(out=rr[64:65, hs], in_=pO[hb][64:65, :], func=AF.Ln), reads=[b_pO[hb]], writes=[b_rr])
                S.op("act", lambda e, hs=hs: e.activation(out=rr[64:65, hs], in_=rr[64:65, hs], func=AF.Exp, scale=-1.0), reads=[b_rr], writes=[b_rr])
                ps, bp = pST.get()
                S.op("pe", lambda e, ps=ps, hs=hs: e.matmul(ps[0:64, :], lhsT=ones_f[64:65, 0:64], rhs=rr[64:65, hs], start=True, stop=True),
                     reads=[b_rr], writes=[bp])
                S.op("act", lambda e, ps=ps, hs=hs: e.activation(out=rbc[:, hs], in_=ps[0:64, :], func=AF.Copy), reads=[bp], writes=[b_rbc])
                S.op("dve", lambda e, hb=hb, hs=hs: e.tensor_tensor(out=o1[:, hs], in0=pO[hb][0:64, :], in1=rbc[:, hs], op=ALU.mult),
                     reads=[b_pO[hb], b_rbc], writes=[b_o1])
              S.op("pool", lambda e: e.tensor_tensor(out=ogb[g][:].rearrange("p h t -> p (h t)"), in0=o1[:],
                                                     in1=ga[g][:].rearrange("p h t -> p (h t)"), op=ALU.mult),
                   reads=[b_o1, b_ga[g]], writes=[b_ogb[g]])
              S.dma("pool", OG[:, :, tok].rearrange("h d t -> d h t"), ogb[g][:], reads=[b_ogb[g]])

            def mk(j):
                def u():
                    block(j)
                    if j > 0:
                        pv(j - 1)
                return u

            for j in range(i + 1):
                units.append(mk(j))

            def last():
                pv(i)
                fin()
            units.append(last)
            return units

        nq = NT if not (dbg and "nq" in dbg) else dbg["nq"]
        groups = [list(range(a, min(a + NB, nq))) for a in range(0, nq, NB)]
        prev = []
        for grp in groups:
            stage1_group(grp, prev)
            prev = []
            for t in grp:
                prev += stage2_units(t)
        for u in prev:
            u()
        S.emit()


def phase_ssd(nc, S, l, XBC, ZS, DTs, YN, convw_d, convb_d, alog_d, dskip_d, sng_d, triu_d, sut_d, ident, identf, ones_f, ones_b, DBG, dbg):
    with ExitStack() as st:
        def sb(name, shape, dt=F32):
            return st.enter_context(nc.sbuf_tensor(uname(name), list(shape), dt))

        import os
        PE_ = os.environ.get('SSD_POOL', 'pool')

        def bc3(ap, shape):
            return ap.unsqueeze(2).to_broadcast(shape)

        cw = sb("cw", [128, 12, 4]); b_cw = Buf()
        cbv = sb("cbv", [128, 12]); b_cbv = Buf()
        a_b = sb("a_b", [128, 16]); b_ab = Buf()
        dsk = sb("dsk", [128, 16]); b_dsk = Buf()
        sng = sb("sng", [128, D]); b_sng = Buf()
        triu = sb("triu", [128, 128]); b_triu = Buf()
        sut = sb("sut", [128, 128]); b_sut = Buf()
        b_idf = Buf()
        DTall = sb("DTall", [128, NT, 16]); b_DT = Buf()
        H = sb("H", [128, 16, 64]); b_H = Buf()
        Hb = sb("Hb", [128, 16, 64], BF16); b_Hb = Buf()
        Ht = sb("Ht", [128, 16, 64]); b_Ht = Buf()
        S.dma("sp", cw[:], convw_d[l], writes=[b_cw])
        S.dma("sp", cbv[:], convb_d[l], writes=[b_cbv])
        S.dma("sp", a_b[:], alog_d[l].partition_broadcast(128), writes=[b_ab])
        S.dma("sp", dsk[:], dskip_d[l].partition_broadcast(128), writes=[b_dsk])
        S.dma("sp", sng[:], sng_d[l].partition_broadcast(128), writes=[b_sng])
        S.dma("sp", triu[:], triu_d, writes=[b_triu])
        S.dma("sp", sut[:], sut_d, writes=[b_sut])
        S.dma("sp", DTall[:], DTs.rearrange("(c p) h -> p c h", p=128), writes=[b_DT])
        S.op("act", lambda e: e.activation(out=a_b[:], in_=a_b[:], func=AF.Exp), reads=[b_ab], writes=[b_ab])
        S.op("dve", lambda e: e.tensor_scalar(out=a_b[:], in0=a_b[:], scalar1=-1.0, scalar2=None, op0=ALU.mult), reads=[b_ab], writes=[b_ab])
        S.op("dve", lambda e: e.memset(H[:], 0.0), writes=[b_H])
        S.op("dve", lambda e: e.memset(Hb[:], 0.0), writes=[b_Hb])

        pp = PsumPool(nc, st, [f"ps{i}" for i in range(6)])
        tpB = st.enter_context(nc.psum_tensor(uname("tpB"), [128, 1024], BF16)); b_tpB = Buf(excl=True)
        tpY = st.enter_context(nc.psum_tensor(uname("tpY"), [128, 1024], BF16)); b_tpY = Buf(excl=True)

        xin = sb("xin", [128, 12, 515]); b_xin = Buf()
        cv = sb("cv", [128, 12, 512]); b_cv = Buf()
        cvx = sb("cvx", [128, 8, 512]); b_cvx = Buf()
        cvbc2 = [sb("cvbc", [128, 4, 512], BF16) for _ in range(2)]; b_cvbc2 = [Buf(), Buf()]
        xs_tok2 = [sb("xs_tok", [128, 16, 64]) for _ in range(2)]; b_xst2 = [Buf(), Buf()]
        xd2 = [sb("xd", [128, 16, 64], BF16) for _ in range(2)]; b_xd2 = [Buf(), Buf()]
        xds2 = [sb("xds", [128, 16, 64], BF16) for _ in range(2)]; b_xds2 = [Buf(), Buf()]
        Btok2 = [sb("Btok", [128, 256], BF16) for _ in range(2)]; b_Btok2 = [Buf(), Buf()]
        dd2 = [sb("dd", [128, 8, 16]) for _ in range(2)]; b_dd2 = [Buf(), Buf()]
        Mm2 = [sb("Mm", [128, 16, 128], BF16) for _ in range(2)]; b_Mm2 = [Buf(), Buf()]
        zs2 = [sb("zs", [128, D], BF16) for _ in range(2)]; b_zs2 = [Buf(), Buf()]
        csb = sb("csb", [128, 32]); b_csb = Buf()
        LH = sb("LH", [128, 2, 16, 128], BF16); b_LH = Buf()
        hl = sb("hl", [128, 2, 16], BF16); b_hl = Buf()
        hlf = sb("hlf", [128, 2, 16]); b_hlf = Buf()
        triub = sb("triub", [128, 128], BF16); b_triub = Buf()
        sutb = sb("sutb", [128, 128], BF16); b_sutb = Buf()
        S.op("dve", lambda e: e.tensor_copy(out=triub[:], in_=triu[:]), reads=[b_triu], writes=[b_triub])
        S.op("dve", lambda e: e.tensor_copy(out=sutb[:], in_=sut[:]), reads=[b_sut], writes=[b_sutb])
        dec = sb("dec", [128, 16, 128], BF16); b_dec = Buf()
        mcb = sb("mcb", [128, 2, 128], BF16); b_mcb = Buf()
        yo = sb("yo", [128, 16, 64]); b_yo = Buf()
        y1 = sb("y1", [128, 16, 64]); b_y1 = Buf()
        y2 = sb("y2", [128, 16, 64]); b_y2 = Buf()
        nst = sb("nst", [128, 8]); b_nst = Buf()
        junk = sb("junk3", [128, 512], BF16); b_junk = Buf()
        yn = sb("yn", [128, D], BF16); b_yn = Buf()
        ynT = sb("ynT", [128, 8, 128], BF16); b_ynT = Buf()

        XBCv = XBC.rearrange("(ct p) t -> p ct t", p=128)
        YNv = YN.rearrange("(ct p) t -> p ct t", p=128)
        nsc = 8 if not (dbg and "nsc" in dbg) else dbg["nsc"]

        xin2 = [xin, sb("xin_b", [128, 12, 515])]; b_xin2 = [b_xin, Buf()]
        b_cvc = [Buf() for _ in range(12)]

        def conv_load(sc):
            t0 = sc * 512
            xi, bx = xin2[sc % 2], b_xin2[sc % 2]
            if sc == 0:
                S.op("pool", lambda e: e.memset(xi[:, :, 0:3], 0.0), writes=[bx])
                S.dma("sp", xi[:, :, 3:515], XBCv[:, :, 0:512], writes=[bx])
            else:
                S.dma("sp", xi[:], XBCv[:, :, t0 - 3:t0 + 512], writes=[bx])

        def conv_units(sc):
            xi, bx = xin2[sc % 2], b_xin2[sc % 2]
            cvbc, b_cvbc = cvbc2[sc % 2], b_cvbc2[sc % 2]

            def mk(cts, last):
                def u():
                    for ct in cts:
                        S.op(PE_, lambda e, ct=ct: e.tensor_scalar(out=cv[:, ct, :], in0=xi[:, ct, 3:515], scalar1=cw[:, ct, 3:4], scalar2=cbv[:, ct:ct + 1],
                                                                   op0=ALU.mult, op1=ALU.add), reads=[bx, b_cw, b_cbv], writes=[b_cvc[ct]])
                        for kk in range(3):
                            S.op("dve", lambda e, ct=ct, kk=kk: e.scalar_tensor_tensor(out=cv[:, ct, :], in0=xi[:, ct, kk:kk + 512], scalar=cw[:, ct, kk:kk + 1],
                                                                                        in1=cv[:, ct, :], op0=ALU.mult, op1=ALU.add),
                                 reads=[bx, b_cw, b_cvc[ct]], writes=[b_cvc[ct]])
                    if last:
                        S.op("act", lambda e: e.activation(out=cvx[:], in_=cv[:, 0:8, :], func=AF.Silu), reads=b_cvc[0:8], writes=[b_cvx])
                        S.op("act", lambda e: e.activation(out=cvbc[:], in_=cv[:, 8:12, :], func=AF.Silu), reads=b_cvc[8:12], writes=[b_cvbc])
                return u
            return [mk([0, 1, 2], False), mk([3, 4, 5], False), mk([6, 7, 8], False), mk([9, 10, 11], True)]

        def units_A(c):
            sc, cc = c // 4, c % 4
            k = c % 2
            cvbc, b_cvbc = cvbc2[sc % 2], b_cvbc2[sc % 2]
            xs_tok, b_xst, xd, b_xd, xds, b_xds = xs_tok2[k], b_xst2[k], xd2[k], b_xd2[k], xds2[k], b_xds2[k]
            Btok, b_Btok, dd, b_dd, Mm, b_Mm, zs, b_zs = Btok2[k], b_Btok2[k], dd2[k], b_dd2[k], Mm2[k], b_Mm2[k], zs2[k], b_zs2[k]
            cols = slice(cc * 128, (cc + 1) * 128)
            tok = slice(c * 128, (c + 1) * 128)
            dt = DTall[:, c, :]

            def a0():
                S.dma("sp", zs[:], ZS[tok, :], writes=[b_zs])
                tA = [pp.get(), pp.get()]
                for ct in range(8):
                    ps, bp = tA[ct // 4]
                    S.op("pe", lambda e, ps=ps, ct=ct: e.transpose(out=ps[:, (ct % 4) * 128:(ct % 4 + 1) * 128], in_=cvx[:, ct, cols], identity=identf[:]),
                         reads=[b_cvx, b_idf], writes=[bp])
                for g in range(2):
                    S.op("pe", lambda e, g=g: e.transpose(out=tpB[:, g * 128:(g + 1) * 128], in_=cvbc[:, g, cols], identity=ident[:]),
                         reads=[b_cvbc], writes=[b_tpB])
                S.op("act", lambda e: e.activation(out=Btok[:], in_=tpB[:, 0:256], func=AF.Copy), reads=[b_tpB], writes=[b_Btok])
                for hb in range(2):
                    ps, bp = tA[hb]
                    hs = slice(hb * 8, (hb + 1) * 8)
                    S.op("dve", lambda e, ps=ps, hs=hs: e.tensor_tensor(out=xd[:, hs, :], in0=ps[:].rearrange("p (h d) -> p h d", h=8),
                                                                        in1=bc3(dt[:, hs], [128, 8, 64]), op=ALU.mult),
                         reads=[bp, b_DT], writes=[b_xd])
                    S.op("act", lambda e, ps=ps, hs=hs: e.activation(out=xs_tok[:, hs, :], in_=ps[:].rearrange("p (h d) -> p h d", h=8), func=AF.Copy),
                         reads=[bp], writes=[b_xst])

            def a1():
                S.op("dve", lambda e: e.tensor_tensor(out=dd[:, 0, :], in0=dt, in1=a_b[:], op=ALU.mult), reads=[b_DT, b_ab], writes=[b_dd])
                psc, bpc = pp.get()
                S.op("dve", lambda e: e.tensor_copy(out=hl[:, 0, :], in_=dd[:, 0, :]), reads=[b_dd], writes=[b_hl])
                S.op("dve", lambda e: e.tensor_tensor(out=hl[:, 1, :], in0=dd[:, 0, :], in1=hl[:, 0, :], op=ALU.subtract), reads=[b_dd, b_hl], writes=[b_hl])
                S.op("dve", lambda e: e.tensor_copy(out=hlf[:], in_=hl[:]), reads=[b_hl], writes=[b_hlf])
                for pi in range(2):
                    S.op("pe", lambda e, pi=pi: e.matmul(psc[:, 0:16], lhsT=triub[:], rhs=hl[:, pi, :], start=(pi == 0), stop=(pi == 1)),
                         reads=[b_triub, b_hl], writes=[bpc])
                for pi in range(2):
                    S.op("pe", lambda e, pi=pi: e.matmul(psc[:, 16:32], lhsT=ones_b[:], rhs=hl[:, pi, :], start=(pi == 0), stop=(pi == 1)),
                         reads=[b_hl], writes=[bpc])
                S.op("act", lambda e: e.activation(out=csb[:], in_=psc[:, 0:32], func=AF.Copy), reads=[bpc], writes=[b_csb])
                S.op("act", lambda e: e.activation(out=dd[:, 1:3, :].rearrange("p a h -> p (a h)"), in_=csb[:], func=AF.Exp), reads=[b_csb], writes=[b_dd])
                S.op("dve", lambda e: e.tensor_tensor(out=dd[:, 3, :], in0=csb[:, 16:32], in1=csb[:, 0:16], op=ALU.subtract), reads=[b_csb], writes=[b_dd])
                S.op("act", lambda e: e.activation(out=dd[:, 4, :], in_=dd[:, 3, :], func=AF.Exp), reads=[b_dd], writes=[b_dd])

            def a2():
                S.op("dve", lambda e: e.tensor_tensor(out=xds[:], in0=xd[:], in1=bc3(dd[:, 4, :], [128, 16, 64]), op=ALU.mult),
                     reads=[b_xd, b_dd], writes=[b_xds])
                pcb, bpcb = pp.get()
                for g in range(2):
                    S.op("pe", lambda e, g=g: e.matmul(pcb[:, g * 128:(g + 1) * 128], lhsT=cvbc[:, g, cols], rhs=cvbc[:, 2 + g, cols], start=True, stop=True),
                         reads=[b_cvbc], writes=[bpcb])
                S.op("dve", lambda e: e.tensor_tensor(out=mcb[:], in0=pcb[:, 0:256].rearrange("p (g l) -> p g l", g=2),
                                                      in1=triu[:].unsqueeze(1).to_broadcast([128, 2, 128]), op=ALU.mult),
                     reads=[bpcb, b_triu], writes=[b_mcb])

            def a3():
                for pi in range(2):
                    S.op("dve", lambda e, pi=pi: e.tensor_tensor(out=LH[:, pi, :, :], in0=sutb[:].unsqueeze(1).to_broadcast([128, 16, 128]),
                                                                 in1=hlf[:, pi, :].unsqueeze(2).to_broadcast([128, 16, 128]), op=ALU.mult),
                         reads=[b_sutb, b_hlf], writes=[b_LH])

            def mk_seg(q4s):
                def u():
                    for q4 in q4s:
                        ps, bp = pp.get()
                        for hh in range(4):
                            h = q4 * 4 + hh
                            for pi in range(2):
                                S.op("pe", lambda e, ps=ps, h=h, hh=hh, pi=pi: e.matmul(ps[:, hh * 128:(hh + 1) * 128], lhsT=LH[:, pi, h, :], rhs=triub[:],
                                                                                       start=(pi == 0), stop=(pi == 1)),
                                     reads=[b_LH, b_triub], writes=[bp])
                        S.op("act", lambda e, ps=ps, q4=q4: e.activation(out=dec[:, q4 * 4:(q4 + 1) * 4, :].rearrange("p h l -> p (h l)"), in_=ps[:], func=AF.Exp),
                             reads=[bp], writes=[b_dec])
                return u

            def a6():
                for g in range(2):
                    S.op("dve", lambda e, g=g: e.tensor_tensor(out=Mm[:, g * 8:(g + 1) * 8, :], in0=dec[:, g * 8:(g + 1) * 8, :],
                                                               in1=mcb[:, g, :].unsqueeze(1).to_broadcast([128, 8, 128]), op=ALU.mult),
                         reads=[b_dec, b_mcb], writes=[b_Mm])

            return [a0, a1, a2, a3, mk_seg([0, 1]), mk_seg([2, 3]), a6]

        def units_B(c):
            sc, cc = c // 4, c % 4
            k = c % 2
            cvbc, b_cvbc = cvbc2[sc % 2], b_cvbc2[sc % 2]
            xs_tok, b_xst, xd, b_xd, xds, b_xds = xs_tok2[k], b_xst2[k], xd2[k], b_xd2[k], xds2[k], b_xds2[k]
            Btok, b_Btok, dd, b_dd, Mm, b_Mm, zs, b_zs = Btok2[k], b_Btok2[k], dd2[k], b_dd2[k], Mm2[k], b_Mm2[k], zs2[k], b_zs2[k]
            cols = slice(cc * 128, (cc + 1) * 128)
            tok = slice(c * 128, (c + 1) * 128)

            def b0():
                for g in range(2):
                    ps, bp = pp.get()
                    S.op("pe", lambda e, ps=ps, g=g: e.matmul(ps[:], lhsT=cvbc[:, 2 + g, cols], rhs=Hb[:, g * 8:(g + 1) * 8, :], start=True, stop=True),
                         reads=[b_cvbc, b_Hb], writes=[bp])
                    S.op("dve", lambda e, ps=ps, g=g: e.tensor_tensor(out=yo[:, g * 8:(g + 1) * 8, :], in0=ps[:].rearrange("p (h d) -> p h d", h=8),
                                                                      in1=bc3(dd[:, 1, g * 8:(g + 1) * 8], [128, 8, 64]), op=ALU.mult),
                         reads=[bp, b_dd], writes=[b_yo])

            def b1():
                sts = [pp.get(), pp.get()]
                for g in range(2):
                    ps, bp = sts[g]
                    S.op("pe", lambda e, ps=ps, g=g: e.matmul(ps[:], lhsT=Btok[:, g * 128:(g + 1) * 128], rhs=xds[:, g * 8:(g + 1) * 8, :], start=True, stop=True),
                         reads=[b_Btok, b_xds], writes=[bp])
                S.op("dve", lambda e: e.tensor_tensor(out=Ht[:], in0=H[:], in1=bc3(dd[:, 2, :], [128, 16, 64]), op=ALU.mult),
                     reads=[b_H, b_dd], writes=[b_Ht])
                for g in range(2):
                    ps, bp = sts[g]
                    S.op("dve", lambda e, ps=ps, g=g: e.tensor_tensor(out=H[:, g * 8:(g + 1) * 8, :], in0=ps[:].rearrange("p (h d) -> p h d", h=8),
                                                                      in1=Ht[:, g * 8:(g + 1) * 8, :], op=ALU.add),
                         reads=[bp, b_Ht], writes=[b_H])
                S.op("act", lambda e: e.activation(out=Hb[:], in_=H[:], func=AF.Copy), reads=[b_H], writes=[b_Hb])

            def b2():
                for hb in range(2):
                    ps, bp = pp.get()
                    for hh in range(8):
                        h = hb * 8 + hh
                        S.op("pe", lambda e, ps=ps, h=h, hh=hh: e.matmul(ps[:, hh * 64:(hh + 1) * 64], lhsT=Mm[:, h, :], rhs=xd[:, h, :], start=True, stop=True),
                             reads=[b_Mm, b_xd], writes=[bp])
                    hs = slice(hb * 8, (hb + 1) * 8)
                    S.op("dve", lambda e, ps=ps, hs=hs: e.tensor_tensor(out=y1[:, hs, :], in0=ps[:].rearrange("p (h d) -> p h d", h=8), in1=yo[:, hs, :], op=ALU.add),
                         reads=[bp, b_yo], writes=[b_y1])

            def b3():
                S.op(PE_, lambda e: e.tensor_tensor(out=y2[:], in0=xs_tok[:], in1=bc3(dsk[:], [128, 16, 64]), op=ALU.mult),
                     reads=[b_xst, b_dsk], writes=[b_y2])
                S.op(PE_, lambda e: e.tensor_tensor(out=y2[:], in0=y2[:], in1=y1[:], op=ALU.add), reads=[b_y2, b_y1], writes=[b_y2])
                S.op(PE_, lambda e: e.tensor_tensor(out=y2[:].rearrange("p h d -> p (h d)"), in0=y2[:].rearrange("p h d -> p (h d)"), in1=zs[:], op=ALU.mult),
                     reads=[b_y2, b_zs], writes=[b_y2])

            def b4():
                for g in range(2):
                    S.op("act", lambda e, g=g: e.activation(out=junk[:], in_=y2[:, g * 8:(g + 1) * 8, :].rearrange("p h d -> p (h d)"), func=AF.Square,
                                                            accum_out=nst[:, g:g + 1]), reads=[b_y2], writes=[b_junk, b_nst])
                S.op("dve", lambda e: e.tensor_scalar(out=nst[:, 2:4], in0=nst[:, 0:2], scalar1=1.0 / 512, scalar2=EPS, op0=ALU.mult, op1=ALU.add),
                     reads=[b_nst], writes=[b_nst])
                S.op("act", lambda e: e.activation(out=nst[:, 4:6], in_=nst[:, 2:4], func=AF.Ln), reads=[b_nst], writes=[b_nst])
                S.op("act", lambda e: e.activation(out=nst[:, 6:8], in_=nst[:, 4:6], func=AF.Exp, scale=-0.5), reads=[b_nst], writes=[b_nst])

            def b5():
                for g in range(2):
                    S.op("dve", lambda e, g=g: e.scalar_tensor_tensor(out=yn[:, g * 512:(g + 1) * 512], in0=y2[:, g * 8:(g + 1) * 8, :].rearrange("p h d -> p (h d)"),
                                                                      scalar=nst[:, 6 + g:7 + g], in1=sng[:, g * 512:(g + 1) * 512], op0=ALU.mult, op1=ALU.mult),
                         reads=[b_y2, b_nst, b_sng], writes=[b_yn])

            def b6():
                for ct in range(8):
                    S.op("pe", lambda e, ct=ct: e.transpose(out=tpY[:, ct * 128:(ct + 1) * 128], in_=yn[:, ct * 128:(ct + 1) * 128], identity=ident[:]),
                         reads=[b_yn], writes=[b_tpY])
                S.op("act", lambda e: e.activation(out=ynT[:].rearrange("p c t -> p (c t)"), in_=tpY[:], func=AF.Copy), reads=[b_tpY], writes=[b_ynT])
                S.dma("sp", YNv[:, :, tok], ynT[:], reads=[b_ynT])

            return [b0, b1, b2, b3, b4, b5, b6]

        def zip_emit(*streams):
            for i in range(max(len(u) for u in streams)):
                for u in streams:
                    if i < len(u):
                        u[i]()

        nch = nsc * 4
        conv_load(0)
        for u in conv_units(0):
            u()
        if nsc > 1:
            conv_load(1)
        zip_emit(units_A(0))
        for c in range(nch):
            sc, cc = c // 4, c % 4
            ua = units_A(c + 1) if c + 1 < nch else []
            uc = []
            if sc + 1 < nsc:
                cu = conv_units(sc + 1)
                uc = [lambda: None] * 2 + [cu[cc]]
                if cc == 3 and sc + 2 < nsc:
                    uc.append(lambda sc=sc: conv_load(sc + 2))
            if cc == 3 and sc + 1 < nsc:
                zip_emit(uc, units_B(c))
                zip_emit(ua)
            else:
                zip_emit(ua, units_B(c), uc)
        S.emit()


def phase_out(nc, S, l, x_src, X1, OG, YN, GLA, GLS, wa_d, ws_d, wo_d, GATE_b, nxt=None):
    with ExitStack() as st:
        if nxt is not None:
            nxt(st)
        def sb(name, shape, dt=F32):
            return st.enter_context(nc.sbuf_tensor(uname(name), list(shape), dt))
        wa = sb("wa", [64, 8, D], BF16); b_wa = [Buf() for _ in range(8)]
        ws = sb("ws", [128, 8, D], BF16); b_ws = [Buf() for _ in range(8)]
        wo = sb("wo", [128, 8, D], BF16); b_wo = [Buf() for _ in range(2)]
        wav = wa_d[l].rearrange("(h d) n -> d h n", d=64)
        wsv = ws_d[l].rearrange("(kc p) n -> p kc n", p=128)
        wov = wo_d[l].rearrange("(kc p) n -> p kc n", p=128)
        for nchunk in range(8):
            ns = slice(nchunk * 128, (nchunk + 1) * 128)
            S.dma("pool", wa[:, :, ns], wav[:, :, ns], writes=[b_wa[nchunk]])
            S.dma("pool", ws[:, :, ns], wsv[:, :, ns], writes=[b_ws[nchunk]])
        for nb in range(2):
            S.dma("pool", wo[:, :, nb * 512:(nb + 1) * 512], wov[:, :, nb * 512:(nb + 1) * 512], writes=[b_wo[nb]])
        pp = PsumPool(nc, st, [f"po{i}" for i in range(8 if nxt is None else 6)])
        ogt2 = [sb("ogt", [64, 8, 512], BF16) for _ in range(2)]; b_ogt2 = [Buf(), Buf()]
        ynt2 = [sb("ynt", [128, 8, 512], BF16) for _ in range(2)]; b_ynt2 = [Buf(), Buf()]
        gla = sb("gla", [128, 8, 512], BF16); b_gla = Buf()
        gls = sb("gls", [128, 8, 512], BF16); b_gls = Buf()
        ta = [sb("ta", [128, 512]) for _ in range(2)]; b_ta = [Buf(), Buf()]
        tb = [sb("tb", [128, 512]) for _ in range(2)]; b_tb = [Buf(), Buf()]
        mg = sb("mg", [128, 8, 512], BF16); b_mg = [Buf() for _ in range(8)]
        xt = [sb("xo", [128, D]) for _ in range(2)]; b_xt = [Buf(), Buf()]
        t3 = [sb("t3", [128, D]) for _ in range(2)]; b_t3 = [Buf(), Buf()]
        OGv = OG.rearrange("h d t -> d h t")
        YNv = YN.rearrange("(ct p) t -> p ct t", p=128)
        GLAv = GLA.rearrange("(ct p) t -> p ct t", p=128)
        GLSv = GLS.rearrange("(ct p) t -> p ct t", p=128)
        def load_blk(c):
            tok = slice(c * 512, (c + 1) * 512)
            j = c % 2
            S.dma("sp", ogt2[j][:], OGv[:, :, tok], writes=[b_ogt2[j]])
            S.dma("sp", ynt2[j][:], YNv[:, :, tok], writes=[b_ynt2[j]])

        def load_gates(c):
            tok = slice(c * 512, (c + 1) * 512)
            S.dma("sp", gla[:], GLAv[:, :, tok], writes=[b_gla])
            S.dma("sp", gls[:], GLSv[:, :, tok], writes=[b_gls])

        load_blk(0)
        load_gates(0)
        for c in range(8):
            tok = slice(c * 512, (c + 1) * 512)
            if c + 1 < 8:
                load_blk(c + 1)
            ogt, b_ogt, ynt, b_ynt = ogt2[c % 2], b_ogt2[c % 2], ynt2[c % 2], b_ynt2[c % 2]
            for nchunk in range(8):
                ns = slice(nchunk * 128, (nchunk + 1) * 128)
                a = nchunk % 2
                psa, bpa = pp.get()
                for h in range(8):
                    S.op("pe", lambda e, psa=psa, h=h, ns=ns, ogt=ogt: e.matmul(psa[:], lhsT=wa[:, h, ns], rhs=ogt[:, h, :], start=(h == 0), stop=(h == 7)),
                         reads=[b_wa[nchunk], b_ogt], writes=[bpa])
                pss, bps = pp.get()
                for kc in range(8):
                    S.op("pe", lambda e, pss=pss, kc=kc, ns=ns, ynt=ynt: e.matmul(pss[:], lhsT=ws[:, kc, ns], rhs=ynt[:, kc, :], start=(kc == 0), stop=(kc == 7)),
                         reads=[b_ws[nchunk], b_ynt], writes=[bps])
                S.op("dve", lambda e, psa=psa, a=a, nchunk=nchunk, gla=gla: e.tensor_tensor(out=ta[a][:], in0=psa[:], in1=gla[:, nchunk, :], op=ALU.mult),
                     reads=[bpa, b_gla], writes=[b_ta[a]])
                S.op("dve", lambda e, pss=pss, a=a, nchunk=nchunk, gls=gls: e.tensor_tensor(out=tb[a][:], in0=pss[:], in1=gls[:, nchunk, :], op=ALU.mult),
                     reads=[bps, b_gls], writes=[b_tb[a]])
                S.op("pool", lambda e, a=a, nchunk=nchunk: e.tensor_tensor(out=mg[:, nchunk, :], in0=ta[a][:], in1=tb[a][:], op=ALU.add),
                     reads=[b_ta[a], b_tb[a]], writes=[b_mg[nchunk]])
            if c + 1 < 8:
                load_gates(c + 1)
            for tq in range(4):
                k = tq % 2
                t128 = slice(c * 512 + tq * 128, c * 512 + (tq + 1) * 128)
                S.dma("sp", xt[k][:], x_src[t128, :], writes=[b_xt[k]])
                for nb in range(2):
                    ps, bp = pp.get()
                    for kc in range(8):
                        S.op("pe", lambda e, ps=ps, kc=kc, tq=tq, nb=nb: e.matmul(ps[:], lhsT=mg[:, kc, tq * 128:(tq + 1) * 128], rhs=wo[:, kc, nb * 512:(nb + 1) * 512],
                                                                                 start=(kc == 0), stop=(kc == 7)),
                             reads=b_mg + [b_wo[nb]], writes=[bp])
                    S.op("dve", lambda e, ps=ps, k=k, nb=nb: e.tensor_tensor(out=t3[k][:, nb * 512:(nb + 1) * 512], in0=ps[:], in1=GATE_b[:, nb * 512:(nb + 1) * 512], op=ALU.mult),
                         reads=[bp], writes=[b_t3[k]])
                S.op("pool", lambda e, k=k: e.tensor_tensor(out=t3[k][:], in0=t3[k][:], in1=xt[k][:], op=ALU.add), reads=[b_t3[k], b_xt[k]], writes=[b_t3[k]])
                S.dma("sp", X1[t128, :], t3[k][:], reads=[b_t3[k]])
        S.emit()


def phase_final(nc, S, x_src, fg_d, out_d):
    with ExitStack() as st:
        def sb(name, shape, dt=F32):
            return st.enter_context(nc.sbuf_tensor(uname(name), list(shape), dt))
        fg = sb("fg", [128, D]); b_fg = Buf()
        S.dma("sp", fg[:], fg_d.partition_broadcast(128), writes=[b_fg])
        NF = 4
        xt = [sb("xf", [128, D]) for _ in range(NF)]; b_xt = [Buf() for _ in range(NF)]
        junk = sb("junkf", [128, D], BF16); b_junk = Buf()
        ss = [sb("ssf", [128, 4]) for _ in range(NF)]; b_ss = [Buf() for _ in range(NF)]
        yo = [sb("yof", [128, D]) for _ in range(NF)]; b_yo = [Buf() for _ in range(NF)]
        for tt in range(NT):
            k = tt % NF
            tok = slice(tt * 128, (tt + 1) * 128)
            S.dma("sp", xt[k][:], x_src[tok, :], writes=[b_xt[k]])
            S.op("act", lambda e, k=k: e.activation(out=junk[:], in_=xt[k][:], func=AF.Square, accum_out=ss[k][:, 0:1]),
                 reads=[b_xt[k]], writes=[b_junk, b_ss[k]])
            S.op("dve", lambda e, k=k: e.tensor_scalar(out=ss[k][:, 1:2], in0=ss[k][:, 0:1], scalar1=1.0 / D, scalar2=EPS, op0=ALU.mult, op1=ALU.add),
                 reads=[b_ss[k]], writes=[b_ss[k]])
            S.op("act", lambda e, k=k: e.activation(out=ss[k][:, 2:3], in_=ss[k][:, 1:2], func=AF.Ln), reads=[b_ss[k]], writes=[b_ss[k]])
            S.op("act", lambda e, k=k: e.activation(out=ss[k][:, 3:4], in_=ss[k][:, 2:3], func=AF.Exp, scale=-0.5), reads=[b_ss[k]], writes=[b_ss[k]])
            S.op("dve", lambda e, k=k: e.scalar_tensor_tensor(out=yo[k][:], in0=xt[k][:], scalar=ss[k][:, 3:4], in1=fg[:], op0=ALU.mult, op1=ALU.mult),
                 reads=[b_xt[k], b_ss[k], b_fg], writes=[b_yo[k]])
            S.dma("sp", out_d[tok, :], yo[k][:], reads=[b_yo[k]])
        S.emit()


def host_consts():
    inv = (10000.0 ** (-np.arange(0, 64, 2, dtype=np.float32) / 64.0)).astype(np.float32)
    ang = np.arange(L, dtype=np.float32)[:, None] * inv[None, :]
    ang = np.concatenate([ang, ang], axis=-1)
    cos = np.cos(ang).astype(np.float32).T
    sin = np.sin(ang).astype(np.float32).T
    sgn = np.where(np.arange(64) < 32, -1.0, 1.0).astype(np.float32)[:, None]
    cosT = np.concatenate([cos, cos], axis=0)
    sinT = np.concatenate([sin * sgn, sin * sgn], axis=0)
    r = np.arange(128)
    ident = np.eye(128, dtype=np.float32)
    cbias = np.where(r[None, :] <= r[:, None], 0.0, -1e30).astype(np.float32)
    triu = (r[:, None] <= r[None, :]).astype(np.float32)
    sut = (r[:, None] > r[None, :]).astype(np.float32)
    pow2 = np.tile(((65.0 / 64.0) * 2.0 ** (-np.arange(NBIS + 2, dtype=np.float64))).astype(np.float32)[None, :], (128, 1))
    return dict(cosT=np.ascontiguousarray(cosT), sinT=np.ascontiguousarray(sinT), ident=ident, cbias=cbias,
                triu=triu, sut=sut, pow2=pow2)


def make_in_maps(inputs, n_cores=8):
    f = lambda a: np.ascontiguousarray(np.asarray(a, dtype=np.float32))
    shared = dict(
        w_ada=f(inputs["w_ada"]), b_ada=f(inputs["b_ada"]).reshape(DEPTH, 1, 3 * D),
        norm_g=f(inputs["norm_g"]).reshape(DEPTH, 1, D), w_in=f(inputs["w_in"]),
        conv_wT=f(np.asarray(inputs["conv_w"]).reshape(DEPTH, 4, 12, 128).transpose(0, 3, 2, 1)),
        conv_bT=f(np.asarray(inputs["conv_b"]).reshape(DEPTH, 12, 128).transpose(0, 2, 1)),
        dt_bias=f(inputs["dt_bias"]).reshape(DEPTH, 1, 16), a_log=f(inputs["a_log"]).reshape(DEPTH, 1, 16),
        d_skip=f(inputs["d_skip"]).reshape(DEPTH, 1, 16), ssm_norm_g=f(inputs["ssm_norm_g"]).reshape(DEPTH, 1, D),
        w_branch_a=f(inputs["w_branch_a"]), w_branch_s=f(inputs["w_branch_s"]), w_out=f(inputs["w_out"]),
        final_g=f(inputs["final_g"]).reshape(1, D),
    )
    shared.update(host_consts())
    maps = []
    for b in range(n_cores):
        m = dict(shared)
        m["x"] = f(inputs["x"][b])
        m["cT"] = f(np.asarray(inputs["c"][b]).reshape(8, 128).T)
        maps.append(m)
    return maps


def kernel(**inputs):
    nc = build_program()
    maps = make_in_maps(inputs)
    res = run_bass_kernel_spmd(nc, maps, core_ids=list(range(8)))
    return np.stack([np.asarray(r["out"], dtype=np.float32) for r in res.results], axis=0)
```

```python
import numpy as np
from contextlib import ExitStack
import concourse.bass as bass
import concourse.mybir as mybir
from concourse.bass_utils import run_bass_kernel_spmd

F32 = mybir.dt.float32
BF16 = mybir.dt.bfloat16
AF = mybir.ActivationFunctionType
ALU = mybir.AluOpType
AX = mybir.AxisListType

L = 4096
D = 1024
NT = L // 128
DEPTH = 4
NIN = 6100
Q0, K0, V0, GA0, QI0, KI0, WI0, Z0, XBC0, DT0, GLA0, GLS0 = 0, 512, 576, 640, 1152, 1408, 1472, 1476, 2500, 4036, 4052, 5076
EPS = 1e-6
NBIS = 20
NDMASEM = 12


class Buf:
    __slots__ = ("name", "lw", "rd", "excl")

    def __init__(self, name="", excl=False):
        self.name = name
        self.lw = None
        self.rd = []
        self.excl = excl


class Sched:
    ENGS = ("sp", "act", "pe", "dve", "pool")
    DQ = ("sp", "act", "pool")

    def __init__(self, nc, stack):
        self.nc = nc
        self.batch = 0
        self.ops = {e: [] for e in self.ENGS}
        self.waited = {e: {} for e in self.ENGS}
        self.base = {e: 0 for e in self.ENGS}
        self.esem = {e: stack.enter_context(nc.semaphore("es_" + e)) for e in self.ENGS}
        self.dsem = {}
        self.dcnt = {}
        self.dval = {}
        for q in self.DQ:
            self.dsem[q] = [stack.enter_context(nc.semaphore(f"ds_{q}{i}")) for i in range(NDMASEM)]
            self.dcnt[q] = 0
            self.dval[q] = [0] * NDMASEM
        self.nops = 0

    def _need(self, eng, ev, waits, same_ok=False):
        if ev is None or ev[1] != self.batch:
            return
        if ev[0] == "e":
            _, _, e2, i2 = ev
            if e2 == eng and same_ok:
                return
            self.ops[e2][i2]["inc"] = True
            if self.waited[eng].get(e2, -1) >= i2:
                return
            self.waited[eng][e2] = i2
            waits.append(ev)
        else:
            _, _, q, si, val = ev
            key = (q, si)
            if self.waited[eng].get(key, -1) >= val:
                return
            self.waited[eng][key] = val
            waits.append(ev)

    def _deps(self, eng, reads, writes, waits):
        for r in reads:
            self._need(eng, r.lw, waits, same_ok=(eng == "pe"))
        for w in writes:
            self._need(eng, w.lw, waits, same_ok=(eng == "pe"))
            for ev in w.rd:
                self._need(eng, ev, waits, same_ok=(eng == "pe"))

    def op(self, eng, fn, reads=(), writes=()):
        ex = [r for r in reads if r.excl and r not in writes]
        if ex:
            writes = list(writes) + ex
        waits = []
        self._deps(eng, reads, writes, waits)
        idx = len(self.ops[eng])
        self.ops[eng].append({"fn": fn, "waits": waits, "inc": False, "dma": None})
        ev = ("e", self.batch, eng, idx)
        for r in reads:
            r.rd.append(ev)
        for w in writes:
            w.lw = ev
            w.rd = []
        self.nops += 1
        return ev

    def dma(self, q, out, in_, reads=(), writes=(), **kw):
        waits = []
        self._deps(q, reads, writes, waits)
        n = self.dcnt[q]
        si = n % NDMASEM
        self.dcnt[q] += 1
        prev = self.dval[q][si]
        if prev > 0:
            self._need(q, ("d", self.batch, q, si, prev), waits)
        val = prev + 16
        self.dval[q][si] = val
        ev = ("d", self.batch, q, si, val)
        self.ops[q].append({"fn": (lambda e: e.dma_start(out=out, in_=in_, **kw)), "waits": waits,
                            "inc": False, "dma": (self.dsem[q][si], 16)})
        for r in reads:
            r.rd.append(ev)
        for w in writes:
            w.lw = ev
            w.rd = []
        self.nops += 1
        return ev

    def emit(self, final=False):
        for e in self.ENGS:
            for o in reversed(self.ops[e]):
                if o["fn"] is not None and o["dma"] is None:
                    o["inc"] = True
                    break
            c = self.base[e]
            for o in self.ops[e]:
                if o["inc"]:
                    c += 1
                o["cnt"] = c
        newbase = {e: (self.ops[e][-1]["cnt"] if self.ops[e] else self.base[e]) for e in self.ENGS}
        dvals = {q: list(self.dval[q]) for q in self.DQ}

        def body(engine, ename):
            for o in self.ops[ename]:
                for ev in o["waits"]:
                    if ev[0] == "e":
                        engine.wait_ge(self.esem[ev[2]], self.ops[ev[2]][ev[3]]["cnt"])
                    else:
                        engine.wait_ge(self.dsem[ev[2]][ev[3]], ev[4])
                if o["fn"] is None:
                    continue
                ins = o["fn"](engine)
                if o["dma"] is not None:
                    ins.then_inc(o["dma"][0], o["dma"][1])
                elif o["inc"]:
                    ins.then_inc(self.esem[ename], 1)
            for e2 in self.ENGS:
                if e2 != ename and newbase[e2] > 0:
                    engine.wait_ge(self.esem[e2], newbase[e2])
            for q in self.DQ:
                for si in range(NDMASEM):
                    if dvals[q][si] > 0:
                        engine.wait_ge(self.dsem[q][si], dvals[q][si])

        with self.nc.Block() as block:
            @block.sync
            def _(eng):
                body(eng, "sp")

            @block.scalar
            def _(eng):
                body(eng, "act")

            @block.tensor
            def _(eng):
                body(eng, "pe")

            @block.vector
            def _(eng):
                body(eng, "dve")

            @block.gpsimd
            def _(eng):
                body(eng, "pool")

        self.base = newbase
        self.batch += 1
        self.ops = {e: [] for e in self.ENGS}
        self.waited = {e: {} for e in self.ENGS}


_UID = [0]


def uname(n):
    _UID[0] += 1
    return f"{n}_u{_UID[0]}"


class PsumPool:
    def __init__(self, nc, stack, names):
        self.banks = [stack.enter_context(nc.psum_tensor(uname(n), [128, 512], F32)) for n in names]
        self.bufs = [Buf(n, excl=True) for n in names]
        self.i = 0

    def get(self):
        k = self.i % len(self.banks)
        self.i += 1
        return self.banks[k], self.bufs[k]


def build_program(n_layers=DEPTH, dbg=None):
    nc = bass.Bass("TRN2", target_bir_lowering=False)

    def din(name, shape, dt=F32):
        return nc.dram_tensor(name, list(shape), dt, kind="ExternalInput").ap()

    def dscr(name, shape, dt):
        kind = "ExternalOutput" if (dbg and name in dbg) else "Internal"
        return nc.dram_tensor(name, list(shape), dt, kind=kind).ap()

    x_in = din("x", [L, D])
    cT_d = din("cT", [128, 8])
    w_ada_d = din("w_ada", [DEPTH, D, 3 * D])
    b_ada_d = din("b_ada", [DEPTH, 1, 3 * D])
    norm_g_d = din("norm_g", [DEPTH, 1, D])
    w_in_d = din("w_in", [DEPTH, D, NIN])
    convw_d = din("conv_wT", [DEPTH, 128, 12, 4])
    convb_d = din("conv_bT", [DEPTH, 128, 12])
    dtb_d = din("dt_bias", [DEPTH, 1, 16])
    alog_d = din("a_log", [DEPTH, 1, 16])
    dskip_d = din("d_skip", [DEPTH, 1, 16])
    sng_d = din("ssm_norm_g", [DEPTH, 1, D])
    wa_d = din("w_branch_a", [DEPTH, 512, D])
    ws_d = din("w_branch_s", [DEPTH, D, D])
    wo_d = din("w_out", [DEPTH, D, D])
    fg_d = din("final_g", [1, D])
    cosT_d = din("cosT", [128, L])
    sinT_d = din("sinT", [128, L])
    ident_d = din("ident", [128, 128])
    cbias_d = din("cbias", [128, 128])
    triu_d = din("triu", [128, 128])
    sut_d = din("sut", [128, 128])
    pow2_d = din("pow2", [128, NBIS + 2])
    out_d = nc.dram_tensor("out", [L, D], F32, kind="ExternalOutput").ap()

    X1 = dscr("X1", [L, D], F32)
    QT = dscr("QT", [8, 64, L], BF16)
    KT = dscr("KT", [64, L], BF16)
    QiT = dscr("QiT", [4, 64, L], BF16)
    KiT = dscr("KiT", [64, L], BF16)
    GA = dscr("GA", [8, 64, L], BF16)
    XBC = dscr("XBC", [1536, L], F32)
    GLA = dscr("GLA", [D, L], BF16)
    GLS = dscr("GLS", [D, L], BF16)
    VA = dscr("VA", [L, 65], BF16)
    WI = dscr("WI", [L, 4], F32)
    ZS = dscr("ZS", [L, D], BF16)
    DTs = dscr("DT", [L, 16], F32)
    OG = dscr("OG", [8, 64, L], BF16)
    YN = dscr("YN", [D, L], BF16)
    DBG = dscr("DBG", [128, 4096], F32)

    with ExitStack() as top:
        S = Sched(nc, top)

        def sbt(stack, name, shape, dt):
            return stack.enter_context(nc.sbuf_tensor(uname(name), list(shape), dt))

        ident = sbt(top, "ident", [128, 128], BF16); b_ident = Buf()
        identf = sbt(top, "identf", [128, 128], F32); b_identf = Buf()
        ones_f = sbt(top, "ones_f", [128, 128], F32); b_ones = Buf()
        ones_b = sbt(top, "ones_b", [128, 128], BF16); b_onesb = Buf()
        S.dma("pool", ident[:], ident_d, writes=[b_ident])
        S.dma("sp", identf[:], ident_d, writes=[b_identf])
        S.op("dve", lambda e: e.memset(ones_f[:], 1.0), writes=[b_ones])
        S.op("dve", lambda e: e.memset(ones_b[:], 1.0), writes=[b_onesb])
        S.emit()

        mods = [(sbt(top, "G_b", [128, D], F32), sbt(top, "SH_b", [128, D], F32), sbt(top, "GATE_b", [128, D], F32)) for _ in range(2)]
        for l in range(n_layers):
            x_src = x_in if l == 0 else X1
            with ExitStack() as lay:
                G_b, SH_b, GATE_b = mods[l % 2]
                if l == 0 or (dbg and "no_hoist" in dbg):
                    phase_adaln(nc, S, lay, l, cT_d, w_ada_d, b_ada_d, norm_g_d, G_b, SH_b, GATE_b, ones_f)
                if dbg and "stop_adaln" in dbg:
                    dump(nc, S, DBG, [G_b, SH_b, GATE_b])
                    break
                phase_proj(nc, S, l, x_src, w_in_d, dtb_d, cosT_d, sinT_d, G_b, SH_b, ident,
                           dict(QT=QT, KT=KT, QiT=QiT, KiT=KiT, GA=GA, XBC=XBC, GLA=GLA, GLS=GLS, VA=VA, WI=WI,
                                ZS=ZS, DT=DTs))
                if dbg and "stop_proj" in dbg:
                    break
                if not (dbg and "skip_attn" in dbg):
                    phase_attn(nc, S, l, QT, KT, QiT, KiT, GA, VA, WI, OG, cbias_d, pow2_d, ident_d, ones_f, ident, DBG, dbg)
                if dbg and "stop_attn" in dbg:
                    break
                phase_ssd(nc, S, l, XBC, ZS, DTs, YN, convw_d, convb_d, alog_d, dskip_d, sng_d, triu_d, sut_d,
                          ident, identf, ones_f, ones_b, DBG, dbg)
                if dbg and "stop_ssd" in dbg:
                    break
                nxt = None
                if l + 1 < n_layers and not (dbg and ("no_hoist" in dbg or "stop_layer" in dbg)):
                    Gn, SHn, GATEn = mods[(l + 1) % 2]
                    nxt = lambda stk, l=l, Gn=Gn, SHn=SHn, GATEn=GATEn: phase_adaln(nc, S, None, l + 1, cT_d, w_ada_d, b_ada_d, norm_g_d,
                                                                                   Gn, SHn, GATEn, ones_f, ext_stack=stk)
                phase_out(nc, S, l, x_src, X1, OG, YN, GLA, GLS, wa_d, ws_d, wo_d, GATE_b, nxt)
                if dbg and "stop_layer" in dbg:
                    break
        else:
            phase_final(nc, S, X1 if n_layers > 0 else x_in, fg_d, out_d)
    return nc


def dump(nc, S, DBG, tiles):
    off = 0
    for t in tiles:
        w = t.shape[1]
        S.dma("sp", DBG[0:t.shape[0], off:off + w], t[:])
        off += w
    S.emit()


def phase_adaln(nc, S, lay, l, cT_d, w_ada_d, b_ada_d, norm_g_d, G_b, SH_b, GATE_b, ones_f, ext_stack=None):
    with ExitStack() as st_own:
        st = ext_stack if ext_stack is not None else st_own
        def sb(name, shape, dt=F32):
            return st.enter_context(nc.sbuf_tensor(uname(name), list(shape), dt))
        cT = sb("cT", [128, 8]); b_cT = Buf()
        sc = sb("sc", [128, 8]); b_sc = Buf()
        sg = sb("sg", [128, 8]); b_sg = Buf()
        modrow = sb("modrow", [1, 3 * D]); b_mod = Buf()
        bada = sb("bada", [1, 3 * D]); b_bada = Buf()
        ng = sb("ng", [1, D]); b_ng = Buf()
        grow = sb("grow", [1, D]); b_grow = Buf()
        nwb = 2 if ext_stack is None else 1
        wts = [sb(f"wada{i}", [128, 8, 512]) for i in range(nwb)]
        b_wts = [Buf() for _ in range(nwb)]
        pp = PsumPool(nc, st, ["pa0", "pa1", "pa2", "pa3"] if ext_stack is None else ["pa0", "pa1"])

        S.dma("sp", cT[:], cT_d, writes=[b_cT])
        S.dma("sp", bada[:], b_ada_d[l], writes=[b_bada])
        S.dma("sp", ng[:], norm_g_d[l], writes=[b_ng])
        S.op("act", lambda e: e.activation(out=sg[:], in_=cT[:], func=AF.Sigmoid), reads=[b_cT], writes=[b_sg])
        S.op("dve", lambda e: e.tensor_tensor(out=sc[:], in0=cT[:], in1=sg[:], op=ALU.mult), reads=[b_cT, b_sg], writes=[b_sc])
        wv = w_ada_d[l].rearrange("(kc p) n -> p kc n", p=128)
        for nb in range(6):
            wt, bw = wts[nb % nwb], b_wts[nb % nwb]
            S.dma("sp", wt[:], wv[:, :, nb * 512:(nb + 1) * 512], writes=[bw])
            ps, bp = pp.get()
            for kc in range(8):
                S.op("pe", lambda e, kc=kc, wt=wt, ps=ps: e.matmul(ps[0:1, :], lhsT=sc[:, kc:kc + 1], rhs=wt[:, kc, :],
                                                                  start=(kc == 0), stop=(kc == 7)),
                     reads=[b_sc, bw], writes=[bp])
            S.op("dve", lambda e, nb=nb, ps=ps: e.tensor_tensor(out=modrow[:, nb * 512:(nb + 1) * 512], in0=ps[0:1, :],
                                                                in1=bada[:, nb * 512:(nb + 1) * 512], op=ALU.add),
                 reads=[bp, b_bada], writes=[b_mod])
        S.op("dve", lambda e: e.scalar_tensor_tensor(out=grow[:], in0=modrow[:, D:2 * D], scalar=1.0, in1=ng[:],
                                                     op0=ALU.add, op1=ALU.mult),
             reads=[b_mod, b_ng], writes=[b_grow])
        b_dst = Buf()
        for (row, bsrc, dst) in ((grow[:, :], b_grow, G_b), (modrow[:, 0:D], b_mod, SH_b), (modrow[:, 2 * D:3 * D], b_mod, GATE_b)):
            for hb in range(2):
                ps, bp = pp.get()
                S.op("pe", lambda e, ps=ps, row=row, hb=hb: e.matmul(ps[:], lhsT=ones_f[0:1, :], rhs=row[:, hb * 512:(hb + 1) * 512],
                                                                     start=True, stop=True),
                     reads=[bsrc], writes=[bp])
                S.op("act", lambda e, ps=ps, dst=dst, hb=hb: e.activation(out=dst[:, hb * 512:(hb + 1) * 512], in_=ps[:], func=AF.Copy),
                     reads=[bp], writes=[b_dst])
        if ext_stack is None:
            S.emit()


def phase_proj(nc, S, l, x_src, w_in_d, dtb_d, cosT_d, sinT_d, G_b, SH_b, ident, dst):
    with ExitStack() as st:
        def sb(name, shape, dt=F32):
            return st.enter_context(nc.sbuf_tensor(uname(name), list(shape), dt))
        hT = sb("hT", [128, 8, L], BF16)
        b_hT = [Buf() for _ in range(NT)]
        cosT = sb("cosT", [128, L], BF16); b_cos = Buf()
        sinT = sb("sinT", [128, L], BF16); b_sin = Buf()
        S.dma("pool", cosT[:], cosT_d, writes=[b_cos])
        S.dma("pool", sinT[:], sinT_d, writes=[b_sin])
        tps = [st.enter_context(nc.psum_tensor(uname("tp"), [128, 1024], BF16)) for _ in range(2)]
        b_tps = [Buf(excl=True), Buf(excl=True)]
        pp = PsumPool(nc, st, [f"pj{i}" for i in range(6)])
        NB = 4
        xt = [sb("xt", [128, D]) for _ in range(NB)]; b_xt = [Buf() for _ in range(NB)]
        junk = sb("junk", [128, D], BF16); b_junk = Buf()
        ss = [sb("ss", [128, 4]) for _ in range(NB)]; b_ss = [Buf() for _ in range(NB)]
        h1 = [sb("h1", [128, D]) for _ in range(NB)]; b_h1 = [Buf() for _ in range(NB)]
        hb = [sb("hb", [128, D], BF16) for _ in range(NB)]; b_hb = [Buf() for _ in range(NB)]
        def stage_x(tt):
            k = tt % NB
            tok = slice(tt * 128, (tt + 1) * 128)
            S.dma("sp", xt[k][:], x_src[tok, :], writes=[b_xt[k]])
            S.op("act", lambda e: e.activation(out=junk[:], in_=xt[k][:], func=AF.Square, accum_out=ss[k][:, 0:1]),
                 reads=[b_xt[k]], writes=[b_junk, b_ss[k]])
            S.op("dve", lambda e: e.tensor_scalar(out=ss[k][:, 1:2], in0=ss[k][:, 0:1], scalar1=1.0 / D, scalar2=EPS,
                                                  op0=ALU.mult, op1=ALU.add), reads=[b_ss[k]], writes=[b_ss[k]])
            S.op("act", lambda e: e.activation(out=ss[k][:, 2:3], in_=ss[k][:, 1:2], func=AF.Ln), reads=[b_ss[k]], writes=[b_ss[k]])
            S.op("act", lambda e: e.activation(out=ss[k][:, 3:4], in_=ss[k][:, 2:3], func=AF.Exp, scale=-0.5), reads=[b_ss[k]], writes=[b_ss[k]])
            S.op("dve", lambda e: e.scalar_tensor_tensor(out=h1[k][:], in0=xt[k][:], scalar=ss[k][:, 3:4], in1=G_b[:],
                                                         op0=ALU.mult, op1=ALU.mult),
                 reads=[b_xt[k], b_ss[k]], writes=[b_h1[k]])
            S.op("pool", lambda e: e.tensor_tensor(out=hb[k][:], in0=h1[k][:], in1=SH_b[:], op=ALU.add),
                 reads=[b_h1[k]], writes=[b_hb[k]])

        def stage_y(tt):
            k = tt % NB
            tok = slice(tt * 128, (tt + 1) * 128)
            tp, btp = tps[tt % 2], b_tps[tt % 2]
            for kc in range(8):
                S.op("pe", lambda e, kc=kc: e.transpose(out=tp[:, kc * 128:(kc + 1) * 128],
                                                        in_=hb[k][:, kc * 128:(kc + 1) * 128], identity=ident[:]),
                     reads=[b_hb[k]], writes=[btp])
            S.op("act", lambda e: e.activation(out=hT[:, :, tok], in_=tp[:].rearrange("p (k t) -> p k t", k=8), func=AF.Copy),
                 reads=[btp], writes=[b_hT[tt]])

        stage_x(0)
        stage_x(1)
        for tt in range(NT):
            if tt + 2 < NT:
                stage_x(tt + 2)
            stage_y(tt)

        wv = w_in_d[l].rearrange("(kc p) n -> p kc n", p=128)
        tiles = []
        for t in range(4):
            tiles.append(dict(cols=[(Q0 + 128 * t, 64), (Q0 + 128 * t + 64, 64)], rope=True, kind="rope",
                              dst=[dst["QT"][2 * t], dst["QT"][2 * t + 1]]))
        for t in range(2):
            tiles.append(dict(cols=[(QI0 + 128 * t, 64), (QI0 + 128 * t + 64, 64)], rope=True, kind="rope",
                              dst=[dst["QiT"][2 * t], dst["QiT"][2 * t + 1]]))
        tiles.append(dict(cols=[(K0, 64), (KI0, 64)], rope=True, kind="rope", dst=[dst["KT"], dst["KiT"]]))
        for t in range(4):
            tiles.append(dict(cols=[(GA0 + 128 * t, 128)], rope=False, kind="silu",
                              dst=[dst["GA"][2 * t], dst["GA"][2 * t + 1]]))
        for t in range(12):
            tiles.append(dict(cols=[(XBC0 + 128 * t, 128)], rope=False, kind="copy", dst=[dst["XBC"][t * 128:(t + 1) * 128]]))
        for t in range(8):
            tiles.append(dict(cols=[(GLA0 + 128 * t, 128)], rope=False, kind="sig", dst=[dst["GLA"][t * 128:(t + 1) * 128]]))
        for t in range(8):
            tiles.append(dict(cols=[(GLS0 + 128 * t, 128)], rope=False, kind="sig", dst=[dst["GLS"][t * 128:(t + 1) * 128]]))
        wts = [sb("wt", [128, 8, 128], BF16) for _ in range(2)]
        wtps = [sb("wtp", [128, 8, 128], BF16) for _ in range(2)]
        b_w = [[Buf() for _ in range(2)] for _ in range(2)]
        b_wp = [[Buf() for _ in range(4)] for _ in range(2)]
        NO = 3
        t1 = [sb("t1", [128, 512]) for _ in range(2)]; b_t1 = [Buf(), Buf()]
        t2 = [sb("t2", [128, 512]) for _ in range(2)]; b_t2 = [Buf(), Buf()]
        ob = [sb("ob", [128, 512], BF16) for _ in range(NO)]; b_ob = [Buf() for _ in range(NO)]
        of = [sb("of", [128, 512]) for _ in range(NO)]; b_of = [Buf() for _ in range(NO)]
        cnt = 0
        def load_w(ti):
            T = tiles[ti]
            w = ti % 2
            wt, wtp = wts[w], wtps[w]
            o = 0
            rb = []
            for ci, (c0, n) in enumerate(T["cols"]):
                S.dma("pool", wt[:, :, o:o + n], wv[:, :, c0:c0 + n], writes=[b_w[w][ci]])
                rb.append(b_w[w][ci])
                o += n
            rbp = []
            if T["rope"]:
                o = 0
                for ci, (c0, n) in enumerate(T["cols"]):
                    S.dma("pool", wtp[:, :, o:o + 32], wv[:, :, c0 + 32:c0 + 64], writes=[b_wp[w][2 * ci]])
                    S.dma("pool", wtp[:, :, o + 32:o + 64], wv[:, :, c0:c0 + 32], writes=[b_wp[w][2 * ci + 1]])
                    rbp += [b_wp[w][2 * ci], b_wp[w][2 * ci + 1]]
                    o += 64
            return rb, rbp

        nxt_w = load_w(0)
        for ti, T in enumerate(tiles):
            w = ti % 2
            wt, wtp = wts[w], wtps[w]
            rb, rbp = nxt_w
            if ti + 1 < len(tiles):
                nxt_w = load_w(ti + 1)
            for c in range(8):
                tok = slice(c * 512, (c + 1) * 512)
                hbufs = b_hT[4 * c:4 * c + 4]
                ps, bp = pp.get()
                for kc in range(8):
                    S.op("pe", lambda e, ps=ps, wt=wt, kc=kc, tok=tok: e.matmul(ps[:], lhsT=wt[:, kc, :], rhs=hT[:, kc, tok],
                                                                               start=(kc == 0), stop=(kc == 7)),
                         reads=rb + hbufs, writes=[bp])
                kind = T["kind"]
                if kind == "rope":
                    psp, bpp = pp.get()
                    for kc in range(8):
                        S.op("pe", lambda e, psp=psp, wtp=wtp, kc=kc, tok=tok: e.matmul(psp[:], lhsT=wtp[:, kc, :], rhs=hT[:, kc, tok],
                                                                                     start=(kc == 0), stop=(kc == 7)),
                             reads=rbp + hbufs, writes=[bpp])
                    a = cnt % 2
                    k = cnt % NO
                    S.op("dve", lambda e, a=a, ps=ps, tok=tok: e.tensor_tensor(out=t1[a][:], in0=ps[:], in1=cosT[:, tok], op=ALU.mult),
                         reads=[bp, b_cos], writes=[b_t1[a]])
                    S.op("dve", lambda e, a=a, psp=psp, tok=tok: e.tensor_tensor(out=t2[a][:], in0=psp[:], in1=sinT[:, tok], op=ALU.mult),
                         reads=[bpp, b_sin], writes=[b_t2[a]])
                    S.op("pool", lambda e, a=a, k=k: e.tensor_tensor(out=ob[k][:], in0=t1[a][:], in1=t2[a][:], op=ALU.add),
                         reads=[b_t1[a], b_t2[a]], writes=[b_ob[k]])
                    S.dma("sp", T["dst"][0][:, tok], ob[k][0:64, :], reads=[b_ob[k]])
                    S.dma("sp", T["dst"][1][:, tok], ob[k][64:128, :], reads=[b_ob[k]])
                elif kind == "copy":
                    k = cnt % NO
                    S.op("act", lambda e, k=k, ps=ps: e.activation(out=of[k][:], in_=ps[:], func=AF.Copy), reads=[bp], writes=[b_of[k]])
                    S.dma("sp", T["dst"][0][:, tok], of[k][:], reads=[b_of[k]])
                else:
                    k = cnt % NO
                    fn = AF.Silu if kind == "silu" else AF.Sigmoid
                    S.op("act", lambda e, k=k, ps=ps, fn=fn: e.activation(out=ob[k][:], in_=ps[:], func=fn), reads=[bp], writes=[b_ob[k]])
                    if len(T["dst"]) == 2:
                        S.dma("sp", T["dst"][0][:, tok], ob[k][0:64, :], reads=[b_ob[k]])
                        S.dma("sp", T["dst"][1][:, tok], ob[k][64:128, :], reads=[b_ob[k]])
                    else:
                        S.dma("sp", T["dst"][0][:, tok], ob[k][:], reads=[b_ob[k]])
                cnt += 1
        wz = sb("wz", [128, 8, 1024], BF16); b_wz = Buf()
        S.dma("pool", wz[:], wv[:, :, Z0:Z0 + 1024], writes=[b_wz])
        wsm = sb("wsm", [128, 8, 84], BF16); b_wsm = [Buf() for _ in range(3)]
        S.dma("pool", wsm[:, :, 0:64], wv[:, :, V0:V0 + 64], writes=[b_wsm[0]])
        S.dma("pool", wsm[:, :, 64:68], wv[:, :, WI0:WI0 + 4], writes=[b_wsm[1]])
        S.dma("pool", wsm[:, :, 68:84], wv[:, :, DT0:DT0 + 16], writes=[b_wsm[2]])
        dtb = sb("dtb", [128, 16]); b_dtb = Buf()
        S.dma("sp", dtb[:], dtb_d[l].partition_broadcast(128), writes=[b_dtb])
        zb = [sb("zb", [128, D], BF16) for _ in range(2)]; b_zb = [Buf(), Buf()]
        va = [sb("va", [128, 65], BF16) for _ in range(2)]; b_va = [Buf(), Buf()]
        wib = [sb("wib", [128, 4]) for _ in range(2)]; b_wib = [Buf(), Buf()]
        wiall = sb("wiall", [128, NT, 4]); b_wiall = Buf()
        dtall = sb("dtall", [128, 2, NT, 16]); b_dtall = Buf()
        for tt in range(NT):
            k = tt % 2
            tok = slice(tt * 128, (tt + 1) * 128)
            for nb in range(2):
                ps, bp = pp.get()
                for kc in range(8):
                    S.op("pe", lambda e, ps=ps, kc=kc, tok=tok, nb=nb: e.matmul(ps[:], lhsT=hT[:, kc, tok], rhs=wz[:, kc, nb * 512:(nb + 1) * 512],
                                                                               start=(kc == 0), stop=(kc == 7)),
                         reads=[b_wz, b_hT[tt]], writes=[bp])
                S.op("act", lambda e, k=k, ps=ps, nb=nb: e.activation(out=zb[k][:, nb * 512:(nb + 1) * 512], in_=ps[:], func=AF.Silu),
                     reads=[bp], writes=[b_zb[k]])
            S.dma("sp", dst["ZS"][tok, :], zb[k][:], reads=[b_zb[k]])
            ps, bp = pp.get()
            for kc in range(8):
                S.op("pe", lambda e, ps=ps, kc=kc, tok=tok: e.matmul(ps[:, 0:84], lhsT=hT[:, kc, tok], rhs=wsm[:, kc, :],
                                                                    start=(kc == 0), stop=(kc == 7)),
                     reads=b_wsm + [b_hT[tt]], writes=[bp])
            S.op("pool", lambda e, k=k: e.memset(va[k][:, 64:65], 1.0), writes=[b_va[k]])
            S.op("act", lambda e, k=k, ps=ps: e.activation(out=va[k][:, 0:64], in_=ps[:, 0:64], func=AF.Copy), reads=[bp], writes=[b_va[k]])
            S.dma("sp", dst["VA"][tok, :], va[k][:], reads=[b_va[k]])
            S.op("dve", lambda e, ps=ps, tt=tt: e.tensor_scalar(out=wiall[:, tt, :], in0=ps[:, 64:68], scalar1=0.5, scalar2=None, op0=ALU.mult),
                 reads=[bp], writes=[b_wiall])
            S.op("dve", lambda e, ps=ps, tt=tt: e.tensor_tensor(out=dtall[:, 0, tt, :], in0=ps[:, 68:84], in1=dtb[:], op=ALU.add),
                 reads=[bp, b_dtb], writes=[b_dtall])
        S.dma("sp", dst["WI"].rearrange("(j p) d -> p j d", p=128), wiall[:], reads=[b_wiall])
        S.op("act", lambda e: e.activation(out=dtall[:, 1, :, :], in_=dtall[:, 0, :, :], func=AF.Exp), reads=[b_dtall], writes=[b_dtall])
        S.op("dve", lambda e: e.tensor_scalar(out=dtall[:, 0, :, :], in0=dtall[:, 1, :, :], scalar1=1.0, scalar2=None, op0=ALU.add),
             reads=[b_dtall], writes=[b_dtall])
        S.op("act", lambda e: e.activation(out=dtall[:, 1, :, :], in_=dtall[:, 0, :, :], func=AF.Ln), reads=[b_dtall], writes=[b_dtall])
        S.dma("sp", dst["DT"].rearrange("(j p) h -> p j h", p=128), dtall[:, 1, :, :], reads=[b_dtall])
        S.emit()


def phase_attn(nc, S, l, QT, KT, QiT, KiT, GA, VA, WI, OG, cbias_d, pow2_d, ident_d, ones_f, ident_bf, DBG, dbg):
    import os
    with ExitStack() as st:
        def sb(name, shape, dt=F32):
            return st.enter_context(nc.sbuf_tensor(uname(name), list(shape), dt))
        KTa = sb("KTa", [65, L], BF16); b_KTa = Buf()
        KiTs = sb("KiTs", [64, L], BF16); b_KiT = Buf()
        VAs = sb("VAs", [128, NT, 65], BF16); b_VA = Buf()
        WIs = sb("WIs", [128, NT, 4]); b_WI = Buf()
        cb = sb("cb", [128, 128]); b_cb = Buf()
        pw = sb("pw", [128, NBIS + 2]); b_pw = Buf()
        Irep = sb("Irep", [128, 8, 128], BF16); b_Irep = [Buf() for _ in range(8)]
        sel = sb("sel", [64, 65], BF16); b_sel = Buf()
        ksq = sb("ksq", [64, L], BF16); b_ksq = Buf()
        km = sb("km", [65, 16]); b_km = Buf()
        S.dma("sp", KTa[0:64, :], KT, writes=[b_KTa])
        b_KTa1 = Buf()
        S.op("dve", lambda e: e.memset(KTa[64:65, :], 1.0), reads=[], writes=[b_KTa1])
        S.dma("sp", KiTs[:], KiT, writes=[b_KiT])
        S.dma("sp", VAs[:], VA.rearrange("(j p) d -> p j d", p=128), writes=[b_VA])
        S.dma("sp", WIs[:], WI.rearrange("(j p) d -> p j d", p=128), writes=[b_WI])
        S.dma("sp", cb[:], cbias_d, writes=[b_cb])
        S.dma("sp", pw[:], pow2_d, writes=[b_pw])
        for h in range(8):
            S.dma("pool", Irep[:, h, :], ident_d, writes=[b_Irep[h]])
        S.op("dve", lambda e: e.memset(sel[:], 0.0), writes=[b_sel])
        S.op("dve", lambda e: e.memset(sel[:, 64:65], 1.0), writes=[b_sel])
        pO = [st.enter_context(nc.psum_tensor(uname("pO"), [128, 512], F32)) for _ in range(2)]
        b_pO = [Buf(excl=True), Buf(excl=True)]
        pST = PsumPool(nc, st, [f"pst{i}" for i in range(2)])
        pX = PsumPool(nc, st, ["px0", "px1", "px2"])
        pS = PsumPool(nc, st, ["psc0"])
        S.op("act", lambda e: e.activation(out=ksq[:], in_=KTa[0:64, :], func=AF.Square), reads=[b_KTa], writes=[b_ksq])
        for c in range(8):
            ps, bp = pX.get()
            S.op("pe", lambda e, ps=ps, c=c: e.matmul(ps[0:65, :], lhsT=sel[:], rhs=ksq[:, c * 512:(c + 1) * 512], start=True, stop=True),
                 reads=[b_sel, b_ksq], writes=[bp])
            S.op("dve", lambda e, ps=ps, c=c: e.tensor_reduce(out=km[64:65, c:c + 1], in_=ps[64:65, :], axis=AX.X, op=ALU.max),
                 reads=[bp], writes=[b_km])
        S.op("dve", lambda e: e.tensor_reduce(out=km[64:65, 8:9], in_=km[64:65, 0:8], axis=AX.X, op=ALU.max), reads=[b_km], writes=[b_km])

        NB = 2
        NR = 4
        qa = [sb("qa", [65, 8, 128], BF16) for _ in range(NR)]; b_qa = [Buf() for _ in range(NR)]; b_qar = [Buf() for _ in range(NR)]
        qi = [sb("qi", [64, 4, 128], BF16) for _ in range(NB)]; b_qi = [Buf() for _ in range(NB)]
        ga = [sb("ga", [64, 8, 128], BF16) for _ in range(2)]; b_ga = [Buf() for _ in range(2)]
        qsq = sb("qsq", [64, 1024], BF16); b_qsq = Buf()
        qtmp = sb("qtmp", [65, 1024]); b_qtmp = Buf()
        wab = [sb("wab", [128, 8]) for _ in range(NB)]; b_wab = [Buf() for _ in range(NB)]
        score = [sb("score", [128, L]) for _ in range(NB)]; b_score = [Buf() for _ in range(NB)]
        mb = [sb("mb", [128, L], BF16) for _ in range(NR)]; b_mb = [Buf() for _ in range(NR)]
        junk = [sb("junkb", [128, L], BF16) for _ in range(NB)]; b_junk = [Buf() for _ in range(NB)]
        NRL = 8
        rl = [sb("rl", [128, 512], BF16) for _ in range(NRL)]; b_rl = [Buf() for _ in range(NRL)]
        Dh = [sb("Dh", [128, 4, 128], BF16) for _ in range(NB)]; b_Dh = [Buf() for _ in range(NB)]
        bs = [sb("bs", [128, 8]) for _ in range(NB)]; b_bs = [Buf() for _ in range(NB)]
        ab = [sb("ab", [128, 8]) for _ in range(NB)]; b_ab = [Buf() for _ in range(NB)]
        p2 = sb("p2", [128, 2]); b_p2 = Buf()
        q2 = sb("q2", [128, 2]); b_q2 = Buf()
        C2 = sb("C2", [128, 2, 4]); b_cnt = [Buf(), Buf()]; b_t2 = Buf()
        ab2 = sb("ab2", [128, 2]); b_ab2 = [Buf(), Buf()]
        Sp2 = sb("Sp2", [128, 2, NBIS + 2]); b_S2 = Buf()
        b_junkA = [Buf() for _ in range(NB)]
        ALPHA = float(os.environ.get("ATT_ALPHA", "0.7"))
        Sp = [sb("Sp", [128, NBIS + 2]) for _ in range(NB)]
        Sn = [sb("Sn", [128, NBIS + 2]) for _ in range(NB)]
        b_S = [Buf() for _ in range(NB)]
        PT = [sb("PT", [128, 1024], BF16) for _ in range(3)]; b_PT = [Buf() for _ in range(3)]
        rr = sb("rr", [65, 1024]); b_rr = Buf()
        rbc = sb("rbc", [64, 1024]); b_rbc = Buf()
        o1 = sb("o1", [64, 1024]); b_o1 = Buf()
        ogb = [sb("ogb", [64, 8, 128], BF16) for _ in range(2)]; b_ogb = [Buf(), Buf()]
        rlc = [0]
        ptc = [0]

        AMX_ = os.environ.get("ATT_AMX", "dve")
        ACT_EVERY = int(os.environ.get("ATT_ACT_EVERY", "4"))

        def on_act(i):
            return (i % ACT_EVERY == ACT_EVERY - 1) and not (dbg and "no_act_bis" in dbg)

        def pre(i):
            k = i % NB
            k4 = i % NR
            n = 128 * (i + 1)
            tok = slice(i * 128, (i + 1) * 128)
            S.dma("sp", qa[k4][0:64, :, :], QT[:, :, tok].rearrange("h d t -> d h t"), writes=[b_qa[k4]])
            S.dma("sp", qi[k][:], QiT[:, :, tok].rearrange("h d t -> d h t"), writes=[b_qi[k]])
            S.op("act", lambda e: e.activation(out=wab[k][:, 0:4], in_=WIs[:, i, :], func=AF.Abs, scale=0.125),
                 reads=[b_WI], writes=[b_wab[k]])
            S.op("dve", lambda e: e.tensor_scalar(out=wab[k][:, 4:8], in0=WIs[:, i, :], scalar1=0.0, scalar2=2.0,
                                                  op0=ALU.is_ge, op1=ALU.mult), reads=[b_WI], writes=[b_wab[k]])
            S.op("dve", lambda e: e.tensor_scalar(out=wab[k][:, 4:8], in0=wab[k][:, 4:8], scalar1=-1.0, scalar2=None,
                                                  op0=ALU.add), reads=[b_wab[k]], writes=[b_wab[k]])
            S.op("act", lambda e: e.activation(out=qsq[:], in_=qa[k4][0:64, :, :].rearrange("p h t -> p (h t)"), func=AF.Square),
                 reads=[b_qa[k4]], writes=[b_qsq])
            for hb in range(2):
                ps, bp = pX.get()
                S.op("pe", lambda e, ps=ps, hb=hb: e.matmul(ps[0:65, :], lhsT=sel[:], rhs=qsq[:, hb * 512:(hb + 1) * 512], start=True, stop=True),
                     reads=[b_sel, b_qsq], writes=[bp])
                S.op("act", lambda e, ps=ps, hb=hb: e.activation(out=qtmp[64:65, hb * 512:(hb + 1) * 512], in_=ps[64:65, :], func=AF.Sqrt,
                                                                 scale=km[64:65, 8:9]), reads=[bp, b_km], writes=[b_qtmp])
                S.op("dve", lambda e, hb=hb: e.tensor_scalar(out=qa[k4][64:65, hb * 4:(hb + 1) * 4, :].rearrange("p h t -> p (h t)"),
                                                             in0=qtmp[64:65, hb * 512:(hb + 1) * 512], scalar1=-1.0, scalar2=None, op0=ALU.mult),
                     reads=[b_qtmp], writes=[b_qar[k4]])
            S.op("dve", lambda e: e.tensor_tensor(out=Dh[k][:], in0=ident_bf[:].unsqueeze(1).to_broadcast([128, 4, 128]),
                                                  in1=wab[k][:, 4:8].unsqueeze(2).to_broadcast([128, 4, 128]), op=ALU.mult),
                 reads=[b_wab[k]], writes=[b_Dh[k]])
            def dots(c0):
                w = min(512, n - c0)
                rs = []
                for h in range(4):
                    ps, bp = pX.get()
                    S.op("pe", lambda e, ps=ps, h=h, c0=c0, w=w: e.matmul(ps[:, 0:w], lhsT=qi[k][:, h, :], rhs=KiTs[:, c0:c0 + w], start=True, stop=True),
                         reads=[b_qi[k], b_KiT], writes=[bp])
                    r = rlc[0] % NRL
                    rlc[0] += 1
                    rs.append(r)
                    if h < 2:
                        S.op("act", lambda e, ps=ps, h=h, w=w, r=r: e.activation(out=rl[r][:, 0:w], in_=ps[:, 0:w], func=AF.Relu, scale=wab[k][:, h:h + 1]),
                             reads=[bp, b_wab[k]], writes=[b_rl[r]])
                    else:
                        S.op("dve", lambda e, ps=ps, h=h, w=w, r=r: e.tensor_scalar(out=rl[r][:, 0:w], in0=ps[:, 0:w], scalar1=wab[k][:, h:h + 1], scalar2=0.0,
                                                                                   op0=ALU.mult, op1=ALU.max),
                             reads=[bp, b_wab[k]], writes=[b_rl[r]])
                return (c0, w, rs)

            def signs(c0, w, rs):
                pss, bps = pS.get()
                for h in range(4):
                    r = rs[h]
                    S.op("pe", lambda e, pss=pss, h=h, w=w, r=r: e.matmul(pss[:, 0:w], lhsT=Dh[k][:, h, :], rhs=rl[r][:, 0:w], start=(h == 0), stop=(h == 3)),
                         reads=[b_Dh[k], b_rl[r]], writes=[bps])
                if (c0 // 512) % 2 == 0:
                    S.op("act", lambda e, pss=pss, w=w, c0=c0: e.activation(out=score[k][:, c0:c0 + w], in_=pss[:, 0:w], func=AF.Copy),
                         reads=[bps], writes=[b_score[k]])
                else:
                    S.op("dve", lambda e, pss=pss, w=w, c0=c0: e.tensor_copy(out=score[k][:, c0:c0 + w], in_=pss[:, 0:w]),
                         reads=[bps], writes=[b_score[k]])

            pend = None
            for c0 in range(0, n, 512):
                cur = dots(c0)
                if pend is not None:
                    signs(*pend)
                pend = cur
            signs(*pend)
            B = bs[k]
            S.op(AMX_, lambda e: e.tensor_reduce(out=B[:, 0:1], in_=score[k][:, 0:n], axis=AX.X, op=ALU.max, apply_absolute_value=True),
                 reads=[b_score[k]], writes=[b_bs[k]])
            S.op("dve", lambda e: e.tensor_scalar(out=B[:, 1:2], in0=B[:, 0:1], scalar1=1.001, scalar2=1e-6, op0=ALU.mult, op1=ALU.add),
                 reads=[b_bs[k]], writes=[b_bs[k]])
            S.op("dve", lambda e: e.tensor_tensor(out=score[k][:, i * 128:(i + 1) * 128], in0=score[k][:, i * 128:(i + 1) * 128], in1=cb[:], op=ALU.add),
                 reads=[b_score[k], b_cb], writes=[b_score[k]])
            S.op("dve", lambda e: e.tensor_scalar(out=Sp2[:, k, :], in0=pw[:], scalar1=B[:, 1:2], scalar2=None, op0=ALU.mult),
                 reads=[b_bs[k], b_pw], writes=[b_S2])
            S.op("dve", lambda e: e.tensor_scalar(out=q2[:, k:k + 1], in0=B[:, 1:2], scalar1=-1.0, scalar2=None, op0=ALU.mult),
                 reads=[b_bs[k]], writes=[b_q2])
            S.op("dve", lambda e: e.tensor_tensor(out=p2[:, k:k + 1], in0=Sp2[:, k, 0:1], in1=q2[:, k:k + 1], op=ALU.add),
                 reads=[b_S2, b_q2], writes=[b_p2])
            if split(n) == n:
                S.op("dve", lambda e: e.memset(ab2[:, k:k + 1], 0.0), writes=[b_ab2[k]])

        def split(n):
            nA = int(round(ALPHA * n / 128.0)) * 128
            nA = max(128, min(n, nA))
            if n - nA < 256:
                nA = n
            return nA

        def bis_iter_group(tiles, j):
            for i in tiles:
                k = i % NB
                n = 128 * (i + 1)
                nA = split(n)
                nB = n - nA
                S.op("dve", lambda e, k=k, nA=nA, nB=nB: e.tensor_scalar(out=junk[k][:, 0:nA], in0=score[k][:, 0:nA], scalar1=p2[:, k:k + 1],
                                                                        scalar2=float(-(256.0 - nB / 2.0)), op0=ALU.is_ge, op1=ALU.add,
                                                                        accum_out=C2[:, k, 0:1]),
                     reads=[b_score[k], b_p2], writes=[b_junk[k], b_cnt[k]])
                if nB > 0:
                    S.op("act", lambda e, k=k, nA=nA, n=n: e.activation(out=junk[k][:, nA:n], in_=score[k][:, nA:n], func=AF.Sign, bias=p2[:, k:k + 1], scale=-1.0,
                                                                         accum_out=ab2[:, k:k + 1]), reads=[b_score[k], b_p2], writes=[b_junkA[k], b_ab2[k]])
            S.op("dve", lambda e: e.scalar_tensor_tensor(out=C2[:, :, 1], in0=ab2[:], scalar=-0.5, in1=C2[:, :, 0], op0=ALU.mult, op1=ALU.add),
                 reads=b_ab2 + b_cnt, writes=[b_t2])
            S.op("dve", lambda e: e.scalar_tensor_tensor(out=C2[:, :, 2], in0=C2[:, :, 1], scalar=0.0, in1=Sp2[:, :, j], op0=ALU.is_ge, op1=ALU.mult),
                 reads=[b_t2, b_S2], writes=[b_t2])
            S.op("dve", lambda e: e.tensor_tensor(out=q2[:], in0=q2[:], in1=C2[:, :, 2], op=ALU.add), reads=[b_q2, b_t2], writes=[b_q2])
            S.op("dve", lambda e: e.tensor_tensor(out=p2[:], in0=q2[:], in1=Sp2[:, :, j + 1], op=ALU.add), reads=[b_q2, b_S2], writes=[b_p2])

        def post(i):
            k = i % NB
            k4 = i % NR
            n = 128 * (i + 1)
            B = bs[k]
            S.op("dve", lambda e: e.tensor_scalar(out=mb[k4][:, 0:n], in0=score[k][:, 0:n], scalar1=q2[:, k:k + 1], scalar2=-30000.0,
                                                  op0=ALU.is_lt, op1=ALU.mult), reads=[b_score[k], b_q2], writes=[b_mb[k4]])
            if dbg and "dump_attn" in dbg and i == dbg["dump_attn"]:
                S.dma("sp", DBG[:, 0:n], score[k][:, 0:n], reads=[b_score[k]])
                S.dma("sp", DBG[:, 4080:4082], bs[k][:, 0:2], reads=[b_bs[k]])
                S.dma("sp", DBG[:, 4085:4086], q2[:, k:k + 1], reads=[b_q2], allow_slow_non_contiguous=True)

        def stage1_group(tiles, units):
            nslots = len(tiles) + NBIS
            per = -(-len(units) // nslots) if units else 0
            pos = [0]

            def drain(final=False):
                m = len(units) if final else min(len(units), pos[0] + per)
                while pos[0] < m:
                    units[pos[0]]()
                    pos[0] += 1

            for t in tiles:
                pre(t)
                drain()
            for j in range(NBIS):
                bis_iter_group(tiles, j)
                drain()
            for t in tiles:
                post(t)
            drain(final=True)

        def stage2_units(i):
            k = i % NR
            tok = slice(i * 128, (i + 1) * 128)
            g = i % 2
            units = []

            pidx = {}

            def block(j):
                if j == 0:
                    S.dma("sp", ga[g][:], GA[:, :, tok].rearrange("h d t -> d h t"), writes=[b_ga[g]])
                ks = slice(j * 128, (j + 1) * 128)
                p = ptc[0] % 3
                ptc[0] += 1
                pidx[j] = p
                for hb in range(2):
                    ps, bp = pST.get()
                    S.op("pe", lambda e, ps=ps, hb=hb, ks=ks: e.matmul(ps[:], lhsT=KTa[:, ks], rhs=qa[k][:, hb * 4:(hb + 1) * 4, :], start=True, stop=False),
                         reads=[b_KTa, b_KTa1, b_qa[k], b_qar[k]], writes=[bp])
                    S.op("pe", lambda e, ps=ps, hb=hb, ks=ks: e.matmul(ps[:], lhsT=mb[k][:, ks], rhs=Irep[:, hb * 4:(hb + 1) * 4, :], start=False, stop=True),
                         reads=[b_mb[k]] + b_Irep, writes=[bp])
                    S.op("act", lambda e, ps=ps, hb=hb, p=p: e.activation(out=PT[p][:, hb * 512:(hb + 1) * 512], in_=ps[:], func=AF.Exp, scale=0.125),
                         reads=[bp], writes=[b_PT[p]])

            def pv(j):
                p = pidx[j]
                for hb in range(2):
                    S.op("pe", lambda e, hb=hb, p=p, j=j: e.matmul(pO[hb][0:65, :], lhsT=VAs[:, j, :], rhs=PT[p][:, hb * 512:(hb + 1) * 512],
                                                                  start=(j == 0), stop=(j == i)), reads=[b_VA, b_PT[p]], writes=[b_pO[hb]])

            def fin():
              for hb in range(2):
                hs = slice(hb * 512, (hb + 1) * 512)
                S.op("act", lambda e, hb=hb, hs=hs: e.activation(out=rr[64:65, hs], in_=pO[hb][64:65, :], func=AF.Ln), reads=[b_pO[hb]], writes=[b_rr])
                S.op("act", lambda e, hs=hs: e.activation(out=rr[64:65, hs], in_=rr[64:65, hs], func=AF.Exp, scale=-1.0), reads=[b_rr], writes=[b_rr])
                ps, bp = pST.get()
                S.op("pe", lambda e, ps=ps, hs=hs: e.matmul(ps[0:64, :], lhsT=ones_f[64:65, 0:64], rhs=rr[64:65, hs], start=True, stop=True),
                     reads=[b_rr], writes=[bp])
                S.op("act", lambda e, ps=ps, hs=hs: e.activation(out=rbc[:, hs], in_=ps[0:64, :], func=AF.Copy), reads=[bp], writes=[b_rbc])
                S.op("dve", lambda e, hb=hb, hs=hs: e.tensor_tensor(out=o1[:, hs], in0=pO[hb][0:64, :], in1=rbc[:, hs], op=ALU.mult),
                     reads=[b_pO[hb], b_rbc], writes=[b_o1])
              S.op("pool", lambda e: e.tensor_tensor(out=ogb[g][:].rearrange("p h t -> p (h t)"), in0=o1[:],
                                                     in1=ga[g][:].rearrange("p h t -> p (h t)"), op=ALU.mult),
                   reads=[b_o1, b_ga[g]], writes=[b_ogb[g]])
              S.dma("sp", OG[:, :, tok].rearrange("h d t -> d h t"), ogb[g][:], reads=[b_ogb[g]])

            def mk(j):
                def u():
                    block(j)
                    if j > 0:
                        pv(j - 1)
                return u

            for j in range(i + 1):
                units.append(mk(j))

            def last():
                pv(i)
                fin()
            units.append(last)
            return units

        nq = NT if not (dbg and "nq" in dbg) else dbg["nq"]
        groups = [list(range(a, min(a + NB, nq))) for a in range(0, nq, NB)]
        prev = []
        for grp in groups:
            stage1_group(grp, prev)
            prev = []
            for t in grp:
                prev += stage2_units(t)
        for u in prev:
            u()
        S.emit()


def phase_ssd(nc, S, l, XBC, ZS, DTs, YN, convw_d, convb_d, alog_d, dskip_d, sng_d, triu_d, sut_d, ident, identf, ones_f, ones_b, DBG, dbg):
    with ExitStack() as st:
        def sb(name, shape, dt=F32):
            return st.enter_context(nc.sbuf_tensor(uname(name), list(shape), dt))

        import os
        PE_ = os.environ.get('SSD_POOL', 'pool')

        def bc3(ap, shape):
            return ap.unsqueeze(2).to_broadcast(shape)

        cw = sb("cw", [128, 12, 4]); b_cw = Buf()
        cbv = sb("cbv", [128, 12]); b_cbv = Buf()
        a_b = sb("a_b", [128, 16]); b_ab = Buf()
        dsk = sb("dsk", [128, 16]); b_dsk = Buf()
        sng = sb("sng", [128, D]); b_sng = Buf()
        triu = sb("triu", [128, 128]); b_triu = Buf()
        sut = sb("sut", [128, 128]); b_sut = Buf()
        b_idf = Buf()
        DTall = sb("DTall", [128, NT, 16]); b_DT = Buf()
        H = sb("H", [128, 16, 64]); b_H = Buf()
        Hb = sb("Hb", [128, 16, 64], BF16); b_Hb = Buf()
        Ht = sb("Ht", [128, 16, 64]); b_Ht = Buf()
        S.dma("sp", cw[:], convw_d[l], writes=[b_cw])
        S.dma("sp", cbv[:], convb_d[l], writes=[b_cbv])
        S.dma("sp", a_b[:], alog_d[l].partition_broadcast(128), writes=[b_ab])
        S.dma("sp", dsk[:], dskip_d[l].partition_broadcast(128), writes=[b_dsk])
        S.dma("sp", sng[:], sng_d[l].partition_broadcast(128), writes=[b_sng])
        S.dma("sp", triu[:], triu_d, writes=[b_triu])
        S.dma("sp", sut[:], sut_d, writes=[b_sut])
        S.dma("sp", DTall[:], DTs.rearrange("(c p) h -> p c h", p=128), writes=[b_DT])
        S.op("act", lambda e: e.activation(out=a_b[:], in_=a_b[:], func=AF.Exp), reads=[b_ab], writes=[b_ab])
        S.op("dve", lambda e: e.tensor_scalar(out=a_b[:], in0=a_b[:], scalar1=-1.0, scalar2=None, op0=ALU.mult), reads=[b_ab], writes=[b_ab])
        S.op("dve", lambda e: e.memset(H[:], 0.0), writes=[b_H])
        S.op("dve", lambda e: e.memset(Hb[:], 0.0), writes=[b_Hb])

        pp = PsumPool(nc, st, [f"ps{i}" for i in range(6)])
        tpB = st.enter_context(nc.psum_tensor(uname("tpB"), [128, 1024], BF16)); b_tpB = Buf(excl=True)
        tpY = st.enter_context(nc.psum_tensor(uname("tpY"), [128, 1024], BF16)); b_tpY = Buf(excl=True)

        xin = sb("xin", [128, 12, 515]); b_xin = Buf()
        cv = sb("cv", [128, 12, 512]); b_cv = Buf()
        cvx = sb("cvx", [128, 8, 512]); b_cvx = Buf()
        cvbc2 = [sb("cvbc", [128, 4, 512], BF16) for _ in range(2)]; b_cvbc2 = [Buf(), Buf()]
        xs_tok2 = [sb("xs_tok", [128, 16, 64]) for _ in range(2)]; b_xst2 = [Buf(), Buf()]
        xd2 = [sb("xd", [128, 16, 64], BF16) for _ in range(2)]; b_xd2 = [Buf(), Buf()]
        xds2 = [sb("xds", [128, 16, 64], BF16) for _ in range(2)]; b_xds2 = [Buf(), Buf()]
        Btok2 = [sb("Btok", [128, 256], BF16) for _ in range(2)]; b_Btok2 = [Buf(), Buf()]
        dd2 = [sb("dd", [128, 8, 16]) for _ in range(2)]; b_dd2 = [Buf(), Buf()]
        Mm2 = [sb("Mm", [128, 16, 128], BF16) for _ in range(2)]; b_Mm2 = [Buf(), Buf()]
        zs2 = [sb("zs", [128, D], BF16) for _ in range(2)]; b_zs2 = [Buf(), Buf()]
        csb = sb("csb", [128, 32]); b_csb = Buf()
        LH = sb("LH", [128, 2, 16, 128], BF16); b_LH = Buf()
        hl = sb("hl", [128, 2, 16], BF16); b_hl = Buf()
        hlf = sb("hlf", [128, 2, 16]); b_hlf = Buf()
        triub = sb("triub", [128, 128], BF16); b_triub = Buf()
        sutb = sb("sutb", [128, 128], BF16); b_sutb = Buf()
        S.op("dve", lambda e: e.tensor_copy(out=triub[:], in_=triu[:]), reads=[b_triu], writes=[b_triub])
        S.op("dve", lambda e: e.tensor_copy(out=sutb[:], in_=sut[:]), reads=[b_sut], writes=[b_sutb])
        dec = sb("dec", [128, 16, 128], BF16); b_dec = Buf()
        mcb = sb("mcb", [128, 2, 128], BF16); b_mcb = Buf()
        yo = sb("yo", [128, 16, 64]); b_yo = Buf()
        y1 = sb("y1", [128, 16, 64]); b_y1 = Buf()
        y2 = sb("y2", [128, 16, 64]); b_y2 = Buf()
        nst = sb("nst", [128, 8]); b_nst = Buf()
        junk = sb("junk3", [128, 512], BF16); b_junk = Buf()
        yn = sb("yn", [128, D], BF16); b_yn = Buf()
        ynT = sb("ynT", [128, 8, 128], BF16); b_ynT = Buf()

        XBCv = XBC.rearrange("(ct p) t -> p ct t", p=128)
        YNv = YN.rearrange("(ct p) t -> p ct t", p=128)
        nsc = 8 if not (dbg and "nsc" in dbg) else dbg["nsc"]

        xin2 = [xin, sb("xin_b", [128, 12, 515])]; b_xin2 = [b_xin, Buf()]
        b_cvc = [Buf() for _ in range(12)]

        def conv_load(sc):
            t0 = sc * 512
            xi, bx = xin2[sc % 2], b_xin2[sc % 2]
            if sc == 0:
                S.op("pool", lambda e: e.memset(xi[:, :, 0:3], 0.0), writes=[bx])
                S.dma("sp", xi[:, :, 3:515], XBCv[:, :, 0:512], writes=[bx])
            else:
                S.dma("sp", xi[:], XBCv[:, :, t0 - 3:t0 + 512], writes=[bx])

        def conv_units(sc):
            xi, bx = xin2[sc % 2], b_xin2[sc % 2]
            cvbc, b_cvbc = cvbc2[sc % 2], b_cvbc2[sc % 2]

            def mk(cts, last):
                def u():
                    for ct in cts:
                        S.op(PE_, lambda e, ct=ct: e.tensor_scalar(out=cv[:, ct, :], in0=xi[:, ct, 3:515], scalar1=cw[:, ct, 3:4], scalar2=cbv[:, ct:ct + 1],
                                                                   op0=ALU.mult, op1=ALU.add), reads=[bx, b_cw, b_cbv], writes=[b_cvc[ct]])
                        for kk in range(3):
                            S.op("dve", lambda e, ct=ct, kk=kk: e.scalar_tensor_tensor(out=cv[:, ct, :], in0=xi[:, ct, kk:kk + 512], scalar=cw[:, ct, kk:kk + 1],
                                                                                        in1=cv[:, ct, :], op0=ALU.mult, op1=ALU.add),
                                 reads=[bx, b_cw, b_cvc[ct]], writes=[b_cvc[ct]])
                    if last:
                        S.op("act", lambda e: e.activation(out=cvx[:], in_=cv[:, 0:8, :], func=AF.Silu), reads=b_cvc[0:8], writes=[b_cvx])
                        S.op("act", lambda e: e.activation(out=cvbc[:], in_=cv[:, 8:12, :], func=AF.Silu), reads=b_cvc[8:12], writes=[b_cvbc])
                return u
            return [mk([0, 1, 2], False), mk([3, 4, 5], False), mk([6, 7, 8], False), mk([9, 10, 11], True)]

        def units_A(c):
            sc, cc = c // 4, c % 4
            k = c % 2
            cvbc, b_cvbc = cvbc2[sc % 2], b_cvbc2[sc % 2]
            xs_tok, b_xst, xd, b_xd, xds, b_xds = xs_tok2[k], b_xst2[k], xd2[k], b_xd2[k], xds2[k], b_xds2[k]
            Btok, b_Btok, dd, b_dd, Mm, b_Mm, zs, b_zs = Btok2[k], b_Btok2[k], dd2[k], b_dd2[k], Mm2[k], b_Mm2[k], zs2[k], b_zs2[k]
            cols = slice(cc * 128, (cc + 1) * 128)
            tok = slice(c * 128, (c + 1) * 128)
            dt = DTall[:, c, :]

            def a0():
                S.dma("sp", zs[:], ZS[tok, :], writes=[b_zs])
                tA = [pp.get(), pp.get()]
                for ct in range(8):
                    ps, bp = tA[ct // 4]
                    S.op("pe", lambda e, ps=ps, ct=ct: e.transpose(out=ps[:, (ct % 4) * 128:(ct % 4 + 1) * 128], in_=cvx[:, ct, cols], identity=identf[:]),
                         reads=[b_cvx, b_idf], writes=[bp])
                for g in range(2):
                    S.op("pe", lambda e, g=g: e.transpose(out=tpB[:, g * 128:(g + 1) * 128], in_=cvbc[:, g, cols], identity=ident[:]),
                         reads=[b_cvbc], writes=[b_tpB])
                S.op("act", lambda e: e.activation(out=Btok[:], in_=tpB[:, 0:256], func=AF.Copy), reads=[b_tpB], writes=[b_Btok])
                for hb in range(2):
                    ps, bp = tA[hb]
                    hs = slice(hb * 8, (hb + 1) * 8)
                    S.op("dve", lambda e, ps=ps, hs=hs: e.tensor_tensor(out=xd[:, hs, :], in0=ps[:].rearrange("p (h d) -> p h d", h=8),
                                                                        in1=bc3(dt[:, hs], [128, 8, 64]), op=ALU.mult),
                         reads=[bp, b_DT], writes=[b_xd])
                    S.op("act", lambda e, ps=ps, hs=hs: e.activation(out=xs_tok[:, hs, :], in_=ps[:].rearrange("p (h d) -> p h d", h=8), func=AF.Copy),
                         reads=[bp], writes=[b_xst])

            def a1():
                S.op("dve", lambda e: e.tensor_tensor(out=dd[:, 0, :], in0=dt, in1=a_b[:], op=ALU.mult), reads=[b_DT, b_ab], writes=[b_dd])
                psc, bpc = pp.get()
                S.op("dve", lambda e: e.tensor_copy(out=hl[:, 0, :], in_=dd[:, 0, :]), reads=[b_dd], writes=[b_hl])
                S.op("dve", lambda e: e.tensor_tensor(out=hl[:, 1, :], in0=dd[:, 0, :], in1=hl[:, 0, :], op=ALU.subtract), reads=[b_dd, b_hl], writes=[b_hl])
                S.op("dve", lambda e: e.tensor_copy(out=hlf[:], in_=hl[:]), reads=[b_hl], writes=[b_hlf])
                for pi in range(2):
                    S.op("pe", lambda e, pi=pi: e.matmul(psc[:, 0:16], lhsT=triub[:], rhs=hl[:, pi, :], start=(pi == 0), stop=(pi == 1)),
                         reads=[b_triub, b_hl], writes=[bpc])
                for pi in range(2):
                    S.op("pe", lambda e, pi=pi: e.matmul(psc[:, 16:32], lhsT=ones_b[:], rhs=hl[:, pi, :], start=(pi == 0), stop=(pi == 1)),
                         reads=[b_hl], writes=[bpc])
                S.op("act", lambda e: e.activation(out=csb[:], in_=psc[:, 0:32], func=AF.Copy), reads=[bpc], writes=[b_csb])
                S.op("act", lambda e: e.activation(out=dd[:, 1:3, :].rearrange("p a h -> p (a h)"), in_=csb[:], func=AF.Exp), reads=[b_csb], writes=[b_dd])
                S.op("dve", lambda e: e.tensor_tensor(out=dd[:, 3, :], in0=csb[:, 16:32], in1=csb[:, 0:16], op=ALU.subtract), reads=[b_csb], writes=[b_dd])
                S.op("act", lambda e: e.activation(out=dd[:, 4, :], in_=dd[:, 3, :], func=AF.Exp), reads=[b_dd], writes=[b_dd])

            def a2():
                S.op("dve", lambda e: e.tensor_tensor(out=xds[:], in0=xd[:], in1=bc3(dd[:, 4, :], [128, 16, 64]), op=ALU.mult),
                     reads=[b_xd, b_dd], writes=[b_xds])
                pcb, bpcb = pp.get()
                for g in range(2):
                    S.op("pe", lambda e, g=g: e.matmul(pcb[:, g * 128:(g + 1) * 128], lhsT=cvbc[:, g, cols], rhs=cvbc[:, 2 + g, cols], start=True, stop=True),
                         reads=[b_cvbc], writes=[bpcb])
                S.op("dve", lambda e: e.tensor_tensor(out=mcb[:], in0=pcb[:, 0:256].rearrange("p (g l) -> p g l", g=2),
                                                      in1=triu[:].unsqueeze(1).to_broadcast([128, 2, 128]), op=ALU.mult),
                     reads=[bpcb, b_triu], writes=[b_mcb])

            def a3():
                for pi in range(2):
                    S.op("dve", lambda e, pi=pi: e.tensor_tensor(out=LH[:, pi, :, :], in0=sutb[:].unsqueeze(1).to_broadcast([128, 16, 128]),
                                                                 in1=hlf[:, pi, :].unsqueeze(2).to_broadcast([128, 16, 128]), op=ALU.mult),
                         reads=[b_sutb, b_hlf], writes=[b_LH])

            def mk_seg(q4s):
                def u():
                    for q4 in q4s:
                        ps, bp = pp.get()
                        for hh in range(4):
                            h = q4 * 4 + hh
                            for pi in range(2):
                                S.op("pe", lambda e, ps=ps, h=h, hh=hh, pi=pi: e.matmul(ps[:, hh * 128:(hh + 1) * 128], lhsT=LH[:, pi, h, :], rhs=triub[:],
                                                                                       start=(pi == 0), stop=(pi == 1)),
                                     reads=[b_LH, b_triub], writes=[bp])
                        S.op("act", lambda e, ps=ps, q4=q4: e.activation(out=dec[:, q4 * 4:(q4 + 1) * 4, :].rearrange("p h l -> p (h l)"), in_=ps[:], func=AF.Exp),
                             reads=[bp], writes=[b_dec])
                return u

            def a6():
                for g in range(2):
                    S.op("dve", lambda e, g=g: e.tensor_tensor(out=Mm[:, g * 8:(g + 1) * 8, :], in0=dec[:, g * 8:(g + 1) * 8, :],
                                                               in1=mcb[:, g, :].unsqueeze(1).to_broadcast([128, 8, 128]), op=ALU.mult),
                         reads=[b_dec, b_mcb], writes=[b_Mm])

            return [a0, a1, a2, a3, mk_seg([0, 1]), mk_seg([2, 3]), a6]

        def units_B(c):
            sc, cc = c // 4, c % 4
            k = c % 2
            cvbc, b_cvbc = cvbc2[sc % 2], b_cvbc2[sc % 2]
            xs_tok, b_xst, xd, b_xd, xds, b_xds = xs_tok2[k], b_xst2[k], xd2[k], b_xd2[k], xds2[k], b_xds2[k]
            Btok, b_Btok, dd, b_dd, Mm, b_Mm, zs, b_zs = Btok2[k], b_Btok2[k], dd2[k], b_dd2[k], Mm2[k], b_Mm2[k], zs2[k], b_zs2[k]
            cols = slice(cc * 128, (cc + 1) * 128)
            tok = slice(c * 128, (c + 1) * 128)

            def b0():
                for g in range(2):
                    ps, bp = pp.get()
                    S.op("pe", lambda e, ps=ps, g=g: e.matmul(ps[:], lhsT=cvbc[:, 2 + g, cols], rhs=Hb[:, g * 8:(g + 1) * 8, :], start=True, stop=True),
                         reads=[b_cvbc, b_Hb], writes=[bp])
                    S.op("dve", lambda e, ps=ps, g=g: e.tensor_tensor(out=yo[:, g * 8:(g + 1) * 8, :], in0=ps[:].rearrange("p (h d) -> p h d", h=8),
                                                                      in1=bc3(dd[:, 1, g * 8:(g + 1) * 8], [128, 8, 64]), op=ALU.mult),
                         reads=[bp, b_dd], writes=[b_yo])

            def b1():
                sts = [pp.get(), pp.get()]
                for g in range(2):
                    ps, bp = sts[g]
                    S.op("pe", lambda e, ps=ps, g=g: e.matmul(ps[:], lhsT=Btok[:, g * 128:(g + 1) * 128], rhs=xds[:, g * 8:(g + 1) * 8, :], start=True, stop=True),
                         reads=[b_Btok, b_xds], writes=[bp])
                S.op("dve", lambda e: e.tensor_tensor(out=Ht[:], in0=H[:], in1=bc3(dd[:, 2, :], [128, 16, 64]), op=ALU.mult),
                     reads=[b_H, b_dd], writes=[b_Ht])
                for g in range(2):
                    ps, bp = sts[g]
                    S.op("dve", lambda e, ps=ps, g=g: e.tensor_tensor(out=H[:, g * 8:(g + 1) * 8, :], in0=ps[:].rearrange("p (h d) -> p h d", h=8),
                                                                      in1=Ht[:, g * 8:(g + 1) * 8, :], op=ALU.add),
                         reads=[bp, b_Ht], writes=[b_H])
                S.op("act", lambda e: e.activation(out=Hb[:], in_=H[:], func=AF.Copy), reads=[b_H], writes=[b_Hb])

            def b2():
                for hb in range(2):
                    ps, bp = pp.get()
                    for hh in range(8):
                        h = hb * 8 + hh
                        S.op("pe", lambda e, ps=ps, h=h, hh=hh: e.matmul(ps[:, hh * 64:(hh + 1) * 64], lhsT=Mm[:, h, :], rhs=xd[:, h, :], start=True, stop=True),
                             reads=[b_Mm, b_xd], writes=[bp])
                    hs = slice(hb * 8, (hb + 1) * 8)
                    S.op("dve", lambda e, ps=ps, hs=hs: e.tensor_tensor(out=y1[:, hs, :], in0=ps[:].rearrange("p (h d) -> p h d", h=8), in1=yo[:, hs, :], op=ALU.add),
                         reads=[bp, b_yo], writes=[b_y1])

            def b3():
                S.op(PE_, lambda e: e.tensor_tensor(out=y2[:], in0=xs_tok[:], in1=bc3(dsk[:], [128, 16, 64]), op=ALU.mult),
                     reads=[b_xst, b_dsk], writes=[b_y2])
                S.op(PE_, lambda e: e.tensor_tensor(out=y2[:], in0=y2[:], in1=y1[:], op=ALU.add), reads=[b_y2, b_y1], writes=[b_y2])
                S.op(PE_, lambda e: e.tensor_tensor(out=y2[:].rearrange("p h d -> p (h d)"), in0=y2[:].rearrange("p h d -> p (h d)"), in1=zs[:], op=ALU.mult),
                     reads=[b_y2, b_zs], writes=[b_y2])

            def b4():
                for g in range(2):
                    S.op("act", lambda e, g=g: e.activation(out=junk[:], in_=y2[:, g * 8:(g + 1) * 8, :].rearrange("p h d -> p (h d)"), func=AF.Square,
                                                            accum_out=nst[:, g:g + 1]), reads=[b_y2], writes=[b_junk, b_nst])
                S.op("dve", lambda e: e.tensor_scalar(out=nst[:, 2:4], in0=nst[:, 0:2], scalar1=1.0 / 512, scalar2=EPS, op0=ALU.mult, op1=ALU.add),
                     reads=[b_nst], writes=[b_nst])
                S.op("act", lambda e: e.activation(out=nst[:, 4:6], in_=nst[:, 2:4], func=AF.Ln), reads=[b_nst], writes=[b_nst])
                S.op("act", lambda e: e.activation(out=nst[:, 6:8], in_=nst[:, 4:6], func=AF.Exp, scale=-0.5), reads=[b_nst], writes=[b_nst])

            def b5():
                for g in range(2):
                    S.op("dve", lambda e, g=g: e.scalar_tensor_tensor(out=yn[:, g * 512:(g + 1) * 512], in0=y2[:, g * 8:(g + 1) * 8, :].rearrange("p h d -> p (h d)"),
                                                                      scalar=nst[:, 6 + g:7 + g], in1=sng[:, g * 512:(g + 1) * 512], op0=ALU.mult, op1=ALU.mult),
                         reads=[b_y2, b_nst, b_sng], writes=[b_yn])

            def b6():
                for ct in range(8):
                    S.op("pe", lambda e, ct=ct: e.transpose(out=tpY[:, ct * 128:(ct + 1) * 128], in_=yn[:, ct * 128:(ct + 1) * 128], identity=ident[:]),
                         reads=[b_yn], writes=[b_tpY])
                S.op("act", lambda e: e.activation(out=ynT[:].rearrange("p c t -> p (c t)"), in_=tpY[:], func=AF.Copy), reads=[b_tpY], writes=[b_ynT])
                S.dma("sp", YNv[:, :, tok], ynT[:], reads=[b_ynT])

            return [b0, b1, b2, b3, b4, b5, b6]

        def zip_emit(*streams):
            for i in range(max(len(u) for u in streams)):
                for u in streams:
                    if i < len(u):
                        u[i]()

        nch = nsc * 4
        conv_load(0)
        for u in conv_units(0):
            u()
        if nsc > 1:
            conv_load(1)
        zip_emit(units_A(0))
        for c in range(nch):
            sc, cc = c // 4, c % 4
            ua = units_A(c + 1) if c + 1 < nch else []
            uc = []
            if sc + 1 < nsc:
                cu = conv_units(sc + 1)
                uc = [lambda: None] * 2 + [cu[cc]]
                if cc == 3 and sc + 2 < nsc:
                    uc.append(lambda sc=sc: conv_load(sc + 2))
            if cc == 3 and sc + 1 < nsc:
                zip_emit(uc, units_B(c))
                zip_emit(ua)
            else:
                zip_emit(units_B(c), ua, uc)
        S.emit()


def phase_out(nc, S, l, x_src, X1, OG, YN, GLA, GLS, wa_d, ws_d, wo_d, GATE_b, nxt=None):
    with ExitStack() as st:
        if nxt is not None:
            nxt(st)
        def sb(name, shape, dt=F32):
            return st.enter_context(nc.sbuf_tensor(uname(name), list(shape), dt))
        wa = sb("wa", [64, 8, D], BF16); b_wa = [Buf() for _ in range(8)]
        ws = sb("ws", [128, 8, D], BF16); b_ws = [Buf() for _ in range(8)]
        wo = sb("wo", [128, 8, D], BF16); b_wo = [Buf() for _ in range(2)]
        wav = wa_d[l].rearrange("(h d) n -> d h n", d=64)
        wsv = ws_d[l].rearrange("(kc p) n -> p kc n", p=128)
        wov = wo_d[l].rearrange("(kc p) n -> p kc n", p=128)
        for nchunk in range(8):
            ns = slice(nchunk * 128, (nchunk + 1) * 128)
            S.dma("pool", wa[:, :, ns], wav[:, :, ns], writes=[b_wa[nchunk]])
            S.dma("pool", ws[:, :, ns], wsv[:, :, ns], writes=[b_ws[nchunk]])
        for nb in range(2):
            S.dma("pool", wo[:, :, nb * 512:(nb + 1) * 512], wov[:, :, nb * 512:(nb + 1) * 512], writes=[b_wo[nb]])
        pp = PsumPool(nc, st, [f"po{i}" for i in range(8 if nxt is None else 6)])
        ogt2 = [sb("ogt", [64, 8, 512], BF16) for _ in range(2)]; b_ogt2 = [Buf(), Buf()]
        ynt2 = [sb("ynt", [128, 8, 512], BF16) for _ in range(2)]; b_ynt2 = [Buf(), Buf()]
        gla = sb("gla", [128, 8, 512], BF16); b_gla = Buf()
        gls = sb("gls", [128, 8, 512], BF16); b_gls = Buf()
        ta = [sb("ta", [128, 512]) for _ in range(2)]; b_ta = [Buf(), Buf()]
        tb = [sb("tb", [128, 512]) for _ in range(2)]; b_tb = [Buf(), Buf()]
        mg = sb("mg", [128, 8, 512], BF16); b_mg = [Buf() for _ in range(8)]
        xt = [sb("xo", [128, D]) for _ in range(2)]; b_xt = [Buf(), Buf()]
        t3 = [sb("t3", [128, D]) for _ in range(2)]; b_t3 = [Buf(), Buf()]
        OGv = OG.rearrange("h d t -> d h t")
        YNv = YN.rearrange("(ct p) t -> p ct t", p=128)
        GLAv = GLA.rearrange("(ct p) t -> p ct t", p=128)
        GLSv = GLS.rearrange("(ct p) t -> p ct t", p=128)
        def load_blk(c):
            tok = slice(c * 512, (c + 1) * 512)
            j = c % 2
            S.dma("sp", ogt2[j][:], OGv[:, :, tok], writes=[b_ogt2[j]])
            S.dma("sp", ynt2[j][:], YNv[:, :, tok], writes=[b_ynt2[j]])

        def load_gates(c):
            tok = slice(c * 512, (c + 1) * 512)
            S.dma("sp", gla[:], GLAv[:, :, tok], writes=[b_gla])
            S.dma("sp", gls[:], GLSv[:, :, tok], writes=[b_gls])

        load_blk(0)
        load_gates(0)
        for c in range(8):
            tok = slice(c * 512, (c + 1) * 512)
            if c + 1 < 8:
                load_blk(c + 1)
            ogt, b_ogt, ynt, b_ynt = ogt2[c % 2], b_ogt2[c % 2], ynt2[c % 2], b_ynt2[c % 2]
            for nchunk in range(8):
                ns = slice(nchunk * 128, (nchunk + 1) * 128)
                a = nchunk % 2
                psa, bpa = pp.get()
                for h in range(8):
                    S.op("pe", lambda e, psa=psa, h=h, ns=ns, ogt=ogt: e.matmul(psa[:], lhsT=wa[:, h, ns], rhs=ogt[:, h, :], start=(h == 0), stop=(h == 7)),
                         reads=[b_wa[nchunk], b_ogt], writes=[bpa])
                pss, bps = pp.get()
                for kc in range(8):
                    S.op("pe", lambda e, pss=pss, kc=kc, ns=ns, ynt=ynt: e.matmul(pss[:], lhsT=ws[:, kc, ns], rhs=ynt[:, kc, :], start=(kc == 0), stop=(kc == 7)),
                         reads=[b_ws[nchunk], b_ynt], writes=[bps])
                S.op("dve", lambda e, psa=psa, a=a, nchunk=nchunk, gla=gla: e.tensor_tensor(out=ta[a][:], in0=psa[:], in1=gla[:, nchunk, :], op=ALU.mult),
                     reads=[bpa, b_gla], writes=[b_ta[a]])
                S.op("dve", lambda e, pss=pss, a=a, nchunk=nchunk, gls=gls: e.tensor_tensor(out=tb[a][:], in0=pss[:], in1=gls[:, nchunk, :], op=ALU.mult),
                     reads=[bps, b_gls], writes=[b_tb[a]])
                S.op("pool", lambda e, a=a, nchunk=nchunk: e.tensor_tensor(out=mg[:, nchunk, :], in0=ta[a][:], in1=tb[a][:], op=ALU.add),
                     reads=[b_ta[a], b_tb[a]], writes=[b_mg[nchunk]])
            if c + 1 < 8:
                load_gates(c + 1)
            for tq in range(4):
                k = tq % 2
                t128 = slice(c * 512 + tq * 128, c * 512 + (tq + 1) * 128)
                S.dma("sp", xt[k][:], x_src[t128, :], writes=[b_xt[k]])
                for nb in range(2):
                    ps, bp = pp.get()
                    for kc in range(8):
                        S.op("pe", lambda e, ps=ps, kc=kc, tq=tq, nb=nb: e.matmul(ps[:], lhsT=mg[:, kc, tq * 128:(tq + 1) * 128], rhs=wo[:, kc, nb * 512:(nb + 1) * 512],
                                                                                 start=(kc == 0), stop=(kc == 7)),
                             reads=b_mg + [b_wo[nb]], writes=[bp])
                    S.op("dve", lambda e, ps=ps, k=k, nb=nb: e.tensor_tensor(out=t3[k][:, nb * 512:(nb + 1) * 512], in0=ps[:], in1=GATE_b[:, nb * 512:(nb + 1) * 512], op=ALU.mult),
                         reads=[bp], writes=[b_t3[k]])
                S.op("pool", lambda e, k=k: e.tensor_tensor(out=t3[k][:], in0=t3[k][:], in1=xt[k][:], op=ALU.add), reads=[b_t3[k], b_xt[k]], writes=[b_t3[k]])
                S.dma("sp", X1[t128, :], t3[k][:], reads=[b_t3[k]])
        S.emit()


def phase_final(nc, S, x_src, fg_d, out_d):
    with ExitStack() as st:
        def sb(name, shape, dt=F32):
            return st.enter_context(nc.sbuf_tensor(uname(name), list(shape), dt))
        fg = sb("fg", [128, D]); b_fg = Buf()
        S.dma("sp", fg[:], fg_d.partition_broadcast(128), writes=[b_fg])
        NF = 4
        xt = [sb("xf", [128, D]) for _ in range(NF)]; b_xt = [Buf() for _ in range(NF)]
        junk = sb("junkf", [128, D], BF16); b_junk = Buf()
        ss = [sb("ssf", [128, 4]) for _ in range(NF)]; b_ss = [Buf() for _ in range(NF)]
        yo = [sb("yof", [128, D]) for _ in range(NF)]; b_yo = [Buf() for _ in range(NF)]
        for tt in range(NT):
            k = tt % NF
            tok = slice(tt * 128, (tt + 1) * 128)
            S.dma("sp", xt[k][:], x_src[tok, :], writes=[b_xt[k]])
            S.op("act", lambda e, k=k: e.activation(out=junk[:], in_=xt[k][:], func=AF.Square, accum_out=ss[k][:, 0:1]),
                 reads=[b_xt[k]], writes=[b_junk, b_ss[k]])
            S.op("dve", lambda e, k=k: e.tensor_scalar(out=ss[k][:, 1:2], in0=ss[k][:, 0:1], scalar1=1.0 / D, scalar2=EPS, op0=ALU.mult, op1=ALU.add),
                 reads=[b_ss[k]], writes=[b_ss[k]])
            S.op("act", lambda e, k=k: e.activation(out=ss[k][:, 2:3], in_=ss[k][:, 1:2], func=AF.Ln), reads=[b_ss[k]], writes=[b_ss[k]])
            S.op("act", lambda e, k=k: e.activation(out=ss[k][:, 3:4], in_=ss[k][:, 2:3], func=AF.Exp, scale=-0.5), reads=[b_ss[k]], writes=[b_ss[k]])
            S.op("dve", lambda e, k=k: e.scalar_tensor_tensor(out=yo[k][:], in0=xt[k][:], scalar=ss[k][:, 3:4], in1=fg[:], op0=ALU.mult, op1=ALU.mult),
                 reads=[b_xt[k], b_ss[k], b_fg], writes=[b_yo[k]])
            S.dma("sp", out_d[tok, :], yo[k][:], reads=[b_yo[k]])
        S.emit()


def host_consts():
    inv = (10000.0 ** (-np.arange(0, 64, 2, dtype=np.float32) / 64.0)).astype(np.float32)
    ang = np.arange(L, dtype=np.float32)[:, None] * inv[None, :]
    ang = np.concatenate([ang, ang], axis=-1)
    cos = np.cos(ang).astype(np.float32).T
    sin = np.sin(ang).astype(np.float32).T
    sgn = np.where(np.arange(64) < 32, -1.0, 1.0).astype(np.float32)[:, None]
    cosT = np.concatenate([cos, cos], axis=0)
    sinT = np.concatenate([sin * sgn, sin * sgn], axis=0)
    r = np.arange(128)
    ident = np.eye(128, dtype=np.float32)
    cbias = np.where(r[None, :] <= r[:, None], 0.0, -1e30).astype(np.float32)
    triu = (r[:, None] <= r[None, :]).astype(np.float32)
    sut = (r[:, None] > r[None, :]).astype(np.float32)
    pow2 = np.tile(((65.0 / 64.0) * 2.0 ** (-np.arange(NBIS + 2, dtype=np.float64))).astype(np.float32)[None, :], (128, 1))
    return dict(cosT=np.ascontiguousarray(cosT), sinT=np.ascontiguousarray(sinT), ident=ident, cbias=cbias,
                triu=triu, sut=sut, pow2=pow2)


def make_in_maps(inputs, n_cores=8):
    f = lambda a: np.ascontiguousarray(np.asarray(a, dtype=np.float32))
    shared = dict(
        w_ada=f(inputs["w_ada"]), b_ada=f(inputs["b_ada"]).reshape(DEPTH, 1, 3 * D),
        norm_g=f(inputs["norm_g"]).reshape(DEPTH, 1, D), w_in=f(inputs["w_in"]),
        conv_wT=f(np.asarray(inputs["conv_w"]).reshape(DEPTH, 4, 12, 128).transpose(0, 3, 2, 1)),
        conv_bT=f(np.asarray(inputs["conv_b"]).reshape(DEPTH, 12, 128).transpose(0, 2, 1)),
        dt_bias=f(inputs["dt_bias"]).reshape(DEPTH, 1, 16), a_log=f(inputs["a_log"]).reshape(DEPTH, 1, 16),
        d_skip=f(inputs["d_skip"]).reshape(DEPTH, 1, 16), ssm_norm_g=f(inputs["ssm_norm_g"]).reshape(DEPTH, 1, D),
        w_branch_a=f(inputs["w_branch_a"]), w_branch_s=f(inputs["w_branch_s"]), w_out=f(inputs["w_out"]),
        final_g=f(inputs["final_g"]).reshape(1, D),
    )
    shared.update(host_consts())
    maps = []
    for b in range(n_cores):
        m = dict(shared)
        m["x"] = f(inputs["x"][b])
        m["cT"] = f(np.asarray(inputs["c"][b]).reshape(8, 128).T)
        maps.append(m)
    return maps


def kernel(**inputs):
    nc = build_program()
    maps = make_in_maps(inputs)
    res = run_bass_kernel_spmd(nc, maps, core_ids=list(range(8)))
    return np.stack([np.asarray(r["out"], dtype=np.float32) for r in res.results], axis=0)
```

```python
import numpy as np
from contextlib import ExitStack
import concourse.bass as bass
import concourse.mybir as mybir
from concourse.bass_utils import run_bass_kernel_spmd

F32 = mybir.dt.float32
BF16 = mybir.dt.bfloat16
AF = mybir.ActivationFunctionType
ALU = mybir.AluOpType
AX = mybir.AxisListType

L = 4096
D = 1024
NT = L // 128
DEPTH = 4
NIN = 6100
Q0, K0, V0, GA0, QI0, KI0, WI0, Z0, XBC0, DT0, GLA0, GLS0 = 0, 512, 576, 640, 1152, 1408, 1472, 1476, 2500, 4036, 4052, 5076
EPS = 1e-6
NBIS = 20
NDMASEM = 12


class Buf:
    __slots__ = ("name", "lw", "rd", "excl")

    def __init__(self, name="", excl=False):
        self.name = name
        self.lw = None
        self.rd = []
        self.excl = excl


class Sched:
    ENGS = ("sp", "act", "pe", "dve", "pool")
    DQ = ("sp", "act", "pool")

    def __init__(self, nc, stack):
        self.nc = nc
        self.batch = 0
        self.ops = {e: [] for e in self.ENGS}
        self.waited = {e: {} for e in self.ENGS}
        self.base = {e: 0 for e in self.ENGS}
        self.esem = {e: stack.enter_context(nc.semaphore("es_" + e)) for e in self.ENGS}
        self.dsem = {}
        self.dcnt = {}
        self.dval = {}
        for q in self.DQ:
            self.dsem[q] = [stack.enter_context(nc.semaphore(f"ds_{q}{i}")) for i in range(NDMASEM)]
            self.dcnt[q] = 0
            self.dval[q] = [0] * NDMASEM
        self.nops = 0

    def _need(self, eng, ev, waits, same_ok=False):
        if ev is None or ev[1] != self.batch:
            return
        if ev[0] == "e":
            _, _, e2, i2 = ev
            if e2 == eng and same_ok:
                return
            self.ops[e2][i2]["inc"] = True
            if self.waited[eng].get(e2, -1) >= i2:
                return
            self.waited[eng][e2] = i2
            waits.append(ev)
        else:
            _, _, q, si, val = ev
            key = (q, si)
            if self.waited[eng].get(key, -1) >= val:
                return
            self.waited[eng][key] = val
            waits.append(ev)

    def _deps(self, eng, reads, writes, waits):
        for r in reads:
            self._need(eng, r.lw, waits, same_ok=(eng == "pe"))
        for w in writes:
            self._need(eng, w.lw, waits, same_ok=(eng == "pe"))
            for ev in w.rd:
                self._need(eng, ev, waits, same_ok=(eng == "pe"))

    def op(self, eng, fn, reads=(), writes=()):
        ex = [r for r in reads if r.excl and r not in writes]
        if ex:
            writes = list(writes) + ex
        waits = []
        self._deps(eng, reads, writes, waits)
        idx = len(self.ops[eng])
        self.ops[eng].append({"fn": fn, "waits": waits, "inc": False, "dma": None})
        ev = ("e", self.batch, eng, idx)
        for r in reads:
            r.rd.append(ev)
        for w in writes:
            w.lw = ev
            w.rd = []
        self.nops += 1
        return ev

    def dma(self, q, out, in_, reads=(), writes=(), **kw):
        waits = []
        self._deps(q, reads, writes, waits)
        n = self.dcnt[q]
        si = n % NDMASEM
        self.dcnt[q] += 1
        prev = self.dval[q][si]
        if prev > 0:
            self._need(q, ("d", self.batch, q, si, prev), waits)
        val = prev + 16
        self.dval[q][si] = val
        ev = ("d", self.batch, q, si, val)
        self.ops[q].append({"fn": (lambda e: e.dma_start(out=out, in_=in_, **kw)), "waits": waits,
                            "inc": False, "dma": (self.dsem[q][si], 16)})
        for r in reads:
            r.rd.append(ev)
        for w in writes:
            w.lw = ev
            w.rd = []
        self.nops += 1
        return ev

    def emit(self, final=False):
        for e in self.ENGS:
            for o in reversed(self.ops[e]):
                if o["fn"] is not None and o["dma"] is None:
                    o["inc"] = True
                    break
            c = self.base[e]
            for o in self.ops[e]:
                if o["inc"]:
                    c += 1
                o["cnt"] = c
        newbase = {e: (self.ops[e][-1]["cnt"] if self.ops[e] else self.base[e]) for e in self.ENGS}
        dvals = {q: list(self.dval[q]) for q in self.DQ}

        def body(engine, ename):
            for o in self.ops[ename]:
                for ev in o["waits"]:
                    if ev[0] == "e":
                        engine.wait_ge(self.esem[ev[2]], self.ops[ev[2]][ev[3]]["cnt"])
                    else:
                        engine.wait_ge(self.dsem[ev[2]][ev[3]], ev[4])
                if o["fn"] is None:
                    continue
                ins = o["fn"](engine)
                if o["dma"] is not None:
                    ins.then_inc(o["dma"][0], o["dma"][1])
                elif o["inc"]:
                    ins.then_inc(self.esem[ename], 1)
            for e2 in self.ENGS:
                if e2 != ename and newbase[e2] > 0:
                    engine.wait_ge(self.esem[e2], newbase[e2])
            for q in self.DQ:
                for si in range(NDMASEM):
                    if dvals[q][si] > 0:
                        engine.wait_ge(self.dsem[q][si], dvals[q][si])

        with self.nc.Block() as block:
            @block.sync
            def _(eng):
                body(eng, "sp")

            @block.scalar
            def _(eng):
                body(eng, "act")

            @block.tensor
            def _(eng):
                body(eng, "pe")

            @block.vector
            def _(eng):
                body(eng, "dve")

            @block.gpsimd
            def _(eng):
                body(eng, "pool")

        self.base = newbase
        self.batch += 1
        self.ops = {e: [] for e in self.ENGS}
        self.waited = {e: {} for e in self.ENGS}


_UID = [0]


def uname(n):
    _UID[0] += 1
    return f"{n}_u{_UID[0]}"


class PsumPool:
    def __init__(self, nc, stack, names):
        self.banks = [stack.enter_context(nc.psum_tensor(uname(n), [128, 512], F32)) for n in names]
        self.bufs = [Buf(n, excl=True) for n in names]
        self.i = 0

    def get(self):
        k = self.i % len(self.banks)
        self.i += 1
        return self.banks[k], self.bufs[k]


def build_program(n_layers=DEPTH, dbg=None):
    nc = bass.Bass("TRN2", target_bir_lowering=False)

    def din(name, shape, dt=F32):
        return nc.dram_tensor(name, list(shape), dt, kind="ExternalInput").ap()

    def dscr(name, shape, dt):
        kind = "ExternalOutput" if (dbg and name in dbg) else "Internal"
        return nc.dram_tensor(name, list(shape), dt, kind=kind).ap()

    x_in = din("x", [L, D])
    cT_d = din("cT", [128, 8])
    w_ada_d = din("w_ada", [DEPTH, D, 3 * D])
    b_ada_d = din("b_ada", [DEPTH, 1, 3 * D])
    norm_g_d = din("norm_g", [DEPTH, 1, D])
    w_in_d = din("w_in", [DEPTH, D, NIN])
    convw_d = din("conv_wT", [DEPTH, 128, 12, 4])
    convb_d = din("conv_bT", [DEPTH, 128, 12])
    dtb_d = din("dt_bias", [DEPTH, 1, 16])
    alog_d = din("a_log", [DEPTH, 1, 16])
    dskip_d = din("d_skip", [DEPTH, 1, 16])
    sng_d = din("ssm_norm_g", [DEPTH, 1, D])
    wa_d = din("w_branch_a", [DEPTH, 512, D])
    ws_d = din("w_branch_s", [DEPTH, D, D])
    wo_d = din("w_out", [DEPTH, D, D])
    fg_d = din("final_g", [1, D])
    cosT_d = din("cosT", [128, L])
    sinT_d = din("sinT", [128, L])
    ident_d = din("ident", [128, 128])
    cbias_d = din("cbias", [128, 128])
    triu_d = din("triu", [128, 128])
    sut_d = din("sut", [128, 128])
    pow2_d = din("pow2", [128, NBIS + 2])
    out_d = nc.dram_tensor("out", [L, D], F32, kind="ExternalOutput").ap()

    X1 = dscr("X1", [L, D], F32)
    QT = dscr("QT", [8, 64, L], BF16)
    KT = dscr("KT", [64, L], BF16)
    QiT = dscr("QiT", [4, 64, L], BF16)
    KiT = dscr("KiT", [64, L], BF16)
    GA = dscr("GA", [8, 64, L], BF16)
    XBC = dscr("XBC", [1536, L], F32)
    GLA = dscr("GLA", [D, L], BF16)
    GLS = dscr("GLS", [D, L], BF16)
    VA = dscr("VA", [L, 65], BF16)
    WI = dscr("WI", [L, 4], F32)
    ZS = dscr("ZS", [L, D], BF16)
    DTs = dscr("DT", [L, 16], F32)
    OG = dscr("OG", [8, 64, L], BF16)
    YN = dscr("YN", [D, L], BF16)
    DBG = dscr("DBG", [128, 4096], F32)

    with ExitStack() as top:
        S = Sched(nc, top)

        def sbt(stack, name, shape, dt):
            return stack.enter_context(nc.sbuf_tensor(uname(name), list(shape), dt))

        ident = sbt(top, "ident", [128, 128], BF16); b_ident = Buf()
        identf = sbt(top, "identf", [128, 128], F32); b_identf = Buf()
        ones_f = sbt(top, "ones_f", [128, 128], F32); b_ones = Buf()
        ones_b = sbt(top, "ones_b", [128, 128], BF16); b_onesb = Buf()
        S.dma("pool", ident[:], ident_d, writes=[b_ident])
        S.dma("sp", identf[:], ident_d, writes=[b_identf])
        S.op("dve", lambda e: e.memset(ones_f[:], 1.0), writes=[b_ones])
        S.op("dve", lambda e: e.memset(ones_b[:], 1.0), writes=[b_onesb])
        S.emit()

        mods = [(sbt(top, "G_b", [128, D], F32), sbt(top, "SH_b", [128, D], F32), sbt(top, "GATE_b", [128, D], F32)) for _ in range(2)]
        for l in range(n_layers):
            x_src = x_in if l == 0 else X1
            with ExitStack() as lay:
                G_b, SH_b, GATE_b = mods[l % 2]
                if l == 0 or (dbg and "no_hoist" in dbg):
                    phase_adaln(nc, S, lay, l, cT_d, w_ada_d, b_ada_d, norm_g_d, G_b, SH_b, GATE_b, ones_f)
                if dbg and "stop_adaln" in dbg:
                    dump(nc, S, DBG, [G_b, SH_b, GATE_b])
                    break
                phase_proj(nc, S, l, x_src, w_in_d, dtb_d, cosT_d, sinT_d, G_b, SH_b, ident,
                           dict(QT=QT, KT=KT, QiT=QiT, KiT=KiT, GA=GA, XBC=XBC, GLA=GLA, GLS=GLS, VA=VA, WI=WI,
                                ZS=ZS, DT=DTs))
                if dbg and "stop_proj" in dbg:
                    break
                if not (dbg and "skip_attn" in dbg):
                    phase_attn(nc, S, l, QT, KT, QiT, KiT, GA, VA, WI, OG, cbias_d, pow2_d, ident_d, ones_f, ident, DBG, dbg)
                if dbg and "stop_attn" in dbg:
                    break
                phase_ssd(nc, S, l, XBC, ZS, DTs, YN, convw_d, convb_d, alog_d, dskip_d, sng_d, triu_d, sut_d,
                          ident, identf, ones_f, ones_b, DBG, dbg)
                if dbg and "stop_ssd" in dbg:
                    break
                nxt = None
                if l + 1 < n_layers and not (dbg and ("no_hoist" in dbg or "stop_layer" in dbg)):
                    Gn, SHn, GATEn = mods[(l + 1) % 2]
                    nxt = lambda stk, l=l, Gn=Gn, SHn=SHn, GATEn=GATEn: phase_adaln(nc, S, None, l + 1, cT_d, w_ada_d, b_ada_d, norm_g_d,
                                                                                   Gn, SHn, GATEn, ones_f, ext_stack=stk)
                phase_out(nc, S, l, x_src, X1, OG, YN, GLA, GLS, wa_d, ws_d, wo_d, GATE_b, nxt)
                if dbg and "stop_layer" in dbg:
                    break
        else:
            phase_final(nc, S, X1 if n_layers > 0 else x_in, fg_d, out_d)
    return nc


def dump(nc, S, DBG, tiles):
    off = 0
    for t in tiles:
        w = t.shape[1]
        S.dma("sp", DBG[0:t.shape[0], off:off + w], t[:])
        off += w
    S.emit()


def phase_adaln(nc, S, lay, l, cT_d, w_ada_d, b_ada_d, norm_g_d, G_b, SH_b, GATE_b, ones_f, ext_stack=None):
    with ExitStack() as st_own:
        st = ext_stack if ext_stack is not None else st_own
        def sb(name, shape, dt=F32):
            return st.enter_context(nc.sbuf_tensor(uname(name), list(shape), dt))
        cT = sb("cT", [128, 8]); b_cT = Buf()
        sc = sb("sc", [128, 8]); b_sc = Buf()
        sg = sb("sg", [128, 8]); b_sg = Buf()
        modrow = sb("modrow", [1, 3 * D]); b_mod = Buf()
        bada = sb("bada", [1, 3 * D]); b_bada = Buf()
        ng = sb("ng", [1, D]); b_ng = Buf()
        grow = sb("grow", [1, D]); b_grow = Buf()
        nwb = 2 if ext_stack is None else 1
        wts = [sb(f"wada{i}", [128, 8, 512]) for i in range(nwb)]
        b_wts = [Buf() for _ in range(nwb)]
        pp = PsumPool(nc, st, ["pa0", "pa1", "pa2", "pa3"] if ext_stack is None else ["pa0", "pa1"])

        S.dma("sp", cT[:], cT_d, writes=[b_cT])
        S.dma("sp", bada[:], b_ada_d[l], writes=[b_bada])
        S.dma("sp", ng[:], norm_g_d[l], writes=[b_ng])
        S.op("act", lambda e: e.activation(out=sg[:], in_=cT[:], func=AF.Sigmoid), reads=[b_cT], writes=[b_sg])
        S.op("dve", lambda e: e.tensor_tensor(out=sc[:], in0=cT[:], in1=sg[:], op=ALU.mult), reads=[b_cT, b_sg], writes=[b_sc])
        wv = w_ada_d[l].rearrange("(kc p) n -> p kc n", p=128)
        for nb in range(6):
            wt, bw = wts[nb % nwb], b_wts[nb % nwb]
            S.dma("sp", wt[:], wv[:, :, nb * 512:(nb + 1) * 512], writes=[bw])
            ps, bp = pp.get()
            for kc in range(8):
                S.op("pe", lambda e, kc=kc, wt=wt, ps=ps: e.matmul(ps[0:1, :], lhsT=sc[:, kc:kc + 1], rhs=wt[:, kc, :],
                                                                  start=(kc == 0), stop=(kc == 7)),
                     reads=[b_sc, bw], writes=[bp])
            S.op("dve", lambda e, nb=nb, ps=ps: e.tensor_tensor(out=modrow[:, nb * 512:(nb + 1) * 512], in0=ps[0:1, :],
                                                                in1=bada[:, nb * 512:(nb + 1) * 512], op=ALU.add),
                 reads=[bp, b_bada], writes=[b_mod])
        S.op("dve", lambda e: e.scalar_tensor_tensor(out=grow[:], in0=modrow[:, D:2 * D], scalar=1.0, in1=ng[:],
                                                     op0=ALU.add, op1=ALU.mult),
             reads=[b_mod, b_ng], writes=[b_grow])
        b_dst = Buf()
        for (row, bsrc, dst) in ((grow[:, :], b_grow, G_b), (modrow[:, 0:D], b_mod, SH_b), (modrow[:, 2 * D:3 * D], b_mod, GATE_b)):
            for hb in range(2):
                ps, bp = pp.get()
                S.op("pe", lambda e, ps=ps, row=row, hb=hb: e.matmul(ps[:], lhsT=ones_f[0:1, :], rhs=row[:, hb * 512:(hb + 1) * 512],
                                                                     start=True, stop=True),
                     reads=[bsrc], writes=[bp])
                S.op("act", lambda e, ps=ps, dst=dst, hb=hb: e.activation(out=dst[:, hb * 512:(hb + 1) * 512], in_=ps[:], func=AF.Copy),
                     reads=[bp], writes=[b_dst])
        if ext_stack is None:
            S.emit()


def phase_proj(nc, S, l, x_src, w_in_d, dtb_d, cosT_d, sinT_d, G_b, SH_b, ident, dst):
    with ExitStack() as st:
        def sb(name, shape, dt=F32):
            return st.enter_context(nc.sbuf_tensor(uname(name), list(shape), dt))
        hT = sb("hT", [128, 8, L], BF16)
        b_hT = [Buf() for _ in range(NT)]
        cosT = sb("cosT", [128, L], BF16); b_cos = Buf()
        sinT = sb("sinT", [128, L], BF16); b_sin = Buf()
        S.dma("pool", cosT[:], cosT_d, writes=[b_cos])
        S.dma("pool", sinT[:], sinT_d, writes=[b_sin])
        tps = [st.enter_context(nc.psum_tensor(uname("tp"), [128, 1024], BF16)) for _ in range(2)]
        b_tps = [Buf(excl=True), Buf(excl=True)]
        pp = PsumPool(nc, st, [f"pj{i}" for i in range(6)])
        NB = 4
        xt = [sb("xt", [128, D]) for _ in range(NB)]; b_xt = [Buf() for _ in range(NB)]
        junk = sb("junk", [128, D], BF16); b_junk = Buf()
        ss = [sb("ss", [128, 4]) for _ in range(NB)]; b_ss = [Buf() for _ in range(NB)]
        h1 = [sb("h1", [128, D]) for _ in range(NB)]; b_h1 = [Buf() for _ in range(NB)]
        hb = [sb("hb", [128, D], BF16) for _ in range(NB)]; b_hb = [Buf() for _ in range(NB)]
        def stage_x(tt):
            k = tt % NB
            tok = slice(tt * 128, (tt + 1) * 128)
            S.dma("sp", xt[k][:], x_src[tok, :], writes=[b_xt[k]])
            S.op("act", lambda e: e.activation(out=junk[:], in_=xt[k][:], func=AF.Square, accum_out=ss[k][:, 0:1]),
                 reads=[b_xt[k]], writes=[b_junk, b_ss[k]])
            S.op("dve", lambda e: e.tensor_scalar(out=ss[k][:, 1:2], in0=ss[k][:, 0:1], scalar1=1.0 / D, scalar2=EPS,
                                                  op0=ALU.mult, op1=ALU.add), reads=[b_ss[k]], writes=[b_ss[k]])
            S.op("act", lambda e: e.activation(out=ss[k][:, 2:3], in_=ss[k][:, 1:2], func=AF.Ln), reads=[b_ss[k]], writes=[b_ss[k]])
            S.op("act", lambda e: e.activation(out=ss[k][:, 3:4], in_=ss[k][:, 2:3], func=AF.Exp, scale=-0.5), reads=[b_ss[k]], writes=[b_ss[k]])
            S.op("dve", lambda e: e.scalar_tensor_tensor(out=h1[k][:], in0=xt[k][:], scalar=ss[k][:, 3:4], in1=G_b[:],
                                                         op0=ALU.mult, op1=ALU.mult),
                 reads=[b_xt[k], b_ss[k]], writes=[b_h1[k]])
            S.op("pool", lambda e: e.tensor_tensor(out=hb[k][:], in0=h1[k][:], in1=SH_b[:], op=ALU.add),
                 reads=[b_h1[k]], writes=[b_hb[k]])

        def stage_y(tt):
            k = tt % NB
            tok = slice(tt * 128, (tt + 1) * 128)
            tp, btp = tps[tt % 2], b_tps[tt % 2]
            for kc in range(8):
                S.op("pe", lambda e, kc=kc: e.transpose(out=tp[:, kc * 128:(kc + 1) * 128],
                                                        in_=hb[k][:, kc * 128:(kc + 1) * 128], identity=ident[:]),
                     reads=[b_hb[k]], writes=[btp])
            S.op("act", lambda e: e.activation(out=hT[:, :, tok], in_=tp[:].rearrange("p (k t) -> p k t", k=8), func=AF.Copy),
                 reads=[btp], writes=[b_hT[tt]])

        stage_x(0)
        stage_x(1)
        for tt in range(NT):
            if tt + 2 < NT:
                stage_x(tt + 2)
            stage_y(tt)

        wv = w_in_d[l].rearrange("(kc p) n -> p kc n", p=128)
        tiles = []
        for t in range(4):
            tiles.append(dict(cols=[(Q0 + 128 * t, 64), (Q0 + 128 * t + 64, 64)], rope=True, kind="rope",
                              dst=[dst["QT"][2 * t], dst["QT"][2 * t + 1]]))
        for t in range(2):
            tiles.append(dict(cols=[(QI0 + 128 * t, 64), (QI0 + 128 * t + 64, 64)], rope=True, kind="rope",
                              dst=[dst["QiT"][2 * t], dst["QiT"][2 * t + 1]]))
        tiles.append(dict(cols=[(K0, 64), (KI0, 64)], rope=True, kind="rope", dst=[dst["KT"], dst["KiT"]]))
        for t in range(4):
            tiles.append(dict(cols=[(GA0 + 128 * t, 128)], rope=False, kind="silu",
                              dst=[dst["GA"][2 * t], dst["GA"][2 * t + 1]]))
        for t in range(12):
            tiles.append(dict(cols=[(XBC0 + 128 * t, 128)], rope=False, kind="copy", dst=[dst["XBC"][t * 128:(t + 1) * 128]]))
        for t in range(8):
            tiles.append(dict(cols=[(GLA0 + 128 * t, 128)], rope=False, kind="sig", dst=[dst["GLA"][t * 128:(t + 1) * 128]]))
        for t in range(8):
            tiles.append(dict(cols=[(GLS0 + 128 * t, 128)], rope=False, kind="sig", dst=[dst["GLS"][t * 128:(t + 1) * 128]]))
        wts = [sb("wt", [128, 8, 128], BF16) for _ in range(2)]
        wtps = [sb("wtp", [128, 8, 128], BF16) for _ in range(2)]
        b_w = [[Buf() for _ in range(2)] for _ in range(2)]
        b_wp = [[Buf() for _ in range(4)] for _ in range(2)]
        NO = 3
        t1 = [sb("t1", [128, 512]) for _ in range(2)]; b_t1 = [Buf(), Buf()]
        t2 = [sb("t2", [128, 512]) for _ in range(2)]; b_t2 = [Buf(), Buf()]
        ob = [sb("ob", [128, 512], BF16) for _ in range(NO)]; b_ob = [Buf() for _ in range(NO)]
        of = [sb("of", [128, 512]) for _ in range(NO)]; b_of = [Buf() for _ in range(NO)]
        cnt = 0
        def load_w(ti):
            T = tiles[ti]
            w = ti % 2
            wt, wtp = wts[w], wtps[w]
            o = 0
            rb = []
            for ci, (c0, n) in enumerate(T["cols"]):
                S.dma("pool", wt[:, :, o:o + n], wv[:, :, c0:c0 + n], writes=[b_w[w][ci]])
                rb.append(b_w[w][ci])
                o += n
            rbp = []
            if T["rope"]:
                o = 0
                for ci, (c0, n) in enumerate(T["cols"]):
                    S.dma("pool", wtp[:, :, o:o + 32], wv[:, :, c0 + 32:c0 + 64], writes=[b_wp[w][2 * ci]])
                    S.dma("pool", wtp[:, :, o + 32:o + 64], wv[:, :, c0:c0 + 32], writes=[b_wp[w][2 * ci + 1]])
                    rbp += [b_wp[w][2 * ci], b_wp[w][2 * ci + 1]]
                    o += 64
            return rb, rbp

        nxt_w = load_w(0)
        for ti, T in enumerate(tiles):
            w = ti % 2
            wt, wtp = wts[w], wtps[w]
            rb, rbp = nxt_w
            if ti + 1 < len(tiles):
                nxt_w = load_w(ti + 1)
            for c in range(8):
                tok = slice(c * 512, (c + 1) * 512)
                hbufs = b_hT[4 * c:4 * c + 4]
                ps, bp = pp.get()
                for kc in range(8):
                    S.op("pe", lambda e, ps=ps, wt=wt, kc=kc, tok=tok: e.matmul(ps[:], lhsT=wt[:, kc, :], rhs=hT[:, kc, tok],
                                                                               start=(kc == 0), stop=(kc == 7)),
                         reads=rb + hbufs, writes=[bp])
                kind = T["kind"]
                if kind == "rope":
                    psp, bpp = pp.get()
                    for kc in range(8):
                        S.op("pe", lambda e, psp=psp, wtp=wtp, kc=kc, tok=tok: e.matmul(psp[:], lhsT=wtp[:, kc, :], rhs=hT[:, kc, tok],
                                                                                     start=(kc == 0), stop=(kc == 7)),
                             reads=rbp + hbufs, writes=[bpp])
                    a = cnt % 2
                    k = cnt % NO
                    S.op("dve", lambda e, a=a, ps=ps, tok=tok: e.tensor_tensor(out=t1[a][:], in0=ps[:], in1=cosT[:, tok], op=ALU.mult),
                         reads=[bp, b_cos], writes=[b_t1[a]])
                    S.op("dve", lambda e, a=a, psp=psp, tok=tok: e.tensor_tensor(out=t2[a][:], in0=psp[:], in1=sinT[:, tok], op=ALU.mult),
                         reads=[bpp, b_sin], writes=[b_t2[a]])
                    S.op("pool", lambda e, a=a, k=k: e.tensor_tensor(out=ob[k][:], in0=t1[a][:], in1=t2[a][:], op=ALU.add),
                         reads=[b_t1[a], b_t2[a]], writes=[b_ob[k]])
                    S.dma("sp", T["dst"][0][:, tok], ob[k][0:64, :], reads=[b_ob[k]])
                    S.dma("sp", T["dst"][1][:, tok], ob[k][64:128, :], reads=[b_ob[k]])
                elif kind == "copy":
                    k = cnt % NO
                    S.op("act", lambda e, k=k, ps=ps: e.activation(out=of[k][:], in_=ps[:], func=AF.Copy), reads=[bp], writes=[b_of[k]])
                    S.dma("sp", T["dst"][0][:, tok], of[k][:], reads=[b_of[k]])
                else:
                    k = cnt % NO
                    fn = AF.Silu if kind == "silu" else AF.Sigmoid
                    S.op("act", lambda e, k=k, ps=ps, fn=fn: e.activation(out=ob[k][:], in_=ps[:], func=fn), reads=[bp], writes=[b_ob[k]])
                    if len(T["dst"]) == 2:
                        S.dma("sp", T["dst"][0][:, tok], ob[k][0:64, :], reads=[b_ob[k]])
                        S.dma("sp", T["dst"][1][:, tok], ob[k][64:128, :], reads=[b_ob[k]])
                    else:
                        S.dma("sp", T["dst"][0][:, tok], ob[k][:], reads=[b_ob[k]])
                cnt += 1
        wz = sb("wz", [128, 8, 1024], BF16); b_wz = Buf()
        S.dma("pool", wz[:], wv[:, :, Z0:Z0 + 1024], writes=[b_wz])
        wsm = sb("wsm", [128, 8, 84], BF16); b_wsm = [Buf() for _ in range(3)]
        S.dma("pool", wsm[:, :, 0:64], wv[:, :, V0:V0 + 64], writes=[b_wsm[0]])
        S.dma("pool", wsm[:, :, 64:68], wv[:, :, WI0:WI0 + 4], writes=[b_wsm[1]])
        S.dma("pool", wsm[:, :, 68:84], wv[:, :, DT0:DT0 + 16], writes=[b_wsm[2]])
        dtb = sb("dtb", [128, 16]); b_dtb = Buf()
        S.dma("sp", dtb[:], dtb_d[l].partition_broadcast(128), writes=[b_dtb])
        zb = [sb("zb", [128, D], BF16) for _ in range(2)]; b_zb = [Buf(), Buf()]
        va = [sb("va", [128, 65], BF16) for _ in range(2)]; b_va = [Buf(), Buf()]
        wib = [sb("wib", [128, 4]) for _ in range(2)]; b_wib = [Buf(), Buf()]
        wiall = sb("wiall", [128, NT, 4]); b_wiall = Buf()
        dtall = sb("dtall", [128, 2, NT, 16]); b_dtall = Buf()
        for tt in range(NT):
            k = tt % 2
            tok = slice(tt * 128, (tt + 1) * 128)
            for nb in range(2):
                ps, bp = pp.get()
                for kc in range(8):
                    S.op("pe", lambda e, ps=ps, kc=kc, tok=tok, nb=nb: e.matmul(ps[:], lhsT=hT[:, kc, tok], rhs=wz[:, kc, nb * 512:(nb + 1) * 512],
                                                                               start=(kc == 0), stop=(kc == 7)),
                         reads=[b_wz, b_hT[tt]], writes=[bp])
                S.op("act", lambda e, k=k, ps=ps, nb=nb: e.activation(out=zb[k][:, nb * 512:(nb + 1) * 512], in_=ps[:], func=AF.Silu),
                     reads=[bp], writes=[b_zb[k]])
            S.dma("sp", dst["ZS"][tok, :], zb[k][:], reads=[b_zb[k]])
            ps, bp = pp.get()
            for kc in range(8):
                S.op("pe", lambda e, ps=ps, kc=kc, tok=tok: e.matmul(ps[:, 0:84], lhsT=hT[:, kc, tok], rhs=wsm[:, kc, :],
                                                                    start=(kc == 0), stop=(kc == 7)),
                     reads=b_wsm + [b_hT[tt]], writes=[bp])
            S.op("pool", lambda e, k=k: e.memset(va[k][:, 64:65], 1.0), writes=[b_va[k]])
            S.op("act", lambda e, k=k, ps=ps: e.activation(out=va[k][:, 0:64], in_=ps[:, 0:64], func=AF.Copy), reads=[bp], writes=[b_va[k]])
            S.dma("sp", dst["VA"][tok, :], va[k][:], reads=[b_va[k]])
            S.op("dve", lambda e, ps=ps, tt=tt: e.tensor_scalar(out=wiall[:, tt, :], in0=ps[:, 64:68], scalar1=0.5, scalar2=None, op0=ALU.mult),
                 reads=[bp], writes=[b_wiall])
            S.op("dve", lambda e, ps=ps, tt=tt: e.tensor_tensor(out=dtall[:, 0, tt, :], in0=ps[:, 68:84], in1=dtb[:], op=ALU.add),
                 reads=[bp, b_dtb], writes=[b_dtall])
        S.dma("sp", dst["WI"].rearrange("(j p) d -> p j d", p=128), wiall[:], reads=[b_wiall])
        S.op("act", lambda e: e.activation(out=dtall[:, 1, :, :], in_=dtall[:, 0, :, :], func=AF.Exp), reads=[b_dtall], writes=[b_dtall])
        S.op("dve", lambda e: e.tensor_scalar(out=dtall[:, 0, :, :], in0=dtall[:, 1, :, :], scalar1=1.0, scalar2=None, op0=ALU.add),
             reads=[b_dtall], writes=[b_dtall])
        S.op("act", lambda e: e.activation(out=dtall[:, 1, :, :], in_=dtall[:, 0, :, :], func=AF.Ln), reads=[b_dtall], writes=[b_dtall])
        S.dma("sp", dst["DT"].rearrange("(j p) h -> p j h", p=128), dtall[:, 1, :, :], reads=[b_dtall])
        S.emit()


def phase_attn(nc, S, l, QT, KT, QiT, KiT, GA, VA, WI, OG, cbias_d, pow2_d, ident_d, ones_f, ident_bf, DBG, dbg):
    import os
    with ExitStack() as st:
        def sb(name, shape, dt=F32):
            return st.enter_context(nc.sbuf_tensor(uname(name), list(shape), dt))
        KTa = sb("KTa", [65, L], BF16); b_KTa = Buf()
        KiTs = sb("KiTs", [64, L], BF16); b_KiT = Buf()
        VAs = sb("VAs", [128, NT, 65], BF16); b_VA = Buf()
        WIs = sb("WIs", [128, NT, 4]); b_WI = Buf()
        cb = sb("cb", [128, 128]); b_cb = Buf()
        pw = sb("pw", [128, NBIS + 2]); b_pw = Buf()
        Irep = sb("Irep", [128, 8, 128], BF16); b_Irep = [Buf() for _ in range(8)]
        sel = sb("sel", [64, 65], BF16); b_sel = Buf()
        ksq = sb("ksq", [64, L], BF16); b_ksq = Buf()
        km = sb("km", [65, 16]); b_km = Buf()
        S.dma("sp", KTa[0:64, :], KT, writes=[b_KTa])
        b_KTa1 = Buf()
        S.op("dve", lambda e: e.memset(KTa[64:65, :], 1.0), reads=[], writes=[b_KTa1])
        S.dma("sp", KiTs[:], KiT, writes=[b_KiT])
        S.dma("sp", VAs[:], VA.rearrange("(j p) d -> p j d", p=128), writes=[b_VA])
        S.dma("sp", WIs[:], WI.rearrange("(j p) d -> p j d", p=128), writes=[b_WI])
        S.dma("sp", cb[:], cbias_d, writes=[b_cb])
        S.dma("sp", pw[:], pow2_d, writes=[b_pw])
        for h in range(8):
            S.dma("pool", Irep[:, h, :], ident_d, writes=[b_Irep[h]])
        S.op("dve", lambda e: e.memset(sel[:], 0.0), writes=[b_sel])
        S.op("dve", lambda e: e.memset(sel[:, 64:65], 1.0), writes=[b_sel])
        pO = [st.enter_context(nc.psum_tensor(uname("pO"), [128, 512], F32)) for _ in range(2)]
        b_pO = [Buf(excl=True), Buf(excl=True)]
        pST = PsumPool(nc, st, [f"pst{i}" for i in range(2)])
        pX = PsumPool(nc, st, ["px0", "px1", "px2"])
        pS = PsumPool(nc, st, ["psc0"])
        S.op("act", lambda e: e.activation(out=ksq[:], in_=KTa[0:64, :], func=AF.Square), reads=[b_KTa], writes=[b_ksq])
        for c in range(8):
            ps, bp = pX.get()
            S.op("pe", lambda e, ps=ps, c=c: e.matmul(ps[0:65, :], lhsT=sel[:], rhs=ksq[:, c * 512:(c + 1) * 512], start=True, stop=True),
                 reads=[b_sel, b_ksq], writes=[bp])
            S.op("dve", lambda e, ps=ps, c=c: e.tensor_reduce(out=km[64:65, c:c + 1], in_=ps[64:65, :], axis=AX.X, op=ALU.max),
                 reads=[bp], writes=[b_km])
        S.op("dve", lambda e: e.tensor_reduce(out=km[64:65, 8:9], in_=km[64:65, 0:8], axis=AX.X, op=ALU.max), reads=[b_km], writes=[b_km])

        NB = 2
        NR = 4
        qa = [sb("qa", [65, 8, 128], BF16) for _ in range(NR)]; b_qa = [Buf() for _ in range(NR)]; b_qar = [Buf() for _ in range(NR)]
        qi = [sb("qi", [64, 4, 128], BF16) for _ in range(NB)]; b_qi = [Buf() for _ in range(NB)]
        ga = [sb("ga", [64, 8, 128], BF16) for _ in range(2)]; b_ga = [Buf() for _ in range(2)]
        qsq = sb("qsq", [64, 1024], BF16); b_qsq = Buf()
        qtmp = sb("qtmp", [65, 1024]); b_qtmp = Buf()
        wab = [sb("wab", [128, 8]) for _ in range(NB)]; b_wab = [Buf() for _ in range(NB)]
        score = [sb("score", [128, L]) for _ in range(NB)]; b_score = [Buf() for _ in range(NB)]
        mb = [sb("mb", [128, L], BF16) for _ in range(NR)]; b_mb = [Buf() for _ in range(NR)]
        junk = [sb("junkb", [128, L], BF16) for _ in range(NB)]; b_junk = [Buf() for _ in range(NB)]
        NRL = 8
        rl = [sb("rl", [128, 512], BF16) for _ in range(NRL)]; b_rl = [Buf() for _ in range(NRL)]
        Dh = [sb("Dh", [128, 4, 128], BF16) for _ in range(NB)]; b_Dh = [Buf() for _ in range(NB)]
        bs = [sb("bs", [128, 8]) for _ in range(NB)]; b_bs = [Buf() for _ in range(NB)]
        ab = [sb("ab", [128, 8]) for _ in range(NB)]; b_ab = [Buf() for _ in range(NB)]
        p2 = sb("p2", [128, 2]); b_p2 = Buf()
        q2 = sb("q2", [128, 2]); b_q2 = Buf()
        C2 = sb("C2", [128, 2, 4]); b_cnt = [Buf(), Buf()]; b_t2 = Buf()
        ab2 = sb("ab2", [128, 2]); b_ab2 = [Buf(), Buf()]
        Sp2 = sb("Sp2", [128, 2, NBIS + 2]); b_S2 = Buf()
        b_junkA = [Buf() for _ in range(NB)]
        ALPHA = float(os.environ.get("ATT_ALPHA", "0.7"))
        Sp = [sb("Sp", [128, NBIS + 2]) for _ in range(NB)]
        Sn = [sb("Sn", [128, NBIS + 2]) for _ in range(NB)]
        b_S = [Buf() for _ in range(NB)]
        PT = [sb("PT", [128, 1024], BF16) for _ in range(3)]; b_PT = [Buf() for _ in range(3)]
        rr = sb("rr", [65, 1024]); b_rr = Buf()
        rbc = sb("rbc", [64, 1024]); b_rbc = Buf()
        o1 = sb("o1", [64, 1024]); b_o1 = Buf()
        ogb = [sb("ogb", [64, 8, 128], BF16) for _ in range(2)]; b_ogb = [Buf(), Buf()]
        rlc = [0]
        ptc = [0]

        AMX_ = os.environ.get("ATT_AMX", "dve")
        ACT_EVERY = int(os.environ.get("ATT_ACT_EVERY", "4"))

        def on_act(i):
            return (i % ACT_EVERY == ACT_EVERY - 1) and not (dbg and "no_act_bis" in dbg)

        def pre(i):
            k = i % NB
            k4 = i % NR
            n = 128 * (i + 1)
            tok = slice(i * 128, (i + 1) * 128)
            S.dma("sp", qa[k4][0:64, :, :], QT[:, :, tok].rearrange("h d t -> d h t"), writes=[b_qa[k4]])
            S.dma("sp", qi[k][:], QiT[:, :, tok].rearrange("h d t -> d h t"), writes=[b_qi[k]])
            S.op("act", lambda e: e.activation(out=wab[k][:, 0:4], in_=WIs[:, i, :], func=AF.Abs, scale=0.125),
                 reads=[b_WI], writes=[b_wab[k]])
            S.op("dve", lambda e: e.tensor_scalar(out=wab[k][:, 4:8], in0=WIs[:, i, :], scalar1=0.0, scalar2=2.0,
                                                  op0=ALU.is_ge, op1=ALU.mult), reads=[b_WI], writes=[b_wab[k]])
            S.op("dve", lambda e: e.tensor_scalar(out=wab[k][:, 4:8], in0=wab[k][:, 4:8], scalar1=-1.0, scalar2=None,
                                                  op0=ALU.add), reads=[b_wab[k]], writes=[b_wab[k]])
            S.op("act", lambda e: e.activation(out=qsq[:], in_=qa[k4][0:64, :, :].rearrange("p h t -> p (h t)"), func=AF.Square),
                 reads=[b_qa[k4]], writes=[b_qsq])
            for hb in range(2):
                ps, bp = pX.get()
                S.op("pe", lambda e, ps=ps, hb=hb: e.matmul(ps[0:65, :], lhsT=sel[:], rhs=qsq[:, hb * 512:(hb + 1) * 512], start=True, stop=True),
                     reads=[b_sel, b_qsq], writes=[bp])
                S.op("act", lambda e, ps=ps, hb=hb: e.activation(out=qtmp[64:65, hb * 512:(hb + 1) * 512], in_=ps[64:65, :], func=AF.Sqrt,
                                                                 scale=km[64:65, 8:9]), reads=[bp, b_km], writes=[b_qtmp])
                S.op("dve", lambda e, hb=hb: e.tensor_scalar(out=qa[k4][64:65, hb * 4:(hb + 1) * 4, :].rearrange("p h t -> p (h t)"),
                                                             in0=qtmp[64:65, hb * 512:(hb + 1) * 512], scalar1=-1.0, scalar2=None, op0=ALU.mult),
                     reads=[b_qtmp], writes=[b_qar[k4]])
            S.op("dve", lambda e: e.tensor_tensor(out=Dh[k][:], in0=ident_bf[:].unsqueeze(1).to_broadcast([128, 4, 128]),
                                                  in1=wab[k][:, 4:8].unsqueeze(2).to_broadcast([128, 4, 128]), op=ALU.mult),
                 reads=[b_wab[k]], writes=[b_Dh[k]])
            def dots(c0):
                w = min(512, n - c0)
                rs = []
                for h in range(4):
                    ps, bp = pX.get()
                    S.op("pe", lambda e, ps=ps, h=h, c0=c0, w=w: e.matmul(ps[:, 0:w], lhsT=qi[k][:, h, :], rhs=KiTs[:, c0:c0 + w], start=True, stop=True),
                         reads=[b_qi[k], b_KiT], writes=[bp])
                    r = rlc[0] % NRL
                    rlc[0] += 1
                    rs.append(r)
                    if h < 2:
                        S.op("act", lambda e, ps=ps, h=h, w=w, r=r: e.activation(out=rl[r][:, 0:w], in_=ps[:, 0:w], func=AF.Relu, scale=wab[k][:, h:h + 1]),
                             reads=[bp, b_wab[k]], writes=[b_rl[r]])
                    else:
                        S.op("dve", lambda e, ps=ps, h=h, w=w, r=r: e.tensor_scalar(out=rl[r][:, 0:w], in0=ps[:, 0:w], scalar1=wab[k][:, h:h + 1], scalar2=0.0,
                                                                                   op0=ALU.mult, op1=ALU.max),
                             reads=[bp, b_wab[k]], writes=[b_rl[r]])
                return (c0, w, rs)

            def signs(c0, w, rs):
                pss, bps = pS.get()
                for h in range(4):
                    r = rs[h]
                    S.op("pe", lambda e, pss=pss, h=h, w=w, r=r: e.matmul(pss[:, 0:w], lhsT=Dh[k][:, h, :], rhs=rl[r][:, 0:w], start=(h == 0), stop=(h == 3)),
                         reads=[b_Dh[k], b_rl[r]], writes=[bps])
                if (c0 // 512) % 2 == 0:
                    S.op("act", lambda e, pss=pss, w=w, c0=c0: e.activation(out=score[k][:, c0:c0 + w], in_=pss[:, 0:w], func=AF.Copy),
                         reads=[bps], writes=[b_score[k]])
                else:
                    S.op("dve", lambda e, pss=pss, w=w, c0=c0: e.tensor_copy(out=score[k][:, c0:c0 + w], in_=pss[:, 0:w]),
                         reads=[bps], writes=[b_score[k]])

            pend = None
            for c0 in range(0, n, 512):
                cur = dots(c0)
                if pend is not None:
                    signs(*pend)
                pend = cur
            signs(*pend)
            B = bs[k]
            S.op(AMX_, lambda e: e.tensor_reduce(out=B[:, 0:1], in_=score[k][:, 0:n], axis=AX.X, op=ALU.max, apply_absolute_value=True),
                 reads=[b_score[k]], writes=[b_bs[k]])
            S.op("dve", lambda e: e.tensor_scalar(out=B[:, 1:2], in0=B[:, 0:1], scalar1=1.001, scalar2=1e-6, op0=ALU.mult, op1=ALU.add),
                 reads=[b_bs[k]], writes=[b_bs[k]])
            S.op("dve", lambda e: e.tensor_tensor(out=score[k][:, i * 128:(i + 1) * 128], in0=score[k][:, i * 128:(i + 1) * 128], in1=cb[:], op=ALU.add),
                 reads=[b_score[k], b_cb], writes=[b_score[k]])
            S.op("dve", lambda e: e.tensor_scalar(out=Sp2[:, k, :], in0=pw[:], scalar1=B[:, 1:2], scalar2=None, op0=ALU.mult),
                 reads=[b_bs[k], b_pw], writes=[b_S2])
            S.op("dve", lambda e: e.tensor_scalar(out=q2[:, k:k + 1], in0=B[:, 1:2], scalar1=-1.0, scalar2=None, op0=ALU.mult),
                 reads=[b_bs[k]], writes=[b_q2])
            S.op("dve", lambda e: e.tensor_tensor(out=p2[:, k:k + 1], in0=Sp2[:, k, 0:1], in1=q2[:, k:k + 1], op=ALU.add),
                 reads=[b_S2, b_q2], writes=[b_p2])
            if split(n) == n:
                S.op("dve", lambda e: e.memset(ab2[:, k:k + 1], 0.0), writes=[b_ab2[k]])

        def split(n):
            nA = int(round(ALPHA * n / 128.0)) * 128
            nA = max(128, min(n, nA))
            if n - nA < 256:
                nA = n
            return nA

        def bis_iter_group(tiles, j):
            for i in tiles:
                k = i % NB
                n = 128 * (i + 1)
                nA = split(n)
                nB = n - nA
                S.op("dve", lambda e, k=k, nA=nA, nB=nB: e.tensor_scalar(out=junk[k][:, 0:nA], in0=score[k][:, 0:nA], scalar1=p2[:, k:k + 1],
                                                                        scalar2=float(-(256.0 - nB / 2.0)), op0=ALU.is_ge, op1=ALU.add,
                                                                        accum_out=C2[:, k, 0:1]),
                     reads=[b_score[k], b_p2], writes=[b_junk[k], b_cnt[k]])
                if nB > 0:
                    S.op("act", lambda e, k=k, nA=nA, n=n: e.activation(out=junk[k][:, nA:n], in_=score[k][:, nA:n], func=AF.Sign, bias=p2[:, k:k + 1], scale=-1.0,
                                                                         accum_out=ab2[:, k:k + 1]), reads=[b_score[k], b_p2], writes=[b_junkA[k], b_ab2[k]])
            S.op("dve", lambda e: e.scalar_tensor_tensor(out=C2[:, :, 1], in0=ab2[:], scalar=-0.5, in1=C2[:, :, 0], op0=ALU.mult, op1=ALU.add),
                 reads=b_ab2 + b_cnt, writes=[b_t2])
            S.op("dve", lambda e: e.scalar_tensor_tensor(out=C2[:, :, 2], in0=C2[:, :, 1], scalar=0.0, in1=Sp2[:, :, j], op0=ALU.is_ge, op1=ALU.mult),
                 reads=[b_t2, b_S2], writes=[b_t2])
            S.op("dve", lambda e: e.tensor_tensor(out=q2[:], in0=q2[:], in1=C2[:, :, 2], op=ALU.add), reads=[b_q2, b_t2], writes=[b_q2])
            S.op("dve", lambda e: e.tensor_tensor(out=p2[:], in0=q2[:], in1=Sp2[:, :, j + 1], op=ALU.add), reads=[b_q2, b_S2], writes=[b_p2])

        def post(i):
            k = i % NB
            k4 = i % NR
            n = 128 * (i + 1)
            B = bs[k]
            S.op("dve", lambda e: e.tensor_scalar(out=mb[k4][:, 0:n], in0=score[k][:, 0:n], scalar1=q2[:, k:k + 1], scalar2=-30000.0,
                                                  op0=ALU.is_lt, op1=ALU.mult), reads=[b_score[k], b_q2], writes=[b_mb[k4]])
            if dbg and "dump_attn" in dbg and i == dbg["dump_attn"]:
                S.dma("sp", DBG[:, 0:n], score[k][:, 0:n], reads=[b_score[k]])
                S.dma("sp", DBG[:, 4080:4082], bs[k][:, 0:2], reads=[b_bs[k]])
                S.dma("sp", DBG[:, 4085:4086], q2[:, k:k + 1], reads=[b_q2], allow_slow_non_contiguous=True)

        def stage1_group(tiles, units):
            nslots = len(tiles) + NBIS
            per = -(-len(units) // nslots) if units else 0
            pos = [0]

            def drain(final=False):
                m = len(units) if final else min(len(units), pos[0] + per)
                while pos[0] < m:
                    units[pos[0]]()
                    pos[0] += 1

            for t in tiles:
                pre(t)
                drain()
            for j in range(NBIS):
                drain()
                bis_iter_group(tiles, j)
            for t in tiles:
                post(t)
            drain(final=True)

        def stage2_units(i):
            k = i % NR
            tok = slice(i * 128, (i + 1) * 128)
            g = i % 2
            units = []

            pidx = {}

            def block(j):
                if j == 0:
                    S.dma("sp", ga[g][:], GA[:, :, tok].rearrange("h d t -> d h t"), writes=[b_ga[g]])
                ks = slice(j * 128, (j + 1) * 128)
                p = ptc[0] % 3
                ptc[0] += 1
                pidx[j] = p
                for hb in range(2):
                    ps, bp = pST.get()
                    S.op("pe", lambda e, ps=ps, hb=hb, ks=ks: e.matmul(ps[:], lhsT=KTa[:, ks], rhs=qa[k][:, hb * 4:(hb + 1) * 4, :], start=True, stop=False),
                         reads=[b_KTa, b_KTa1, b_qa[k], b_qar[k]], writes=[bp])
                    S.op("pe", lambda e, ps=ps, hb=hb, ks=ks: e.matmul(ps[:], lhsT=mb[k][:, ks], rhs=Irep[:, hb * 4:(hb + 1) * 4, :], start=False, stop=True),
                         reads=[b_mb[k]] + b_Irep, writes=[bp])
                    S.op("act", lambda e, ps=ps, hb=hb, p=p: e.activation(out=PT[p][:, hb * 512:(hb + 1) * 512], in_=ps[:], func=AF.Exp, scale=0.125),
                         reads=[bp], writes=[b_PT[p]])

            def pv(j):
                p = pidx[j]
                for hb in range(2):
                    S.op("pe", lambda e, hb=hb, p=p, j=j: e.matmul(pO[hb][0:65, :], lhsT=VAs[:, j, :], rhs=PT[p][:, hb * 512:(hb + 1) * 512],
                                                                  start=(j == 0), stop=(j == i)), reads=[b_VA, b_PT[p]], writes=[b_pO[hb]])

            def fin():
              for hb in range(2):
                hs = slice(hb * 512, (hb + 1) * 512)
                S.op("act", lambda e, hb=hb, hs=hs: e.activation(out=rr[64:65, hs], in_=pO[hb][64:65, :], func=AF.Ln), reads=[b_pO[hb]], writes=[b_rr])
                S.op("act", lambda e, hs=hs: e.activation(out=rr[64:65, hs], in_=rr[64:65, hs], func=AF.Exp, scale=-1.0), reads=[b_rr], writes=[b_rr])
                ps, bp = pST.get()
                S.op("pe", lambda e, ps=ps, hs=hs: e.matmul(ps[0:64, :], lhsT=ones_f[64:65, 0:64], rhs=rr[64:65, hs], start=True, stop=True),
                     reads=[b_rr], writes=[bp])
                S.op("act", lambda e, ps=ps, hs=hs: e.activation(out=rbc[:, hs], in_=ps[0:64, :], func=AF.Copy), reads=[bp], writes=[b_rbc])
                S.op("dve", lambda e, hb=hb, hs=hs: e.tensor_tensor(out=o1[:, hs], in0=pO[hb][0:64, :], in1=rbc[:, hs], op=ALU.mult),
                     reads=[b_pO[hb], b_rbc], writes=[b_o1])
              S.op("pool", lambda e: e.tensor_tensor(out=ogb[g][:].rearrange("p h t -> p (h t)"), in0=o1[:],
                                                     in1=ga[g][:].rearrange("p h t -> p (h t)"), op=ALU.mult),
                   reads=[b_o1, b_ga[g]], writes=[b_ogb[g]])
              S.dma("sp", OG[:, :, tok].rearrange("h d t -> d h t"), ogb[g][:], reads=[b_ogb[g]])

            def mk(j):
                def u():
                    block(j)
                    if j > 0:
                        pv(j - 1)
                return u

            for j in range(i + 1):
                units.append(mk(j))

            def last():
                pv(i)
                fin()
            units.append(last)
            return units

        nq = NT if not (dbg and "nq" in dbg) else dbg["nq"]
        groups = [list(range(a, min(a + NB, nq))) for a in range(0, nq, NB)]
        prev = []
        for grp in groups:
            stage1_group(grp, prev)
            prev = []
            for t in grp:
                prev += stage2_units(t)
        for u in prev:
            u()
        S.emit()


def phase_ssd(nc, S, l, XBC, ZS, DTs, YN, convw_d, convb_d, alog_d, dskip_d, sng_d, triu_d, sut_d, ident, identf, ones_f, ones_b, DBG, dbg):
    with ExitStack() as st:
        def sb(name, shape, dt=F32):
            return st.enter_context(nc.sbuf_tensor(uname(name), list(shape), dt))

        import os
        PE_ = os.environ.get('SSD_POOL', 'pool')

        def bc3(ap, shape):
            return ap.unsqueeze(2).to_broadcast(shape)

        cw = sb("cw", [128, 12, 4]); b_cw = Buf()
        cbv = sb("cbv", [128, 12]); b_cbv = Buf()
        a_b = sb("a_b", [128, 16]); b_ab = Buf()
        dsk = sb("dsk", [128, 16]); b_dsk = Buf()
        sng = sb("sng", [128, D]); b_sng = Buf()
        triu = sb("triu", [128, 128]); b_triu = Buf()
        sut = sb("sut", [128, 128]); b_sut = Buf()
        b_idf = Buf()
        DTall = sb("DTall", [128, NT, 16]); b_DT = Buf()
        H = sb("H", [128, 16, 64]); b_H = Buf()
        Hb = sb("Hb", [128, 16, 64], BF16); b_Hb = Buf()
        Ht = sb("Ht", [128, 16, 64]); b_Ht = Buf()
        S.dma("sp", cw[:], convw_d[l], writes=[b_cw])
        S.dma("sp", cbv[:], convb_d[l], writes=[b_cbv])
        S.dma("sp", a_b[:], alog_d[l].partition_broadcast(128), writes=[b_ab])
        S.dma("sp", dsk[:], dskip_d[l].partition_broadcast(128), writes=[b_dsk])
        S.dma("sp", sng[:], sng_d[l].partition_broadcast(128), writes=[b_sng])
        S.dma("sp", triu[:], triu_d, writes=[b_triu])
        S.dma("sp", sut[:], sut_d, writes=[b_sut])
        S.dma("sp", DTall[:], DTs.rearrange("(c p) h -> p c h", p=128), writes=[b_DT])
        S.op("act", lambda e: e.activation(out=a_b[:], in_=a_b[:], func=AF.Exp), reads=[b_ab], writes=[b_ab])
        S.op("dve", lambda e: e.tensor_scalar(out=a_b[:], in0=a_b[:], scalar1=-1.0, scalar2=None, op0=ALU.mult), reads=[b_ab], writes=[b_ab])
        S.op("dve", lambda e: e.memset(H[:], 0.0), writes=[b_H])
        S.op("dve", lambda e: e.memset(Hb[:], 0.0), writes=[b_Hb])

        pp = PsumPool(nc, st, [f"ps{i}" for i in range(6)])
        tpB = st.enter_context(nc.psum_tensor(uname("tpB"), [128, 1024], BF16)); b_tpB = Buf(excl=True)
        tpY = st.enter_context(nc.psum_tensor(uname("tpY"), [128, 1024], BF16)); b_tpY = Buf(excl=True)

        xin = sb("xin", [128, 12, 515]); b_xin = Buf()
        cv = sb("cv", [128, 12, 512]); b_cv = Buf()
        cvx = sb("cvx", [128, 8, 512]); b_cvx = Buf()
        cvbc2 = [sb("cvbc", [128, 4, 512], BF16) for _ in range(2)]; b_cvbc2 = [Buf(), Buf()]
        xs_tok2 = [sb("xs_tok", [128, 16, 64]) for _ in range(2)]; b_xst2 = [Buf(), Buf()]
        xd2 = [sb("xd", [128, 16, 64], BF16) for _ in range(2)]; b_xd2 = [Buf(), Buf()]
        xds2 = [sb("xds", [128, 16, 64], BF16) for _ in range(2)]; b_xds2 = [Buf(), Buf()]
        Btok2 = [sb("Btok", [128, 256], BF16) for _ in range(2)]; b_Btok2 = [Buf(), Buf()]
        dd2 = [sb("dd", [128, 8, 16]) for _ in range(2)]; b_dd2 = [Buf(), Buf()]
        Mm2 = [sb("Mm", [128, 16, 128], BF16) for _ in range(2)]; b_Mm2 = [Buf(), Buf()]
        zs2 = [sb("zs", [128, D], BF16) for _ in range(2)]; b_zs2 = [Buf(), Buf()]
        csb = sb("csb", [128, 32]); b_csb = Buf()
        LH = sb("LH", [128, 2, 16, 128], BF16); b_LH = Buf()
        hl = sb("hl", [128, 2, 16], BF16); b_hl = Buf()
        hlf = sb("hlf", [128, 2, 16]); b_hlf = Buf()
        triub = sb("triub", [128, 128], BF16); b_triub = Buf()
        sutb = sb("sutb", [128, 128], BF16); b_sutb = Buf()
        S.op("dve", lambda e: e.tensor_copy(out=triub[:], in_=triu[:]), reads=[b_triu], writes=[b_triub])
        S.op("dve", lambda e: e.tensor_copy(out=sutb[:], in_=sut[:]), reads=[b_sut], writes=[b_sutb])
        dec = sb("dec", [128, 16, 128], BF16); b_dec = Buf()
        mcb = sb("mcb", [128, 2, 128], BF16); b_mcb = Buf()
        yo = sb("yo", [128, 16, 64]); b_yo = Buf()
        y1 = sb("y1", [128, 16, 64]); b_y1 = Buf()
        y2 = sb("y2", [128, 16, 64]); b_y2 = Buf()
        nst = sb("nst", [128, 8]); b_nst = Buf()
        junk = sb("junk3", [128, 512], BF16); b_junk = Buf()
        yn = sb("yn", [128, D], BF16); b_yn = Buf()
        ynT = sb("ynT", [128, 8, 128], BF16); b_ynT = Buf()

        XBCv = XBC.rearrange("(ct p) t -> p ct t", p=128)
        YNv = YN.rearrange("(ct p) t -> p ct t", p=128)
        nsc = 8 if not (dbg and "nsc" in dbg) else dbg["nsc"]

        xin2 = [xin, sb("xin_b", [128, 12, 515])]; b_xin2 = [b_xin, Buf()]
        b_cvc = [Buf() for _ in range(12)]

        def conv_load(sc):
            t0 = sc * 512
            xi, bx = xin2[sc % 2], b_xin2[sc % 2]
            if sc == 0:
                S.op("pool", lambda e: e.memset(xi[:, :, 0:3], 0.0), writes=[bx])
                S.dma("sp", xi[:, :, 3:515], XBCv[:, :, 0:512], writes=[bx])
            else:
                S.dma("sp", xi[:], XBCv[:, :, t0 - 3:t0 + 512], writes=[bx])

        def conv_units(sc):
            xi, bx = xin2[sc % 2], b_xin2[sc % 2]
            cvbc, b_cvbc = cvbc2[sc % 2], b_cvbc2[sc % 2]

            def mk(cts, last):
                def u():
                    for ct in cts:
                        S.op(PE_, lambda e, ct=ct: e.tensor_scalar(out=cv[:, ct, :], in0=xi[:, ct, 3:515], scalar1=cw[:, ct, 3:4], scalar2=cbv[:, ct:ct + 1],
                                                                   op0=ALU.mult, op1=ALU.add), reads=[bx, b_cw, b_cbv], writes=[b_cvc[ct]])
                        for kk in range(3):
                            S.op("dve", lambda e, ct=ct, kk=kk: e.scalar_tensor_tensor(out=cv[:, ct, :], in0=xi[:, ct, kk:kk + 512], scalar=cw[:, ct, kk:kk + 1],
                                                                                        in1=cv[:, ct, :], op0=ALU.mult, op1=ALU.add),
                                 reads=[bx, b_cw, b_cvc[ct]], writes=[b_cvc[ct]])
                    if last:
                        S.op("act", lambda e: e.activation(out=cvx[:], in_=cv[:, 0:8, :], func=AF.Silu), reads=b_cvc[0:8], writes=[b_cvx])
                        S.op("act", lambda e: e.activation(out=cvbc[:], in_=cv[:, 8:12, :], func=AF.Silu), reads=b_cvc[8:12], writes=[b_cvbc])
                return u
            return [mk([0, 1, 2], False), mk([3, 4, 5], False), mk([6, 7, 8], False), mk([9, 10, 11], True)]

        def units_A(c):
            sc, cc = c // 4, c % 4
            k = c % 2
            cvbc, b_cvbc = cvbc2[sc % 2], b_cvbc2[sc % 2]
            xs_tok, b_xst, xd, b_xd, xds, b_xds = xs_tok2[k], b_xst2[k], xd2[k], b_xd2[k], xds2[k], b_xds2[k]
            Btok, b_Btok, dd, b_dd, Mm, b_Mm, zs, b_zs = Btok2[k], b_Btok2[k], dd2[k], b_dd2[k], Mm2[k], b_Mm2[k], zs2[k], b_zs2[k]
            cols = slice(cc * 128, (cc + 1) * 128)
            tok = slice(c * 128, (c + 1) * 128)
            dt = DTall[:, c, :]

            def a0():
                S.dma("sp", zs[:], ZS[tok, :], writes=[b_zs])
                tA = [pp.get(), pp.get()]
                for ct in range(8):
                    ps, bp = tA[ct // 4]
                    S.op("pe", lambda e, ps=ps, ct=ct: e.transpose(out=ps[:, (ct % 4) * 128:(ct % 4 + 1) * 128], in_=cvx[:, ct, cols], identity=identf[:]),
                         reads=[b_cvx, b_idf], writes=[bp])
                for g in range(2):
                    S.op("pe", lambda e, g=g: e.transpose(out=tpB[:, g * 128:(g + 1) * 128], in_=cvbc[:, g, cols], identity=ident[:]),
                         reads=[b_cvbc], writes=[b_tpB])
                S.op("act", lambda e: e.activation(out=Btok[:], in_=tpB[:, 0:256], func=AF.Copy), reads=[b_tpB], writes=[b_Btok])
                for hb in range(2):
                    ps, bp = tA[hb]
                    hs = slice(hb * 8, (hb + 1) * 8)
                    S.op("dve", lambda e, ps=ps, hs=hs: e.tensor_tensor(out=xd[:, hs, :], in0=ps[:].rearrange("p (h d) -> p h d", h=8),
                                                                        in1=bc3(dt[:, hs], [128, 8, 64]), op=ALU.mult),
                         reads=[bp, b_DT], writes=[b_xd])
                    S.op("act", lambda e, ps=ps, hs=hs: e.activation(out=xs_tok[:, hs, :], in_=ps[:].rearrange("p (h d) -> p h d", h=8), func=AF.Copy),
                         reads=[bp], writes=[b_xst])

            def a1():
                S.op("dve", lambda e: e.tensor_tensor(out=dd[:, 0, :], in0=dt, in1=a_b[:], op=ALU.mult), reads=[b_DT, b_ab], writes=[b_dd])
                psc, bpc = pp.get()
                S.op("dve", lambda e: e.tensor_copy(out=hl[:, 0, :], in_=dd[:, 0, :]), reads=[b_dd], writes=[b_hl])
                S.op("dve", lambda e: e.tensor_tensor(out=hl[:, 1, :], in0=dd[:, 0, :], in1=hl[:, 0, :], op=ALU.subtract), reads=[b_dd, b_hl], writes=[b_hl])
                S.op("dve", lambda e: e.tensor_copy(out=hlf[:], in_=hl[:]), reads=[b_hl], writes=[b_hlf])
                for pi in range(2):
                    S.op("pe", lambda e, pi=pi: e.matmul(psc[:, 0:16], lhsT=triub[:], rhs=hl[:, pi, :], start=(pi == 0), stop=(pi == 1)),
                         reads=[b_triub, b_hl], writes=[bpc])
                for pi in range(2):
                    S.op("pe", lambda e, pi=pi: e.matmul(psc[:, 16:32], lhsT=ones_b[:], rhs=hl[:, pi, :], start=(pi == 0), stop=(pi == 1)),
                         reads=[b_hl], writes=[bpc])
                S.op("act", lambda e: e.activation(out=csb[:], in_=psc[:, 0:32], func=AF.Copy), reads=[bpc], writes=[b_csb])
                S.op("act", lambda e: e.activation(out=dd[:, 1:3, :].rearrange("p a h -> p (a h)"), in_=csb[:], func=AF.Exp), reads=[b_csb], writes=[b_dd])
                S.op("dve", lambda e: e.tensor_tensor(out=dd[:, 3, :], in0=csb[:, 16:32], in1=csb[:, 0:16], op=ALU.subtract), reads=[b_csb], writes=[b_dd])
                S.op("act", lambda e: e.activation(out=dd[:, 4, :], in_=dd[:, 3, :], func=AF.Exp), reads=[b_dd], writes=[b_dd])

            def a2():
                S.op("dve", lambda e: e.tensor_tensor(out=xds[:], in0=xd[:], in1=bc3(dd[:, 4, :], [128, 16, 64]), op=ALU.mult),
                     reads=[b_xd, b_dd], writes=[b_xds])
                pcb, bpcb = pp.get()
                for g in range(2):
                    S.op("pe", lambda e, g=g: e.matmul(pcb[:, g * 128:(g + 1) * 128], lhsT=cvbc[:, g, cols], rhs=cvbc[:, 2 + g, cols], start=True, stop=True),
                         reads=[b_cvbc], writes=[bpcb])
                S.op("dve", lambda e: e.tensor_tensor(out=mcb[:], in0=pcb[:, 0:256].rearrange("p (g l) -> p g l", g=2),
                                                      in1=triu[:].unsqueeze(1).to_broadcast([128, 2, 128]), op=ALU.mult),
                     reads=[bpcb, b_triu], writes=[b_mcb])

            def a3():
                for pi in range(2):
                    S.op("dve", lambda e, pi=pi: e.tensor_tensor(out=LH[:, pi, :, :], in0=sutb[:].unsqueeze(1).to_broadcast([128, 16, 128]),
                                                                 in1=hlf[:, pi, :].unsqueeze(2).to_broadcast([128, 16, 128]), op=ALU.mult),
                         reads=[b_sutb, b_hlf], writes=[b_LH])

            def mk_seg(q4s):
                def u():
                    for q4 in q4s:
                        ps, bp = pp.get()
                        for hh in range(4):
                            h = q4 * 4 + hh
                            for pi in range(2):
                                S.op("pe", lambda e, ps=ps, h=h, hh=hh, pi=pi: e.matmul(ps[:, hh * 128:(hh + 1) * 128], lhsT=LH[:, pi, h, :], rhs=triub[:],
                                                                                       start=(pi == 0), stop=(pi == 1)),
                                     reads=[b_LH, b_triub], writes=[bp])
                        S.op("act", lambda e, ps=ps, q4=q4: e.activation(out=dec[:, q4 * 4:(q4 + 1) * 4, :].rearrange("p h l -> p (h l)"), in_=ps[:], func=AF.Exp),
                             reads=[bp], writes=[b_dec])
                return u

            def a6():
                for g in range(2):
                    S.op("dve", lambda e, g=g: e.tensor_tensor(out=Mm[:, g * 8:(g + 1) * 8, :], in0=dec[:, g * 8:(g + 1) * 8, :],
                                                               in1=mcb[:, g, :].unsqueeze(1).to_broadcast([128, 8, 128]), op=ALU.mult),
                         reads=[b_dec, b_mcb], writes=[b_Mm])

            return [a0, a1, a2, a3, mk_seg([0, 1]), mk_seg([2, 3]), a6]

        def units_B(c):
            sc, cc = c // 4, c % 4
            k = c % 2
            cvbc, b_cvbc = cvbc2[sc % 2], b_cvbc2[sc % 2]
            xs_tok, b_xst, xd, b_xd, xds, b_xds = xs_tok2[k], b_xst2[k], xd2[k], b_xd2[k], xds2[k], b_xds2[k]
            Btok, b_Btok, dd, b_dd, Mm, b_Mm, zs, b_zs = Btok2[k], b_Btok2[k], dd2[k], b_dd2[k], Mm2[k], b_Mm2[k], zs2[k], b_zs2[k]
            cols = slice(cc * 128, (cc + 1) * 128)
            tok = slice(c * 128, (c + 1) * 128)

            def b0():
                for g in range(2):
                    ps, bp = pp.get()
                    S.op("pe", lambda e, ps=ps, g=g: e.matmul(ps[:], lhsT=cvbc[:, 2 + g, cols], rhs=Hb[:, g * 8:(g + 1) * 8, :], start=True, stop=True),
                         reads=[b_cvbc, b_Hb], writes=[bp])
                    S.op("dve", lambda e, ps=ps, g=g: e.tensor_tensor(out=yo[:, g * 8:(g + 1) * 8, :], in0=ps[:].rearrange("p (h d) -> p h d", h=8),
                                                                      in1=bc3(dd[:, 1, g * 8:(g + 1) * 8], [128, 8, 64]), op=ALU.mult),
                         reads=[bp, b_dd], writes=[b_yo])

            def b1():
                sts = [pp.get(), pp.get()]
                for g in range(2):
                    ps, bp = sts[g]
                    S.op("pe", lambda e, ps=ps, g=g: e.matmul(ps[:], lhsT=Btok[:, g * 128:(g + 1) * 128], rhs=xds[:, g * 8:(g + 1) * 8, :], start=True, stop=True),
                         reads=[b_Btok, b_xds], writes=[bp])
                S.op("dve", lambda e: e.tensor_tensor(out=Ht[:], in0=H[:], in1=bc3(dd[:, 2, :], [128, 16, 64]), op=ALU.mult),
                     reads=[b_H, b_dd], writes=[b_Ht])
                for g in range(2):
                    ps, bp = sts[g]
                    S.op("dve", lambda e, ps=ps, g=g: e.tensor_tensor(out=H[:, g * 8:(g + 1) * 8, :], in0=ps[:].rearrange("p (h d) -> p h d", h=8),
                                                                      in1=Ht[:, g * 8:(g + 1) * 8, :], op=ALU.add),
                         reads=[bp, b_Ht], writes=[b_H])
                S.op("act", lambda e: e.activation(out=Hb[:], in_=H[:], func=AF.Copy), reads=[b_H], writes=[b_Hb])

            def b2():
                for hb in range(2):
                    ps, bp = pp.get()
                    for hh in range(8):
                        h = hb * 8 + hh
                        S.op("pe", lambda e, ps=ps, h=h, hh=hh: e.matmul(ps[:, hh * 64:(hh + 1) * 64], lhsT=Mm[:, h, :], rhs=xd[:, h, :], start=True, stop=True),
                             reads=[b_Mm, b_xd], writes=[bp])
                    hs = slice(hb * 8, (hb + 1) * 8)
                    S.op("dve", lambda e, ps=ps, hs=hs: e.tensor_tensor(out=y1[:, hs, :], in0=ps[:].rearrange("p (h d) -> p h d", h=8), in1=yo[:, hs, :], op=ALU.add),
                         reads=[bp, b_yo], writes=[b_y1])

            def b3():
                S.op(PE_, lambda e: e.tensor_tensor(out=y2[:], in0=xs_tok[:], in1=bc3(dsk[:], [128, 16, 64]), op=ALU.mult),
                     reads=[b_xst, b_dsk], writes=[b_y2])
                S.op(PE_, lambda e: e.tensor_tensor(out=y2[:], in0=y2[:], in1=y1[:], op=ALU.add), reads=[b_y2, b_y1], writes=[b_y2])
                S.op(PE_, lambda e: e.tensor_tensor(out=y2[:].rearrange("p h d -> p (h d)"), in0=y2[:].rearrange("p h d -> p (h d)"), in1=zs[:], op=ALU.mult),
                     reads=[b_y2, b_zs], writes=[b_y2])

            def b4():
                for g in range(2):
                    S.op("act", lambda e, g=g: e.activation(out=junk[:], in_=y2[:, g * 8:(g + 1) * 8, :].rearrange("p h d -> p (h d)"), func=AF.Square,
                                                            accum_out=nst[:, g:g + 1]), reads=[b_y2], writes=[b_junk, b_nst])
                S.op("dve", lambda e: e.tensor_scalar(out=nst[:, 2:4], in0=nst[:, 0:2], scalar1=1.0 / 512, scalar2=EPS, op0=ALU.mult, op1=ALU.add),
                     reads=[b_nst], writes=[b_nst])
                S.op("act", lambda e: e.activation(out=nst[:, 4:6], in_=nst[:, 2:4], func=AF.Ln), reads=[b_nst], writes=[b_nst])
                S.op("act", lambda e: e.activation(out=nst[:, 6:8], in_=nst[:, 4:6], func=AF.Exp, scale=-0.5), reads=[b_nst], writes=[b_nst])

            def b5():
                for g in range(2):
                    S.op("dve", lambda e, g=g: e.scalar_tensor_tensor(out=yn[:, g * 512:(g + 1) * 512], in0=y2[:, g * 8:(g + 1) * 8, :].rearrange("p h d -> p (h d)"),
                                                                      scalar=nst[:, 6 + g:7 + g], in1=sng[:, g * 512:(g + 1) * 512], op0=ALU.mult, op1=ALU.mult),
                         reads=[b_y2, b_nst, b_sng], writes=[b_yn])

            def b6():
                for ct in range(8):
                    S.op("pe", lambda e, ct=ct: e.transpose(out=tpY[:, ct * 128:(ct + 1) * 128], in_=yn[:, ct * 128:(ct + 1) * 128], identity=ident[:]),
                         reads=[b_yn], writes=[b_tpY])
                S.op("act", lambda e: e.activation(out=ynT[:].rearrange("p c t -> p (c t)"), in_=tpY[:], func=AF.Copy), reads=[b_tpY], writes=[b_ynT])
                S.dma("sp", YNv[:, :, tok], ynT[:], reads=[b_ynT])

            return [b0, b1, b2, b3, b4, b5, b6]

        def zip_emit(*streams):
            for i in range(max(len(u) for u in streams)):
                for u in streams:
                    if i < len(u):
                        u[i]()

        nch = nsc * 4
        conv_load(0)
        for u in conv_units(0):
            u()
        if nsc > 1:
            conv_load(1)
        zip_emit(units_A(0))
        for c in range(nch):
            sc, cc = c // 4, c % 4
            ua = units_A(c + 1) if c + 1 < nch else []
            uc = []
            if sc + 1 < nsc:
                cu = conv_units(sc + 1)
                uc = [lambda: None] * 2 + [cu[cc]]
                if cc == 3 and sc + 2 < nsc:
                    uc.append(lambda sc=sc: conv_load(sc + 2))
            if cc == 3 and sc + 1 < nsc:
                zip_emit(uc, units_B(c))
                zip_emit(ua)
            else:
                zip_emit(units_B(c), ua, uc)
        S.emit()


def phase_out(nc, S, l, x_src, X1, OG, YN, GLA, GLS, wa_d, ws_d, wo_d, GATE_b, nxt=None):
    with ExitStack() as st:
        if nxt is not None:
            nxt(st)
        def sb(name, shape, dt=F32):
            return st.enter_context(nc.sbuf_tensor(uname(name), list(shape), dt))
        wa = sb("wa", [64, 8, D], BF16); b_wa = [Buf() for _ in range(8)]
        ws = sb("ws", [128, 8, D], BF16); b_ws = [Buf() for _ in range(8)]
        wo = sb("wo", [128, 8, D], BF16); b_wo = [Buf() for _ in range(2)]
        wav = wa_d[l].rearrange("(h d) n -> d h n", d=64)
        wsv = ws_d[l].rearrange("(kc p) n -> p kc n", p=128)
        wov = wo_d[l].rearrange("(kc p) n -> p kc n", p=128)
        for nchunk in range(8):
            ns = slice(nchunk * 128, (nchunk + 1) * 128)
            S.dma("pool", wa[:, :, ns], wav[:, :, ns], writes=[b_wa[nchunk]])
            S.dma("pool", ws[:, :, ns], wsv[:, :, ns], writes=[b_ws[nchunk]])
        for nb in range(2):
            S.dma("pool", wo[:, :, nb * 512:(nb + 1) * 512], wov[:, :, nb * 512:(nb + 1) * 512], writes=[b_wo[nb]])
        pp = PsumPool(nc, st, [f"po{i}" for i in range(8 if nxt is None else 6)])
        ogt2 = [sb("ogt", [64, 8, 512], BF16) for _ in range(2)]; b_ogt2 = [Buf(), Buf()]
        ynt2 = [sb("ynt", [128, 8, 512], BF16) for _ in range(2)]; b_ynt2 = [Buf(), Buf()]
        gla = sb("gla", [128, 8, 512], BF16); b_gla = Buf()
        gls = sb("gls", [128, 8, 512], BF16); b_gls = Buf()
        ta = [sb("ta", [128, 512]) for _ in range(2)]; b_ta = [Buf(), Buf()]
        tb = [sb("tb", [128, 512]) for _ in range(2)]; b_tb = [Buf(), Buf()]
        mg = sb("mg", [128, 8, 512], BF16); b_mg = [Buf() for _ in range(8)]
        xt = [sb("xo", [128, D]) for _ in range(2)]; b_xt = [Buf(), Buf()]
        t3 = [sb("t3", [128, D]) for _ in range(2)]; b_t3 = [Buf(), Buf()]
        OGv = OG.rearrange("h d t -> d h t")
        YNv = YN.rearrange("(ct p) t -> p ct t", p=128)
        GLAv = GLA.rearrange("(ct p) t -> p ct t", p=128)
        GLSv = GLS.rearrange("(ct p) t -> p ct t", p=128)
        def load_blk(c):
            tok = slice(c * 512, (c + 1) * 512)
            j = c % 2
            S.dma("sp", ogt2[j][:], OGv[:, :, tok], writes=[b_ogt2[j]])
            S.dma("sp", ynt2[j][:], YNv[:, :, tok], writes=[b_ynt2[j]])

        def load_gates(c):
            tok = slice(c * 512, (c + 1) * 512)
            S.dma("sp", gla[:], GLAv[:, :, tok], writes=[b_gla])
            S.dma("sp", gls[:], GLSv[:, :, tok], writes=[b_gls])

        load_blk(0)
        load_gates(0)
        for c in range(8):
            tok = slice(c * 512, (c + 1) * 512)
            if c + 1 < 8:
                load_blk(c + 1)
            ogt, b_ogt, ynt, b_ynt = ogt2[c % 2], b_ogt2[c % 2], ynt2[c % 2], b_ynt2[c % 2]
            for nchunk in range(8):
                ns = slice(nchunk * 128, (nchunk + 1) * 128)
                a = nchunk % 2
                psa, bpa = pp.get()
                for h in range(8):
                    S.op("pe", lambda e, psa=psa, h=h, ns=ns, ogt=ogt: e.matmul(psa[:], lhsT=wa[:, h, ns], rhs=ogt[:, h, :], start=(h == 0), stop=(h == 7)),
                         reads=[b_wa[nchunk], b_ogt], writes=[bpa])
                pss, bps = pp.get()
                for kc in range(8):
                    S.op("pe", lambda e, pss=pss, kc=kc, ns=ns, ynt=ynt: e.matmul(pss[:], lhsT=ws[:, kc, ns], rhs=ynt[:, kc, :], start=(kc == 0), stop=(kc == 7)),
                         reads=[b_ws[nchunk], b_ynt], writes=[bps])
                S.op("dve", lambda e, psa=psa, a=a, nchunk=nchunk, gla=gla: e.tensor_tensor(out=ta[a][:], in0=psa[:], in1=gla[:, nchunk, :], op=ALU.mult),
                     reads=[bpa, b_gla], writes=[b_ta[a]])
                S.op("dve", lambda e, pss=pss, a=a, nchunk=nchunk, gls=gls: e.tensor_tensor(out=tb[a][:], in0=pss[:], in1=gls[:, nchunk, :], op=ALU.mult),
                     reads=[bps, b_gls], writes=[b_tb[a]])
                S.op("pool", lambda e, a=a, nchunk=nchunk: e.tensor_tensor(out=mg[:, nchunk, :], in0=ta[a][:], in1=tb[a][:], op=ALU.add),
                     reads=[b_ta[a], b_tb[a]], writes=[b_mg[nchunk]])
            if c + 1 < 8:
                load_gates(c + 1)
            for tq in range(4):
                k = tq % 2
                t128 = slice(c * 512 + tq * 128, c * 512 + (tq + 1) * 128)
                S.dma("sp", xt[k][:], x_src[t128, :], writes=[b_xt[k]])
                for nb in range(2):
                    ps, bp = pp.get()
                    for kc in range(8):
                        S.op("pe", lambda e, ps=ps, kc=kc, tq=tq, nb=nb: e.matmul(ps[:], lhsT=mg[:, kc, tq * 128:(tq + 1) * 128], rhs=wo[:, kc, nb * 512:(nb + 1) * 512],
                                                                                 start=(kc == 0), stop=(kc == 7)),
                             reads=b_mg + [b_wo[nb]], writes=[bp])
                    S.op("dve", lambda e, ps=ps, k=k, nb=nb: e.tensor_tensor(out=t3[k][:, nb * 512:(nb + 1) * 512], in0=ps[:], in1=GATE_b[:, nb * 512:(nb + 1) * 512], op=ALU.mult),
                         reads=[bp], writes=[b_t3[k]])
                S.op("pool", lambda e, k=k: e.tensor_tensor(out=t3[k][:], in0=t3[k][:], in1=xt[k][:], op=ALU.add), reads=[b_t3[k], b_xt[k]], writes=[b_t3[k]])
                S.dma("sp", X1[t128, :], t3[k][:], reads=[b_t3[k]])
        S.emit()


def phase_final(nc, S, x_src, fg_d, out_d):
    with ExitStack() as st:
        def sb(name, shape, dt=F32):
            return st.enter_context(nc.sbuf_tensor(uname(name), list(shape), dt))
        fg = sb("fg", [128, D]); b_fg = Buf()
        S.dma("sp", fg[:], fg_d.partition_broadcast(128), writes=[b_fg])
        NF = 4
        xt = [sb("xf", [128, D]) for _ in range(NF)]; b_xt = [Buf() for _ in range(NF)]
        junk = sb("junkf", [128, D], BF16); b_junk = Buf()
        ss = [sb("ssf", [128, 4]) for _ in range(NF)]; b_ss = [Buf() for _ in range(NF)]
        yo = [sb("yof", [128, D]) for _ in range(NF)]; b_yo = [Buf() for _ in range(NF)]
        for tt in range(NT):
            k = tt % NF
            tok = slice(tt * 128, (tt + 1) * 128)
            S.dma("sp", xt[k][:], x_src[tok, :], writes=[b_xt[k]])
            S.op("act", lambda e, k=k: e.activation(out=junk[:], in_=xt[k][:], func=AF.Square, accum_out=ss[k][:, 0:1]),
                 reads=[b_xt[k]], writes=[b_junk, b_ss[k]])
            S.op("dve", lambda e, k=k: e.tensor_scalar(out=ss[k][:, 1:2], in0=ss[k][:, 0:1], scalar1=1.0 / D, scalar2=EPS, op0=ALU.mult, op1=ALU.add),
                 reads=[b_ss[k]], writes=[b_ss[k]])
            S.op("act", lambda e, k=k: e.activation(out=ss[k][:, 2:3], in_=ss[k][:, 1:2], func=AF.Ln), reads=[b_ss[k]], writes=[b_ss[k]])
            S.op("act", lambda e, k=k: e.activation(out=ss[k][:, 3:4], in_=ss[k][:, 2:3], func=AF.Exp, scale=-0.5), reads=[b_ss[k]], writes=[b_ss[k]])
            S.op("dve", lambda e, k=k: e.scalar_tensor_tensor(out=yo[k][:], in0=xt[k][:], scalar=ss[k][:, 3:4], in1=fg[:], op0=ALU.mult, op1=ALU.mult),
                 reads=[b_xt[k], b_ss[k], b_fg], writes=[b_yo[k]])
            S.dma("sp", out_d[tok, :], yo[k][:], reads=[b_yo[k]])
        S.emit()


def host_consts():
    inv = (10000.0 ** (-np.arange(0, 64, 2, dtype=np.float32) / 64.0)).astype(np.float32)
    ang = np.arange(L, dtype=np.float32)[:, None] * inv[None, :]
    ang = np.concatenate([ang, ang], axis=-1)
    cos = np.cos(ang).astype(np.float32).T
    sin = np.sin(ang).astype(np.float32).T
    sgn = np.where(np.arange(64) < 32, -1.0, 1.0).astype(np.float32)[:, None]
    cosT = np.concatenate([cos, cos], axis=0)
    sinT = np.concatenate([sin * sgn, sin * sgn], axis=0)
    r = np.arange(128)
    ident = np.eye(128, dtype=np.float32)
    cbias = np.where(r[None, :] <= r[:, None], 0.0, -1e30).astype(np.float32)
    triu = (r[:, None] <= r[None, :]).astype(np.float32)
    sut = (r[:, None] > r[None, :]).astype(np.float32)
    pow2 = np.tile(((65.0 / 64.0) * 2.0 ** (-np.arange(NBIS + 2, dtype=np.float64))).astype(np.float32)[None, :], (128, 1))
    return dict(cosT=np.ascontiguousarray(cosT), sinT=np.ascontiguousarray(sinT), ident=ident, cbias=cbias,
                triu=triu, sut=sut, pow2=pow2)


def make_in_maps(inputs, n_cores=8):
    f = lambda a: np.ascontiguousarray(np.asarray(a, dtype=np.float32))
    shared = dict(
        w_ada=f(inputs["w_ada"]), b_ada=f(inputs["b_ada"]).reshape(DEPTH, 1, 3 * D),
        norm_g=f(inputs["norm_g"]).reshape(DEPTH, 1, D), w_in=f(inputs["w_in"]),
        conv_wT=f(np.asarray(inputs["conv_w"]).reshape(DEPTH, 4, 12, 128).transpose(0, 3, 2, 1)),
        conv_bT=f(np.asarray(inputs["conv_b"]).reshape(DEPTH, 12, 128).transpose(0, 2, 1)),
        dt_bias=f(inputs["dt_bias"]).reshape(DEPTH, 1, 16), a_log=f(inputs["a_log"]).reshape(DEPTH, 1, 16),
        d_skip=f(inputs["d_skip"]).reshape(DEPTH, 1, 16), ssm_norm_g=f(inputs["ssm_norm_g"]).reshape(DEPTH, 1, D),
        w_branch_a=f(inputs["w_branch_a"]), w_branch_s=f(inputs["w_branch_s"]), w_out=f(inputs["w_out"]),
        final_g=f(inputs["final_g"]).reshape(1, D),
    )
    shared.update(host_consts())
    maps = []
    for b in range(n_cores):
        m = dict(shared)
        m["x"] = f(inputs["x"][b])
        m["cT"] = f(np.asarray(inputs["c"][b]).reshape(8, 128).T)
        maps.append(m)
    return maps


def kernel(**inputs):
    nc = build_program()
    maps = make_in_maps(inputs)
    res = run_bass_kernel_spmd(nc, maps, core_ids=list(range(8)))
    return np.stack([np.asarray(r["out"], dtype=np.float32) for r in res.results], axis=0)
```
